# Optimizing a Trainium2 kernel written in Bass

```python
import jax, jax.numpy as jnp
from jax import lax
import numpy as np

D_MODEL = 1024
BATCH = 32
SEQ = 256
DEPTH = 2
DEC_BATCH = 4
DEC_SEQ = 4096
PAST_LEN = 256

GRID_W = 64
N_EVEN = (DEPTH + 1) // 2
N_ODD = DEPTH // 2
N_MOD = 9
D_FF = 2816
EPS = 1e-6
CONV_CH = D_MODEL // 2
CONV_W = 31
HG_HEADS = 4
HG_KDIM = 128
HG_VDIM = 128
HG_WIDTH = HG_HEADS * HG_KDIM
HG_CHUNK = 64
EVEN_IN = 2 * CONV_CH + 5 * HG_WIDTH
EVEN_MIX = CONV_CH + HG_HEADS * HG_VDIM
N_HEADS = 16
N_KV_HEADS = 4
HEAD_DIM = 64
GROUP = N_HEADS // N_KV_HEADS
WINDOW = 128
BLOCK = 128
ODD_IN = (N_HEADS + 2 * N_KV_HEADS) * HEAD_DIM
ROPE_AX = HEAD_DIM // 2
ROPE_BASE = 10000.0

kernel_name = 'hybrid_diffusion_conv_hgrn2_swa_step'

F32 = jnp.float32


def rms_norm(x, g):
    xf = x.astype(F32)
    y = xf * lax.rsqrt(jnp.mean(xf * xf, axis=-1, keepdims=True) + EPS)
    return (y * g.astype(F32)).astype(x.dtype)


def layer_norm(x, g, b):
    xf = x.astype(F32)
    mu = jnp.mean(xf, axis=-1, keepdims=True)
    var = jnp.mean(jnp.square(xf - mu), axis=-1, keepdims=True)
    return ((xf - mu) * lax.rsqrt(var + 1e-5) * g.astype(F32) + b.astype(F32)).astype(x.dtype)


def adaln_params(cond, w, b):
    m = jax.nn.silu(cond) @ w + b
    return m.reshape(cond.shape[0], N_MOD, 1, D_MODEL)


def pre_norm(x, g, shift, scale):
    return rms_norm(x, g) * (1 + scale) + shift


def swiglu(h, w_in, w_out):
    gt, up = jnp.split(h @ w_in, 2, axis=-1)
    return (jax.nn.silu(gt) * up) @ w_out


def ffn_sublayer(x, mod, g, w_in, w_out, slot):
    h = pre_norm(x, g, mod[:, 3 * slot], mod[:, 3 * slot + 1])
    return x + 0.5 * mod[:, 3 * slot + 2] * swiglu(h, w_in, w_out)


def conformer_conv(u, w_dw, b_dw, ln_g, ln_b):
    a = u[..., :CONV_CH] * jax.nn.sigmoid(u[..., CONV_CH:])
    y = lax.conv_general_dilated(
        a, w_dw[:, None, :].astype(a.dtype), window_strides=(1,),
        padding=[(CONV_W // 2, CONV_W // 2)],
        dimension_numbers=('NWC', 'WIO', 'NWC'), feature_group_count=CONV_CH) + b_dw
    return jax.nn.silu(layer_norm(y, ln_g, ln_b))


def gla_chunk_scan(q, k, logf, v, s0):
    B, T, H, K = q.shape
    V = v.shape[-1]
    n = T // HG_CHUNK

    def blocks(a):
        return a.astype(F32).reshape(B, n, HG_CHUNK, H, a.shape[-1]).transpose(1, 0, 3, 2, 4)

    causal = jnp.tril(jnp.ones((HG_CHUNK, HG_CHUNK), bool))[:, :, None]

    def step(S, inp):
        qc, kc, gc, vc = inp
        b = jnp.cumsum(gc, axis=2)
        inter = jnp.einsum('bhtk,bhkv->bhtv', qc * jnp.exp(b), S)
        rel = jnp.exp(jnp.where(causal, b[:, :, :, None, :] - b[:, :, None, :, :], -jnp.inf))
        scores = jnp.einsum('bhtk,bhtsk,bhsk->bhts', qc, rel, kc)
        intra = jnp.einsum('bhts,bhsv->bhtv', scores, vc)
        b_end = b[:, :, -1:, :]
        S_new = jnp.exp(b_end[:, :, 0, :])[..., None] * S + jnp.einsum(
            'bhsk,bhsv->bhkv', kc * jnp.exp(b_end - b), vc)
        return S_new, inter + intra

    s_last, o = lax.scan(step, s0.astype(F32), (blocks(q), blocks(k), blocks(logf), blocks(v)))
    return o.transpose(1, 0, 3, 2, 4).reshape(B, T, H, V), s_last


def hgrn2_gate(z, lb):
    f = lb + (1 - lb) * jax.nn.sigmoid(z.astype(F32))
    return jnp.log(f), 1 - f


def hgrn2_mixer(p, lb_f, lb_b, norm_g, s0_f, s0_b):
    B, T, _ = p.shape
    q, zf, zb, i, g = jnp.split(p, 5, axis=-1)
    shp = (B, T, HG_HEADS, HG_KDIM)
    q = jax.nn.silu(q).reshape(shp)
    v = i.reshape(B, T, HG_HEADS, HG_VDIM)
    logf_f, k_f = hgrn2_gate(zf.reshape(shp), lb_f.reshape(HG_HEADS, HG_KDIM))
    logf_b, k_b = hgrn2_gate(zb.reshape(shp), lb_b.reshape(HG_HEADS, HG_KDIM))
    o_f, s_f = gla_chunk_scan(q, k_f, logf_f, v, s0_f)
    rev = lambda a: jnp.flip(a, axis=1)
    o_b, s_b = gla_chunk_scan(rev(q), rev(k_b), rev(logf_b), rev(v), s0_b)
    o = o_f + rev(o_b)
    o = rms_norm(o, norm_g) * jax.nn.silu(g.reshape(B, T, HG_HEADS, HG_VDIM).astype(F32))
    return o.reshape(B, T, HG_HEADS * HG_VDIM), s_f, s_b


def even_mixer(h, w_in, w_out, conv_w, conv_b, ln_g, ln_b, lb_f, lb_b, hg_g, s0_f, s0_b):
    u = h @ w_in
    a = conformer_conv(u[..., :2 * CONV_CH], conv_w, conv_b, ln_g, ln_b)
    r, s_f, s_b = hgrn2_mixer(u[..., 2 * CONV_CH:], lb_f, lb_b, hg_g, s0_f, s0_b)
    out = jnp.concatenate([a, r.astype(a.dtype)], axis=-1) @ w_out
    return out, s_f, s_b


def attn_qkv(h, w_in, qn, kn):
    B, T, _ = h.shape
    q, k, v = jnp.split(h @ w_in, [N_HEADS * HEAD_DIM, (N_HEADS + N_KV_HEADS) * HEAD_DIM], axis=-1)
    q = rms_norm(q.reshape(B, T, N_HEADS, HEAD_DIM), qn)
    k = rms_norm(k.reshape(B, T, N_KV_HEADS, HEAD_DIM), kn)
    return q, k, v.reshape(B, T, N_KV_HEADS, HEAD_DIM)


def axial_rope(x):
    T = x.shape[1]
    rows = T // GRID_W
    row = jnp.repeat(jnp.arange(rows), GRID_W).astype(F32)
    col = jnp.tile(jnp.arange(GRID_W), rows).astype(F32)
    inv = ROPE_BASE ** (-jnp.arange(0, ROPE_AX, 2, dtype=F32) / ROPE_AX)

    def rot(xa, pos):
        ang = pos[:, None] * inv[None, :]
        cos = jnp.cos(ang)[None, :, None, :]
        sin = jnp.sin(ang)[None, :, None, :]
        x1, x2 = jnp.split(xa.astype(F32), 2, axis=-1)
        return jnp.concatenate([x1 * cos - x2 * sin, x2 * cos + x1 * sin], axis=-1)

    out = jnp.concatenate([rot(x[..., :ROPE_AX], row), rot(x[..., ROPE_AX:], col)], axis=-1)
    return out.astype(x.dtype)


def sink_attend(s, sink, v):
    sk = sink.astype(F32)[None, :, :, None, None]
    m = jnp.maximum(jnp.max(s, axis=-1, keepdims=True), sk)
    p = jnp.exp(s - m)
    denom = jnp.sum(p, axis=-1, keepdims=True) + jnp.exp(sk - m)
    return jnp.einsum('bkgqs,bskd->bqkgd', p / denom, v.astype(F32))


def context_attention(q, k, v, sink):
    B, T = q.shape[:2]
    nb = T // BLOCK
    qb = q.reshape(B, nb, BLOCK, N_KV_HEADS, GROUP, HEAD_DIM).transpose(1, 0, 2, 3, 4, 5)
    sk = sink.reshape(N_KV_HEADS, GROUP)
    scale = HEAD_DIM ** -0.5

    def one(qblk):
        s = jnp.einsum('bqkgd,bskd->bkgqs', qblk, k).astype(F32) * scale
        return sink_attend(s, sk, v)

    o = lax.map(one, qb)
    return o.transpose(1, 0, 2, 3, 4, 5).reshape(B, T, N_HEADS * HEAD_DIM)


def latent_attention(q, k, v, ck, cv, sink):
    B, T = q.shape[:2]
    nb = T // BLOCK
    qg = q.reshape(B, T, N_KV_HEADS, GROUP, HEAD_DIM)
    pad = [(0, 0), (BLOCK, BLOCK), (0, 0), (0, 0)]
    kp = jnp.pad(k, pad)
    vp = jnp.pad(v, pad)
    qi = jnp.arange(BLOCK)[:, None]
    kj = jnp.arange(3 * BLOCK)[None, :]
    band = jnp.abs(kj - BLOCK - qi) <= WINDOW
    sk = sink.reshape(N_KV_HEADS, GROUP)
    scale = HEAD_DIM ** -0.5

    def one(n):
        start = n * BLOCK
        qblk = lax.dynamic_slice_in_dim(qg, start, BLOCK, axis=1)
        kblk = lax.dynamic_slice_in_dim(kp, start, 3 * BLOCK, axis=1)
        vblk = lax.dynamic_slice_in_dim(vp, start, 3 * BLOCK, axis=1)
        kpos = start - BLOCK + kj
        mask = band & (kpos >= 0) & (kpos < T)
        s_loc = jnp.where(mask, jnp.einsum('bqkgd,bskd->bkgqs', qblk, kblk).astype(F32) * scale, -jnp.inf)
        s_ctx = jnp.einsum('bqkgd,bskd->bkgqs', qblk, ck).astype(F32) * scale
        s = jnp.concatenate([s_ctx, s_loc], axis=-1)
        vals = jnp.concatenate([cv.astype(F32), vblk.astype(F32)], axis=1)
        return sink_attend(s, sk, vals)

    o = lax.map(one, jnp.arange(nb))
    return o.transpose(1, 0, 2, 3, 4, 5).reshape(B, T, N_HEADS * HEAD_DIM)


def setup_inputs(seed: int = 0) -> dict:
    key = jax.random.key(seed)
    ks = jax.random.split(key, 26)
    nrm = lambda k, shape, s: jax.random.normal(k, shape, F32) * s
    return {
        'x_prompt': nrm(ks[0], (BATCH, SEQ, D_MODEL), 1.0),
        'x_sample': nrm(ks[1], (DEC_BATCH, DEC_SEQ, D_MODEL), 1.0),
        'c': nrm(ks[2], (DEC_BATCH, D_MODEL), 1.0),
        'state_hgrn': nrm(ks[3], (DEC_BATCH, N_EVEN, 2, HG_HEADS, HG_KDIM, HG_VDIM), 0.5),
        'cache_k': nrm(ks[4], (DEC_BATCH, N_ODD, PAST_LEN, N_KV_HEADS, HEAD_DIM), 1.0),
        'cache_v': nrm(ks[5], (DEC_BATCH, N_ODD, PAST_LEN, N_KV_HEADS, HEAD_DIM), 1.0),
        'c_ctx': nrm(ks[6], (D_MODEL,), 1.0),
        'w_mod': nrm(ks[7], (DEPTH, D_MODEL, N_MOD * D_MODEL), 0.5 * D_MODEL ** -0.5),
        'b_mod': nrm(ks[8], (DEPTH, N_MOD * D_MODEL), 0.02),
        'norm_g': 1.0 + nrm(ks[9], (DEPTH, 3, D_MODEL), 0.05),
        'ffn_w_in': nrm(ks[10], (DEPTH, 2, D_MODEL, 2 * D_FF), D_MODEL ** -0.5),
        'ffn_w_out': nrm(ks[11], (DEPTH, 2, D_FF, D_MODEL), D_FF ** -0.5),
        'ev_w_in': nrm(ks[12], (N_EVEN, D_MODEL, EVEN_IN), D_MODEL ** -0.5),
        'ev_w_out': nrm(ks[13], (N_EVEN, EVEN_MIX, D_MODEL), EVEN_MIX ** -0.5),
        'conv_w': nrm(ks[14], (N_EVEN, CONV_W, CONV_CH), CONV_W ** -0.5),
        'conv_b': nrm(ks[15], (N_EVEN, CONV_CH), 0.02),
        'conv_ln_g': 1.0 + nrm(ks[16], (N_EVEN, CONV_CH), 0.05),
        'conv_ln_b': nrm(ks[17], (N_EVEN, CONV_CH), 0.02),
        'hg_lb_raw': nrm(ks[18], (2, DEPTH + 1, HG_WIDTH), 0.5),
        'hg_norm_g': 1.0 + nrm(ks[19], (N_EVEN, HG_VDIM), 0.05),
        'od_w_in': nrm(ks[20], (N_ODD, D_MODEL, ODD_IN), D_MODEL ** -0.5),
        'od_w_out': nrm(ks[21], (N_ODD, N_HEADS * HEAD_DIM, D_MODEL), (N_HEADS * HEAD_DIM) ** -0.5),
        'q_norm_g': 1.0 + nrm(ks[22], (N_ODD, HEAD_DIM), 0.05),
        'k_norm_g': 1.0 + nrm(ks[23], (N_ODD, HEAD_DIM), 0.05),
        'sinks': nrm(ks[24], (N_ODD, N_HEADS), 0.5),
    }


def reference(x_prompt, x_sample, c, state_hgrn, cache_k, cache_v, c_ctx, w_mod, b_mod, norm_g,
              ffn_w_in, ffn_w_out, ev_w_in, ev_w_out, conv_w, conv_b, conv_ln_g, conv_ln_b,
              hg_lb_raw, hg_norm_g, od_w_in, od_w_out, q_norm_g, k_norm_g, sinks):
    lb = jnp.cumsum(jax.nn.softmax(hg_lb_raw.astype(F32), axis=1), axis=1)

    xp = x_prompt
    hg_states, ks_new, vs_new = [], [], []
    for l in range(DEPTH):
        mod = adaln_params(c_ctx[None, :], w_mod[l], b_mod[l])
        xp = ffn_sublayer(xp, mod, norm_g[l, 0], ffn_w_in[l, 0], ffn_w_out[l, 0], 0)
        h = pre_norm(xp, norm_g[l, 1], mod[:, 3], mod[:, 4])
        if l % 2 == 0:
            e = l // 2
            s0 = jnp.zeros((xp.shape[0], HG_HEADS, HG_KDIM, HG_VDIM), F32)
            mix, s_f, s_b = even_mixer(h, ev_w_in[e], ev_w_out[e], conv_w[e], conv_b[e], conv_ln_g[e],
                                       conv_ln_b[e], lb[0, l], lb[1, l], hg_norm_g[e], s0, s0)
            hg_states.append(jnp.stack([s_f, s_b], axis=1))
        else:
            o = l // 2
            q, k, v = attn_qkv(h, od_w_in[o], q_norm_g[o], k_norm_g[o])
            mix = context_attention(q, k, v, sinks[o]).astype(h.dtype) @ od_w_out[o]
            ks_new.append(k)
            vs_new.append(v)
        xp = xp + mod[:, 5] * mix
        xp = ffn_sublayer(xp, mod, norm_g[l, 2], ffn_w_in[l, 1], ffn_w_out[l, 1], 2)
    y_prompt = xp
    new_state_hgrn = jnp.stack(hg_states, axis=1)
    new_cache_k = jnp.stack(ks_new, axis=1)
    new_cache_v = jnp.stack(vs_new, axis=1)

    xs = x_sample
    for l in range(DEPTH):
        mod = adaln_params(c, w_mod[l], b_mod[l])
        xs = ffn_sublayer(xs, mod, norm_g[l, 0], ffn_w_in[l, 0], ffn_w_out[l, 0], 0)
        h = pre_norm(xs, norm_g[l, 1], mod[:, 3], mod[:, 4])
        if l % 2 == 0:
            e = l // 2
            mix, _, _ = even_mixer(h, ev_w_in[e], ev_w_out[e], conv_w[e], conv_b[e], conv_ln_g[e],
                                   conv_ln_b[e], lb[0, l], lb[1, l], hg_norm_g[e],
                                   state_hgrn[:, e, 0], state_hgrn[:, e, 1])
        else:
            o = l // 2
            q, k, v = attn_qkv(h, od_w_in[o], q_norm_g[o], k_norm_g[o])
            q = axial_rope(q)
            k = axial_rope(k)
            mix = latent_attention(q, k, v, cache_k[:, o], cache_v[:, o], sinks[o]).astype(h.dtype) @ od_w_out[o]
        xs = xs + mod[:, 5] * mix
        xs = ffn_sublayer(xs, mod, norm_g[l, 2], ffn_w_in[l, 1], ffn_w_out[l, 1], 2)
    y_sample = xs

    return (y_prompt, y_sample, new_state_hgrn, new_cache_k, new_cache_v)
```

```python
import numpy as np
import concourse.bass as bass
import concourse.mybir as mybir
from concourse.bass_utils import run_bass_kernel_spmd

F32 = mybir.dt.float32
BF16 = mybir.dt.bfloat16
AF = mybir.ActivationFunctionType
ALU = mybir.AluOpType
AX = mybir.AxisListType

NCORES = 8
D = 1024
KC = 8
DFF = 2816
FC = 22
NT = 4096
NTILE = NT // 128
EPS = 1e-6


class Trk:
    __slots__ = ("name", "w", "r", "excl")

    def __init__(self, name="", excl=False):
        self.name = name
        self.w = None
        self.r = []
        self.excl = excl


class Sched:
    ENGS = ("pe", "act", "dve", "pool", "sp")

    def __init__(self, nc):
        self.nc = nc
        self.ops = {e: [] for e in self.ENGS}
        self.cnt = {}
        self.waited = {e: {} for e in self.ENGS}
        self.sems = {}
        self.pend = {e: ([], []) for e in self.ENGS}
        self._semctx = []
        for e in self.ENGS:
            self._mksem("E_" + e)
        self.nops = 0

    def _mksem(self, key):
        ctx = self.nc.semaphore(key)
        h = ctx.__enter__()
        self._semctx.append(ctx)
        self.sems[key] = h
        self.cnt[key] = 0
        return h

    def _deps(self, eng, reads, writes):
        need = {}

        def add(ev):
            if ev is None:
                return
            k, v = ev
            if need.get(k, 0) < v:
                need[k] = v
        for t in reads:
            add(t.w)
        for t in writes:
            add(t.w)
            for ev in t.r:
                add(ev)
        for e2 in self.ENGS:
            if e2 == eng:
                continue
            p = self.pend[e2]
            if p[0] or p[1]:
                ids = set(id(t) for t in p[1])
                idr = set(id(t) for t in p[0])
                for t in reads:
                    assert id(t) not in ids, ("pending unsignaled writer", e2, t.name)
                for t in writes:
                    assert id(t) not in ids and id(t) not in idr, ("pending unsignaled access", e2, t.name)
        out = []
        wd = self.waited[eng]
        for k, v in need.items():
            if wd.get(k, 0) >= v:
                continue
            wd[k] = v
            out.append((k, v))
        return out

    def op(self, eng, fn, reads=(), writes=(), signal=True):
        ex = [t for t in reads if t.excl]
        if ex:
            writes = list(writes) + ex
            reads = [t for t in reads if not t.excl]
        waits = self._deps(eng, reads, writes)
        pr, pw = self.pend[eng]
        pr.extend(reads)
        pw.extend(writes)
        key = "E_" + eng
        if signal:
            self.cnt[key] += 1
            ev = (key, self.cnt[key])
            for t in pw:
                t.w = ev
                t.r = []
            for t in pr:
                if t.w is not ev:
                    t.r.append(ev)
            self.pend[eng] = ([], [])
        self.ops[eng].append((waits, fn, (key, 1) if signal else None))
        self.nops += 1

    def dma(self, eng, fn, dsts, srcs=(), semkey=None):
        if semkey not in self.sems:
            self._mksem(semkey)
        waits = self._deps(eng, list(srcs), list(dsts))
        self.cnt[semkey] += 16
        ev = (semkey, self.cnt[semkey])
        for t in dsts:
            t.w = ev
            t.r = []
        for t in srcs:
            t.r.append(ev)
        self.ops[eng].append((waits, fn, (semkey, 16)))
        self.nops += 1

    def barrier(self):
        for e in self.ENGS:
            assert not self.pend[e][0] and not self.pend[e][1]
        for e in self.ENGS:
            wd = self.waited[e]
            waits = []
            for k, v in self.cnt.items():
                if v > 0 and wd.get(k, 0) < v:
                    wd[k] = v
                    waits.append((k, v))
            self.ops[e].append((waits, None, None))

    def emit(self):
        nc = self.nc
        sems = self.sems
        engobj = {"pe": "tensor", "act": "scalar", "dve": "vector", "pool": "gpsimd", "sp": "sync"}
        with nc.Block() as block:
            for e in self.ENGS:
                lst = self.ops[e]

                def body(eng, lst=lst):
                    for waits, fn, inc in lst:
                        for k, v in waits:
                            eng.wait_ge(sems[k], v)
                        if fn is not None:
                            ins = fn(eng)
                            if inc is not None:
                                ins.then_inc(sems[inc[0]], inc[1])
                getattr(block, engobj[e])(body)


class Carve:
    def __init__(self, arena):
        self.ar = arena
        self.off = 0

    def f32(self, n):
        a = self.ar[:, self.off:self.off + n]
        self.off += n
        assert self.off <= self.ar.shape[1], ("arena overflow", self.off)
        return a

    def bf(self, n):
        assert n % 2 == 0
        a = self.ar[:, self.off:self.off + n // 2].bitcast(BF16)
        self.off += n // 2
        assert self.off <= self.ar.shape[1], ("arena overflow", self.off)
        return a

    def ring(self, n, width, kind, name):
        return [((self.f32(width) if kind == "f32" else self.bf(width)), Trk("%s%d" % (name, i))) for i in range(n)]


class Ring:
    def __init__(self, nc, name, n, shape, dtype):
        self.n = n
        self.name = name
        self.t = [nc.alloc_sbuf_tensor("%s%d" % (name, i), shape, dtype) for i in range(n)]
        self.trk = [Trk("%s%d" % (name, i)) for i in range(n)]
        self.i = 0

    def next(self):
        i = self.i % self.n
        self.i += 1
        return i, self.t[i], self.trk[i]


class Prog:
    def __init__(self, mode="full"):
        self.mode = mode
        nc = bass.Bass("TRN2", target_bir_lowering=False)
        self.nc = nc
        self.s = Sched(nc)
        self.inp = {}
        self.steps = []

    def din(self, name, shape, dt=F32):
        ap = self.nc.dram_tensor(name, list(shape), dt, kind="ExternalInput").ap()
        self.inp[name] = ap
        return ap

    def dout(self, name, shape, dt=F32):
        return self.nc.dram_tensor(name, list(shape), dt, kind="ExternalOutput").ap()

    def dscr(self, name, shape, dt=F32):
        return self.nc.dram_tensor(name, list(shape), dt, kind="Internal").ap()

    POOLS = {"all": list(range(8)), "P": [0, 1, 2, 3, 4], "S": [5, 6, 7], "A3": [0, 1, 2], "B5": [3, 4, 5, 6, 7],
             "A2": [0, 1], "Bx": [2, 3, 4], "By": [5, 6, 7],
             "PA": [0, 1, 2], "PB": [3, 4, 5], "Sk": [6], "So": [7]}

    def bank(self, pool="all"):
        lst = self.POOLS[pool]
        c = self.bank_cnt.get(pool, 0)
        self.bank_cnt[pool] = c + 1
        i = lst[c % len(lst)]
        return self.banks[i], self.bank_trk[i]

    @staticmethod
    def interleave(gens):
        gens = [g for g in gens if g is not None]
        while gens:
            for g in list(gens):
                try:
                    next(g)
                except StopIteration:
                    gens.remove(g)

    def step(self, wloads, body):
        self.steps.append((wloads, body))

    def run_steps(self, lookahead=3):
        s = self.s
        steps = self.steps
        self.steps = []
        slots = {}
        ring = self.wring
        pos = [0]

        def issue(i):
            wl = steps[i][0]
            if wl is None:
                return
            si, st, strk = ring.next()
            slots[i] = (st, strk)
            for k, fn in enumerate(wl):
                s.dma("pool", (lambda e, fn=fn, st=st: fn(e, st)), [self.wpart_trk[si][k]], [], semkey="W%d_%d" % (si, k))
        n = len(steps)
        for i in range(min(lookahead, n)):
            issue(i)
        for i in range(n):
            if i + lookahead < n:
                issue(i + lookahead)
            wl, body = steps[i]
            if wl is None:
                body(None, None)
            else:
                st, strk = slots.pop(i)
                si = ring.t.index(st)
                body(st, list(self.wpart_trk[si]))

    def build(self):
        nc, s = self.nc, self.s
        mode = self.mode
        x_in = self.din("x_in", [NT, D])
        cond = self.din("cond", [128, KC])
        ident_d = self.din("ident", [128, 128])
        w_mod = self.din("w_mod", [2, D, 9 * D])
        bmodT = self.din("bmodT", [2, 128, 72])
        normgT = self.din("normgT", [2, 3, 128, KC])
        ffn_w_in = ffn_w_out = None
        if mode in ("full", "ffn1", "l0"):
            ffn_w_in = self.din("ffn_w_in", [2, 2, D, 2 * DFF])
            ffn_w_out = self.din("ffn_w_out", [2, 2, DFF, D])
        y_out = self.dout("y", [NT, D])
        A = {}
        if mode in ("full", "attn"):
            A["od_w_in"] = self.din("od_w_in", [1, D, 1536])
            A["od_w_out"] = self.din("od_w_out", [1, D, D])
            A["amask"] = self.din("amask", [128, 1536])
            A["cval"] = self.din("cval", [128, 64])
            A["qkgain"] = self.din("qkgain", [128, 128])
            A["sinks"] = self.din("sinks", [1, 16])
            A["ctx_k"] = self.din("ctx_k", [256, 256])
            A["ctx_v"] = self.din("ctx_v", [256, 256])
            A["ropeC"] = self.din("ropeC", [NT, 64])
            A["ropeS"] = self.din("ropeS", [NT, 64])
            A["nk"] = self.dout("nk", [8, 256, 256])
            A["nv"] = self.dout("nv", [8, 256, 256])
            self.t_nk, self.t_nv = Trk("nk"), Trk("nv")
        self.ain = A
        E = {}
        if mode in ("full", "l0") or mode.startswith("even"):
            E["ev_w_in"] = self.din("ev_w_in", [1, D, 3584])
            E["ev_w_out"] = self.din("ev_w_out", [1, D, D])
            E["convwT"] = self.din("convwT", [128, 124])
            E["convvec"] = self.din("convvec", [128, 12])
            E["tmask"] = self.din("tmask", [128, 512])
            E["lbraw"] = self.din("lbraw", [2, 1536])
            E["hgn"] = self.din("hgn", [128, 128])
            E["cumM"] = self.din("cumM", [128, 1024])
            E["scmask"] = self.din("scmask", [128, 128])
            E["cflag"] = self.din("cflag", [128, 1])
            E["tokm"] = self.din("tokm", [128, 4])
            E["s0"] = self.din("s0", [2, 4, 128, 128])
            E["hs_out"] = self.dout("hs_out", [8, 2, 4, 128, 128])
            self.t_hs = Trk("hs")
            self.cvs = self.dscr("cvs", [128, 4, NT], BF16)
            self.obs = self.dscr("obs", [NTILE, 128, 512])
        self.ein = E
        xs = self.dscr("xs", [D, NT])
        xs_v = xs.rearrange("(kc p) t -> p kc t", p=128)
        self.xs_trk = [[Trk("xs_%d_%d" % (kc, tt)) for tt in range(NT // 512)] for kc in range(KC)]

        self.banks = [nc.alloc_psum_tensor("bank%d" % i, [128, 512], F32) for i in range(8)]
        self.bank_trk = [Trk("bank%d" % i, excl=True) for i in range(8)]
        self.bank_cnt = {}
        NW = 4
        self.wring = Ring(nc, "wr", NW, [128, 4096], BF16)
        self.wpart_trk = [[Trk("wr%d_%d" % (i, k)) for k in range(4)] for i in range(NW)]
        ident_f = nc.alloc_sbuf_tensor("ident_f", [128, 128], F32)
        ident_b = nc.alloc_sbuf_tensor("ident_b", [128, 128], BF16)
        ones_b = nc.alloc_sbuf_tensor("ones_b", [128, 128], BF16)
        condt = nc.alloc_sbuf_tensor("condt", [128, KC], F32)
        sc_b = nc.alloc_sbuf_tensor("sc_b", [128, KC], BF16)
        modT = nc.alloc_sbuf_tensor("modT", [128, 72], F32)
        bmt = nc.alloc_sbuf_tensor("bmt", [128, 2, 72], F32)
        ngt = nc.alloc_sbuf_tensor("ngt", [128, 2, 3, KC], F32)
        modA = nc.alloc_sbuf_tensor("modA", [128, 3, KC], F32)
        modG = nc.alloc_sbuf_tensor("modG", [128, 3, KC], F32)
        t_const, t_cond, t_mod, t_modAG = Trk("const"), Trk("cond"), Trk("modT"), Trk("modAG")
        self.ones_b, self.t_const, self.ident_b, self.ident_f = ones_b, t_const, ident_b, ident_f
        self.modT, self.modA, self.modG, self.t_modAG = modT, modA, modG, t_modAG
        ARENA_F32 = 42 * 1024
        arena = nc.alloc_sbuf_tensor("arena", [128, ARENA_F32], F32)
        self.arena = arena

        s.dma("sp", lambda e: e.dma_start(out=ident_f[:], in_=ident_d), [t_const], [], semkey="C0")
        s.dma("sp", lambda e: e.dma_start(out=condt[:], in_=cond), [t_cond], [], semkey="C1")
        s.dma("sp", lambda e: e.dma_start(out=bmt[:], in_=bmodT.rearrange("l p j -> p l j")), [t_cond], [], semkey="C2")
        s.dma("sp", lambda e: e.dma_start(out=ngt[:], in_=normgT.rearrange("l s p k -> p l s k")), [t_cond], [], semkey="C3")
        s.op("dve", lambda e: e.tensor_copy(out=ident_b[:], in_=ident_f[:]), [t_const], [t_const])
        s.op("dve", lambda e: e.memset(ones_b[:], 1.0), [], [t_const])
        self.epsc = nc.alloc_sbuf_tensor("epsc", [128, 1], F32)
        s.op("dve", lambda e: e.memset(self.epsc[:], 1024.0 * EPS), [], [t_const])
        s.op("act", lambda e: e.activation(out=sc_b[:], in_=condt[:], func=AF.Silu), [t_cond], [t_cond])

        self.phase_T0(x_in, xs_v, ident_f, t_const)
        if mode == "ffn1":
            self.phase_mod(0, w_mod, sc_b, t_cond, bmt, ngt)
            self.phase_ffn(0, 0, ffn_w_in, ffn_w_out, xs_v, ones_b, t_const)
        elif mode == "attn":
            self.phase_mod(1, w_mod, sc_b, t_cond, bmt, ngt)
            self.phase_attn(xs_v)
        elif mode.startswith("even"):
            self.phase_mod(0, w_mod, sc_b, t_cond, bmt, ngt)
            self.phase_even(xs_v)
        elif mode in ("full", "l0"):
            for l in ((0, 1) if mode == "full" else (0,)):
                self.phase_mod(l, w_mod, sc_b, t_cond, bmt, ngt)
                self.phase_ffn(l, 0, ffn_w_in, ffn_w_out, xs_v, ones_b, t_const)
                if l == 0:
                    self.phase_even(xs_v)
                else:
                    self.phase_attn(xs_v)
                self.phase_ffn(l, 1, ffn_w_in, ffn_w_out, xs_v, ones_b, t_const)
        out_trks = self.phase_T1(xs_v, y_out, ident_f, t_const)
        s.barrier()
        s.emit()
        return nc

    def phase_T0(self, x_in, xs_v, ident_f, t_const):
        nc, s = self.nc, self.s
        s.barrier()
        ar = self.arena
        tin = [ar[:, i * 1024:(i + 1) * 1024] for i in range(3)]
        tin_trk = [Trk("t0in%d" % i) for i in range(3)]
        stg = [ar[:, 3072 + i * 4096: 3072 + (i + 1) * 4096].rearrange("p (k t) -> p k t", k=KC) for i in range(2)]
        stg_trk = [[Trk("t0stg%d_%d" % (i, h)) for h in range(2)] for i in range(2)]
        x_v = x_in.rearrange("(n p) d -> n p d", p=128)

        def load(n):
            i = n % 3
            s.dma("sp", lambda e: e.dma_start(out=tin[i], in_=x_v[n]), [tin_trk[i]], [], semkey="T0in%d" % i)
        load(0)
        load(1)
        for n in range(NTILE):
            if n + 2 < NTILE:
                load(n + 2)
            i = n % 3
            g = n // 4
            si = g % 2
            for half in range(2):
                bk, bt = self.bank()
                for q in range(4):
                    kc = half * 4 + q
                    s.op("pe", lambda e, kc=kc, q=q, bk=bk, i=i: e.transpose(
                        out=bk[:, q * 128:(q + 1) * 128], in_=tin[i][:, kc * 128:(kc + 1) * 128], identity=ident_f[:]),
                        [tin_trk[i], t_const], [bt], signal=(q == 3))
                dst = stg[si][:, half * 4:(half + 1) * 4, (n % 4) * 128:(n % 4 + 1) * 128]
                src = bk[:, :].rearrange("p (k t) -> p k t", k=4)
                eng = "dve" if half == 0 else "act"
                if eng == "dve":
                    s.op("dve", lambda e, dst=dst, src=src: e.tensor_copy(out=dst, in_=src), [bt], [stg_trk[si][half]])
                else:
                    s.op("act", lambda e, dst=dst, src=src: e.activation(out=dst, in_=src, func=AF.Copy), [bt], [stg_trk[si][half]])
            if n % 4 == 3:
                tt = n // 4
                s.dma("sp", lambda e, si=si, tt=tt: e.dma_start(out=xs_v[:, :, tt * 512:(tt + 1) * 512], in_=stg[si]),
                      [self.xs_trk[kc][tt] for kc in range(KC)], list(stg_trk[si]), semkey="T0st%d" % si)

    def phase_T1(self, xs_v, y_out, ident_f, t_const):
        nc, s = self.nc, self.s
        s.barrier()
        ar = self.arena
        fin = [ar[:, i * 4096:(i + 1) * 4096].rearrange("p (k t) -> p k t", k=KC) for i in range(2)]
        fin_trk = [Trk("t1in%d" % i) for i in range(2)]
        tout = [ar[:, 8192 + i * 1024: 8192 + (i + 1) * 1024] for i in range(3)]
        tout_trk = [Trk("t1out%d" % i) for i in range(3)]
        y_v = y_out.rearrange("(n p) d -> n p d", p=128)
        ytrk = Trk("y")

        def load(tt):
            i = tt % 2
            s.dma("sp", lambda e: e.dma_start(out=fin[i], in_=xs_v[:, :, tt * 512:(tt + 1) * 512]), [fin_trk[i]],
                  [self.xs_trk[kc][tt] for kc in range(KC)], semkey="T1in%d" % i)
        load(0)
        for tt in range(NT // 512):
            if tt + 1 < NT // 512:
                load(tt + 1)
            i = tt % 2
            for q4 in range(4):
                n = tt * 4 + q4
                oi = n % 3
                for half in range(2):
                    bk, bt = self.bank()
                    for q in range(4):
                        kc = half * 4 + q
                        s.op("pe", lambda e, kc=kc, q=q, bk=bk, i=i, q4=q4: e.transpose(
                            out=bk[:, q * 128:(q + 1) * 128], in_=fin[i][:, kc, q4 * 128:(q4 + 1) * 128], identity=ident_f[:]),
                            [fin_trk[i], t_const], [bt], signal=(q == 3))
                    dst = tout[oi][:, half * 512:(half + 1) * 512]
                    if half == 0:
                        s.op("dve", lambda e, dst=dst, bk=bk: e.tensor_copy(out=dst, in_=bk[:, :]), [bt], [tout_trk[oi]])
                    else:
                        s.op("act", lambda e, dst=dst, bk=bk: e.activation(out=dst, in_=bk[:, :], func=AF.Copy), [bt], [tout_trk[oi]])
                s.dma("sp", lambda e, oi=oi, n=n: e.dma_start(out=y_v[n], in_=tout[oi]), [ytrk], [tout_trk[oi]],
                      semkey="T1st%d" % oi)

    def phase_mod(self, l, w_mod, sc_b, t_cond, bmt, ngt):
        nc, s = self.nc, self.s
        s.barrier()
        modT, modA, modG, t_modAG = self.modT, self.modA, self.modG, self.t_modAG
        bk, bt = self.bank()
        wv = w_mod[l].rearrange("(kc p) c -> p kc c", p=128)
        t_mod = Trk("modT")
        for cb in range(18):
            def wl(e, st, cb=cb):
                return e.dma_start(out=st[:, :].rearrange("p (k c) -> p k c", k=KC), in_=wv[:, :, cb * 512:(cb + 1) * 512])

            def body(st, strk, cb=cb):
                stv = st[:, :].rearrange("p (k c) -> p k c", k=KC)
                for q in range(4):
                    j = cb * 4 + q
                    for kc in range(KC):
                        last = (cb == 17 and q == 3 and kc == KC - 1)
                        s.op("pe", lambda e, q=q, kc=kc, j=j, stv=stv: e.matmul(
                            bk[:, j:j + 1], stv[:, kc, q * 128:(q + 1) * 128], sc_b[:, kc:kc + 1],
                            start=(kc == 0), stop=(kc == KC - 1)),
                            list(strk) + [t_cond], [bt], signal=(kc == KC - 1))
            self.step([wl], body)
        self.run_steps()
        s.op("dve", lambda e: e.tensor_tensor(out=modT[:], in0=bk[:, 0:72], in1=bmt[:, l, :], op=ALU.add), [bt, t_cond], [t_mod])
        mv = modT[:, :].rearrange("p (v k) -> p v k", k=KC)
        for sl in range(3):
            s.op("dve", lambda e, sl=sl: e.scalar_tensor_tensor(
                out=modA[:, sl, :], in0=mv[:, 3 * sl + 1, :], scalar=1.0, in1=ngt[:, l, sl, :], op0=ALU.add, op1=ALU.mult),
                [t_mod, t_cond], [t_modAG])
            s.op("dve", lambda e, sl=sl: e.tensor_scalar(
                out=modA[:, sl, :], in0=modA[:, sl, :], scalar1=32.0, scalar2=None, op0=ALU.mult), [t_modAG], [t_modAG])
            gsc = 1.0 if sl == 1 else 0.5
            s.op("dve", lambda e, sl=sl, gsc=gsc: e.tensor_scalar(
                out=modG[:, sl, :], in0=mv[:, 3 * sl + 2, :], scalar1=gsc, scalar2=None, op0=ALU.mult), [t_mod], [t_modAG])
        self.t_mod = t_mod

    def norm_tile(self, xt, xt_trk, sl, hdst, hdst_trk, nr, W=512, pool="all"):
        s = self.s
        modT, modA = self.modT, self.modA
        ones_b, t_const = self.ones_b, self.t_const
        bk, bt = self.bank(pool)
        for kc in range(KC):
            sq, sq_trk = nr["sq"][kc % len(nr["sq"])]
            s.op("act", lambda e, kc=kc, sq=sq: e.activation(out=sq, in_=xt[:, kc, :], func=AF.Square), [xt_trk], [sq_trk])
            s.op("pe", lambda e, kc=kc, sq=sq: e.matmul(bk[:, 0:W], ones_b[:, :], sq, start=(kc == 0), stop=(kc == KC - 1)),
                 [sq_trk, t_const], [bt], signal=True)
        rstd, rstd_trk = nr["rstd"]
        s.op("act", lambda e: e.activation(out=rstd, in_=bk[:, 0:W], func=AF.Ln, bias=self.epsc[:, 0:1], scale=1.0), [bt, t_const], [rstd_trk])
        s.op("act", lambda e: e.activation(out=rstd, in_=rstd, func=AF.Exp, scale=-0.5), [rstd_trk], [rstd_trk])
        for kc in range(KC):
            tmp, tmp_trk = nr["tmp"][kc % len(nr["tmp"])]
            s.op("dve", lambda e, kc=kc, tmp=tmp: e.tensor_tensor(out=tmp, in0=xt[:, kc, :], in1=rstd, op=ALU.mult),
                 [xt_trk, rstd_trk], [tmp_trk])
            s.op("act", lambda e, kc=kc, tmp=tmp: e.activation(out=hdst[:, kc, :], in_=tmp, func=AF.Identity,
                                                               bias=modT[:, 3 * sl * KC + kc: 3 * sl * KC + kc + 1],
                                                               scale=modA[:, sl, kc:kc + 1]),
                 [tmp_trk, self.t_mod, self.t_modAG], [hdst_trk])

    def phase_attn(self, xs_v):
        nc, s = self.nc, self.s
        A = self.ain
        sl = 1
        s.barrier()
        cv = Carve(self.arena)
        ident_b, ones_b, t_const = self.ident_b, self.ones_b, self.t_const
        win = cv.bf(KC * 1536).rearrange("p (k c) -> p k c", k=KC)
        t_win = [Trk("awin%d" % i) for i in range(3)]
        kT = cv.bf(2 * NT).rearrange("p (r t) -> p r t", r=2)
        t_kT = [Trk("kT%d" % n) for n in range(NTILE)]
        Vv = cv.bf(NTILE * 256).rearrange("p (n f) -> p n f", n=NTILE)
        t_V = [Trk("V%d" % n) for n in range(NTILE)]
        kTc = cv.bf(512).rearrange("p (r t) -> p r t", r=2)
        Vc = cv.bf(512).rearrange("p (c f) -> p c f", c=2)
        t_ctx = Trk("ctx")
        msk = cv.bf(1536).rearrange("p (v j q) -> p v j q", v=6, j=2)
        cval = cv.bf(64)
        t_msk = Trk("msk")
        gain = cv.f32(128)
        t_gain = Trk("gain")
        epsq = cv.f32(1)
        xin = cv.f32(KC * 512).rearrange("p (k t) -> p k t", k=KC)
        t_xin = Trk("axin")
        nr = {"sq": cv.ring(2, 512, "bf", "asq"), "tmp": cv.ring(2, 512, "f32", "atmp"), "rstd": (cv.f32(512), Trk("arstd"))}
        h = cv.bf(KC * 512).rearrange("p (k t) -> p k t", k=KC)
        t_h = Trk("ah")
        qkv, t_qkv = cv.f32(1536), Trk("qkv")
        bufA, t_bufA = cv.f32(1280), Trk("bufA")
        bufB, t_bufB = cv.f32(1280), Trk("bufB")
        ss, t_ss = cv.f32(20), Trk("ss")
        rinv, t_rinv = cv.f32(20), Trk("rinv")
        ropeC = cv.ring(2, 64, "f32", "ropeC")
        ropeS = cv.ring(2, 64, "f32", "ropeS")
        qb, t_qb = cv.bf(1024), Trk("qb")
        kb, t_kb = cv.bf(256), Trk("kb")
        qT = cv.ring(3, 1024, "bf", "qT")
        pt = cv.ring(10, 512, "bf", "pt")
        rec = cv.ring(2, 512, "f32", "rec")
        at_g = cv.bf(8 * 512).rearrange("p (c t) -> p c t", c=8)
        t_at = [Trk("at%d" % i) for i in range(4)]
        xc = cv.ring(2, 512, "f32", "axc")
        xo = cv.ring(2, 512, "f32", "axo")
        esink = cv.bf(2048)
        sk, t_sk = cv.f32(16), Trk("sk")
        ctxs = cv.f32(512)
        t_ctxs = Trk("ctxs")
        ctxb = cv.bf(512)
        cnt = {"pt": 0, "rec": 0, "xc": 0}

        wv = A["od_w_in"][0].rearrange("(kc p) c -> p kc c", p=128)
        for c3 in range(3):
            s.dma("pool", lambda e, c3=c3: e.dma_start(out=win[:, :, c3 * 512:(c3 + 1) * 512], in_=wv[:, :, c3 * 512:(c3 + 1) * 512]),
                  [t_win[c3]], [], semkey="AW%d" % c3)
        s.dma("pool", lambda e: e.dma_start(out=msk.rearrange("p v j q -> p (v j q)"), in_=A["amask"]), [t_msk], [], semkey="AM")
        s.dma("pool", lambda e: e.dma_start(out=cval, in_=A["cval"]), [t_msk], [], semkey="AM")
        s.dma("sp", lambda e: e.dma_start(out=gain, in_=A["qkgain"]), [t_gain], [], semkey="AG")
        s.op("dve", lambda e: e.tensor_scalar(out=gain[:, 0:64], in0=gain[:, 0:64], scalar1=0.125, scalar2=None, op0=ALU.mult), [t_gain], [t_gain])
        s.op("dve", lambda e: e.memset(epsq, EPS), [], [t_gain])
        s.dma("sp", lambda e: e.dma_start(out=sk[0:1, :], in_=A["sinks"]), [t_sk], [], semkey="ASK")
        s.op("act", lambda e: e.activation(out=sk[0:1, :], in_=sk[0:1, :], func=AF.Exp), [t_sk], [t_sk])
        s.op("dve", lambda e: e.tensor_copy(out=esink[0:1, :].rearrange("p (h q) -> p h q", h=16),
                                            in_=sk[0:1, :].unsqueeze(2).to_broadcast([1, 16, 128])), [t_sk], [t_sk])
        s.dma("sp", lambda e: e.dma_start(out=ctxs.rearrange("p (c f) -> p c f", c=2), in_=A["ctx_k"].rearrange("(c p) f -> p c f", p=128)),
              [t_ctxs], [], semkey="ACX")
        s.op("dve", lambda e: e.tensor_copy(out=ctxb, in_=ctxs), [t_ctxs], [t_ctxs])
        bk, bt = self.bank()
        bkb = bk[:, :].bitcast(BF16)
        for c in range(2):
            for pr in range(2):
                i4 = c * 2 + pr
                s.op("pe", lambda e, c=c, pr=pr, i4=i4: e.transpose(out=bkb[:, i4 * 128:(i4 + 1) * 128],
                                                                    in_=ctxb[:, c * 256 + pr * 128: c * 256 + (pr + 1) * 128], identity=ident_b[:]),
                     [t_ctxs, t_const], [bt], signal=(i4 == 3))
        s.op("dve", lambda e: e.tensor_copy(out=kTc.rearrange("p r (c t) -> p c r t", c=2),
                                            in_=bkb[:, 0:512].rearrange("p (c r t) -> p c r t", c=2, r=2)), [bt], [t_ctx])
        s.dma("sp", lambda e: e.dma_start(out=ctxs.rearrange("p (c f) -> p c f", c=2), in_=A["ctx_v"].rearrange("(c p) f -> p c f", p=128)),
              [t_ctxs], [], semkey="ACX")
        s.op("dve", lambda e: e.tensor_copy(out=Vc.rearrange("p c f -> p (c f)"), in_=ctxs), [t_ctxs], [t_ctxs, t_ctx])

        def mvar(n):
            return 0 if n == 0 else (5 if n == NTILE - 1 else 1 + n % 4)

        def stageA(n):
            G, p4 = n // 4, n % 4
            if p4 == 0:
                s.dma("sp", lambda e: e.dma_start(out=xin, in_=xs_v[:, :, G * 512:(G + 1) * 512]), [t_xin],
                      [self.xs_trk[kc][G] for kc in range(KC)], semkey="AXin")
                self.norm_tile(xin, t_xin, sl, h, t_h, nr, pool="A2")
                yield
            rC, t_rC = ropeC[n % 2]
            rS, t_rS = ropeS[n % 2]
            s.dma("sp", lambda e: e.dma_start(out=rC, in_=A["ropeC"][n * 128:(n + 1) * 128, :]), [t_rC], [], semkey="ARC%d" % (n % 2))
            s.dma("sp", lambda e: e.dma_start(out=rS, in_=A["ropeS"][n * 128:(n + 1) * 128, :]), [t_rS], [], semkey="ARS%d" % (n % 2))
            for c3 in range(3):
                bk, bt = self.bank("A2")
                for kc in range(KC):
                    s.op("pe", lambda e, kc=kc, c3=c3, bk=bk: e.matmul(bk[:, :], h[:, kc, p4 * 128:(p4 + 1) * 128], win[:, kc, c3 * 512:(c3 + 1) * 512],
                                                                       start=(kc == 0), stop=(kc == KC - 1)),
                         [t_h, t_win[c3]], [bt], signal=(kc == KC - 1))
                if c3 < 2:
                    s.op("act", lambda e, c3=c3, bk=bk: e.activation(out=qkv[:, c3 * 512:(c3 + 1) * 512], in_=bk[:, :], func=AF.Copy), [bt], [t_qkv])
                else:
                    s.op("dve", lambda e, c3=c3, bk=bk: e.tensor_copy(out=qkv[:, c3 * 512:(c3 + 1) * 512], in_=bk[:, :]), [bt], [t_qkv])
                yield
            s.op("act", lambda e: e.activation(out=bufA, in_=qkv[:, 0:1280], func=AF.Square), [t_qkv], [t_bufA])
            s.op("dve", lambda e: e.tensor_reduce(out=ss, in_=bufA.rearrange("p (h d) -> p h d", d=64), axis=AX.X, op=ALU.add), [t_bufA], [t_ss])
            s.op("act", lambda e: e.activation(out=rinv, in_=ss, func=AF.Ln, bias=epsq[:, 0:1], scale=1.0 / 64.0), [t_ss, t_gain], [t_rinv])
            s.op("act", lambda e: e.activation(out=rinv, in_=rinv, func=AF.Exp, scale=-0.5), [t_rinv], [t_rinv])
            yield
            s.op("dve", lambda e: e.tensor_tensor(out=bufB.rearrange("p (h d) -> p h d", d=64), in0=qkv[:, 0:1280].rearrange("p (h d) -> p h d", d=64),
                                                  in1=rinv.unsqueeze(2).to_broadcast([128, 20, 64]), op=ALU.mult), [t_qkv, t_rinv], [t_bufB])
            s.op("pool", lambda e: e.tensor_tensor(out=bufB[:, 0:1024].rearrange("p (h d) -> p h d", d=64), in0=bufB[:, 0:1024].rearrange("p (h d) -> p h d", d=64),
                                                   in1=gain[:, 0:64].unsqueeze(1).to_broadcast([128, 16, 64]), op=ALU.mult), [t_bufB, t_gain], [t_bufB])
            s.op("pool", lambda e: e.tensor_tensor(out=bufB[:, 1024:1280].rearrange("p (h d) -> p h d", d=64), in0=bufB[:, 1024:1280].rearrange("p (h d) -> p h d", d=64),
                                                   in1=gain[:, 64:128].unsqueeze(1).to_broadcast([128, 4, 64]), op=ALU.mult), [t_bufB, t_gain], [t_bufB])
            yield
            s.op("dve", lambda e: e.tensor_tensor(out=bufA.rearrange("p (h d) -> p h d", d=64), in0=bufB.rearrange("p (h d) -> p h d", d=64),
                                                  in1=rC.unsqueeze(1).to_broadcast([128, 20, 64]), op=ALU.mult), [t_bufB, t_rC], [t_bufA])
            xv = bufB.rearrange("p (h a f e) -> p h a f e", a=2, f=2, e=16)
            tv = qkv[:, 0:1280].rearrange("p (h a f e) -> p h a f e", a=2, f=2, e=16)
            sv = rS.rearrange("p (a f e) -> p a f e", a=2, f=2)
            for f in range(2):
                s.op("pool", lambda e, f=f: e.tensor_tensor(out=tv[:, :, :, f, :], in0=xv[:, :, :, 1 - f, :],
                                                            in1=sv[:, :, f, :].unsqueeze(1).to_broadcast([128, 20, 2, 16]), op=ALU.mult),
                     [t_bufB, t_rS], [t_qkv])
            s.op("dve", lambda e: e.tensor_tensor(out=bufB, in0=bufA, in1=qkv[:, 0:1280], op=ALU.add), [t_bufA, t_qkv], [t_bufB])
            yield
            for pr in range(2):
                s.op("act", lambda e, pr=pr: e.activation(
                    out=qb[:, pr * 512:(pr + 1) * 512].rearrange("p (g two d) -> p g two d", g=4, two=2),
                    in_=bufB[:, pr * 512:(pr + 1) * 512].rearrange("p (two g d) -> p g two d", two=2, g=4), func=AF.Copy), [t_bufB], [t_qb])
            s.op("act", lambda e: e.activation(out=kb, in_=bufB[:, 1024:1280], func=AF.Copy), [t_bufB], [t_kb])
            s.op("pool", lambda e: e.tensor_copy(out=Vv[:, n, :], in_=qkv[:, 1280:1536]), [t_qkv], [t_V[n]])
            yield
            if p4 < 2:
                s.dma("sp", lambda e: e.dma_start(out=A["nk"][G, p4 * 128:(p4 + 1) * 128, :], in_=bufB[:, 1024:1280]), [self.t_nk], [t_bufB], semkey="ANK")
                s.dma("sp", lambda e: e.dma_start(out=A["nv"][G, p4 * 128:(p4 + 1) * 128, :], in_=qkv[:, 1280:1536]), [self.t_nv], [t_qkv], semkey="ANV")
            bk, bt = self.bank("A2")
            bkb = bk[:, :].bitcast(BF16)
            for pr in range(2):
                for g in range(4):
                    i8 = pr * 4 + g
                    s.op("pe", lambda e, pr=pr, g=g, i8=i8, bkb=bkb: e.transpose(out=bkb[:, i8 * 128:(i8 + 1) * 128],
                                                                                 in_=qb[:, i8 * 128:(i8 + 1) * 128], identity=ident_b[:]),
                         [t_qb, t_const], [bt], signal=(i8 == 7))
            qTt, t_qT = qT[n % 3]
            s.op("act", lambda e, bkb=bkb, qTt=qTt: e.activation(out=qTt, in_=bkb[:, :], func=AF.Copy), [bt], [t_qT])
            yield
            bk2, bt2 = self.bank("A2")
            bk2b = bk2[:, :].bitcast(BF16)
            for pr in range(2):
                s.op("pe", lambda e, pr=pr, bk2b=bk2b: e.transpose(out=bk2b[:, pr * 128:(pr + 1) * 128], in_=kb[:, pr * 128:(pr + 1) * 128], identity=ident_b[:]),
                     [t_kb, t_const], [bt2], signal=(pr == 1))
            s.op("dve", lambda e, bk2b=bk2b: e.tensor_copy(out=kT[:, :, n * 128:(n + 1) * 128], in_=bk2b[:, 0:256].rearrange("p (r t) -> p r t", r=2)),
                 [bt2], [t_kT[n]])

        def stageB(n, khs, bpool, ptr, rci):
            G, p4 = n // 4, n % 4
            var = mvar(n)
            qTt, t_qT = qT[n % 3]
            qTv = qTt.rearrange("p (r gq) -> p r gq", r=2)
            loc = [max(n - 1, 0), n, min(n + 1, NTILE - 1)]
            for kh in khs:
                pr, lo = kh // 2, (kh % 2) * 64
                pts = []
                for j in range(5):
                    if j < 2:
                        lk = kTc[lo:lo + 64, pr, j * 128:(j + 1) * 128]
                        lv = Vc[:, j, kh * 64:(kh + 1) * 64]
                        rd = [t_ctx]
                    else:
                        m = loc[j - 2]
                        lk = kT[lo:lo + 64, pr, m * 128:(m + 1) * 128]
                        lv = Vv[:, m, kh * 64:(kh + 1) * 64]
                        rd = [t_kT[m], t_V[m]]
                    bk, bt = self.bank(bpool)
                    rq = qTv[lo:lo + 64, pr, :]
                    if j in (2, 4):
                        s.op("pe", lambda e, lk=lk, bk=bk, rq=rq: e.matmul(bk[:, :], lk, rq, start=True, stop=False),
                             rd + [t_qT], [bt], signal=False)
                        mk = msk[:, var, (j - 2) // 2, :]
                        for g in range(4):
                            s.op("pe", lambda e, mk=mk, bk=bk, g=g: e.matmul(bk[:, g * 128:(g + 1) * 128], ident_b[:, :], mk, start=False, stop=(g == 3)),
                                 [t_msk, t_const], [bt], signal=(g == 3))
                    else:
                        s.op("pe", lambda e, lk=lk, bk=bk, rq=rq: e.matmul(bk[:, :], lk, rq, start=True, stop=True),
                             rd + [t_qT], [bt], signal=True)
                    ptt, t_pt = ptr[j]
                    s.op("act", lambda e, bk=bk, ptt=ptt: e.activation(out=ptt, in_=bk[:, :], func=AF.Exp), [bt], [t_pt])
                    pts.append((ptt, t_pt, lv, rd))
                    yield
                bo, bot = self.bank(bpool)
                bd, bdt = self.bank(bpool)
                bo_v = bo[lo:lo + 64, :]
                bd_v = bd[lo:lo + 64, :]
                for j, (ptt, t_pt, lv, rd) in enumerate(pts):
                    s.op("pe", lambda e, ptt=ptt, lv=lv, j=j, bo_v=bo_v: e.matmul(bo_v, lv, ptt, start=(j == 0), stop=(j == 4)),
                         rd + [t_pt], [bot], signal=(j == 4))
                yield
                es = esink[0:1, kh * 512:(kh + 1) * 512]
                s.op("pe", lambda e, bd_v=bd_v, es=es: e.matmul(bd_v, ones_b[0:1, 0:64], es, start=True, stop=False),
                     [t_sk, t_const], [bdt], signal=False)
                for j, (ptt, t_pt, lv, rd) in enumerate(pts):
                    dl = cval[:, 0:64] if j < 2 else ones_b[:, 0:64]
                    s.op("pe", lambda e, ptt=ptt, j=j, bd_v=bd_v, dl=dl: e.matmul(bd_v, dl, ptt, start=False, stop=(j == 4)),
                         [t_pt, t_const, t_msk], [bdt], signal=(j == 4))
                yield
                rc, t_rc = rec[rci]
                rc_v = rc[lo:lo + 64, :]
                at_v = at_g[lo:lo + 64, pr * 4:(pr + 1) * 4, p4 * 128:(p4 + 1) * 128]
                s.op("act", lambda e, rc_v=rc_v, bd_v=bd_v: e.activation(out=rc_v, in_=bd_v, func=AF.Ln), [bdt], [t_rc])
                s.op("act", lambda e, rc_v=rc_v: e.activation(out=rc_v, in_=rc_v, func=AF.Exp, scale=-1.0), [t_rc], [t_rc])
                s.op("dve", lambda e, rc_v=rc_v, bo_v=bo_v, at_v=at_v: e.tensor_tensor(
                    out=at_v, in0=bo_v.rearrange("p (g q) -> p g q", g=4), in1=rc_v.rearrange("p (g q) -> p g q", g=4), op=ALU.mult),
                    [bot, t_rc], [t_at[p4]])
                yield

        wo_v = A["od_w_out"][0].rearrange("(r two g d) c -> two d r g c", r=2, two=2, g=4, d=64)

        def outproj_steps(G):
            for dc in range(KC):
                def mk_wl(two, r, dc=dc):
                    def wl(e, st):
                        return e.dma_start(out=st[two * 64:(two + 1) * 64, r * 512:(r + 1) * 512].rearrange("p (g c) -> p g c", g=4),
                                           in_=wo_v[two, :, r, :, dc * 128:(dc + 1) * 128])
                    return wl
                wls = [mk_wl(two, r) for two in range(2) for r in range(2)]

                def body(st, strk, dc=dc, G=G):
                    wv_ = st[:, 0:1024].rearrange("p (c8 c) -> p c8 c", c8=8)
                    ci = cnt["xc"] % 2
                    cnt["xc"] += 1
                    xct, t_xc = xc[ci]
                    xot, t_xo = xo[ci]
                    s.dma("sp", lambda e: e.dma_start(out=xct, in_=xs_v[:, dc, G * 512:(G + 1) * 512]), [t_xc], [self.xs_trk[dc][G]], semkey="AXc%d" % ci)
                    bk, bt = self.bank()
                    for c8 in range(8):
                        s.op("pe", lambda e, c8=c8, bk=bk: e.matmul(bk[:, :], wv_[:, c8, :], at_g[:, c8, :], start=(c8 == 0), stop=(c8 == 7)),
                             list(strk) + list(t_at), [bt], signal=(c8 == 7))
                    s.op("dve", lambda e, bk=bk: e.scalar_tensor_tensor(out=xot, in0=bk[:, :], scalar=self.modG[:, sl, dc:dc + 1], in1=xct,
                                                                        op0=ALU.mult, op1=ALU.add), [bt, t_xc, self.t_modAG], [t_xo])
                    s.dma("sp", lambda e: e.dma_start(out=xs_v[:, dc, G * 512:(G + 1) * 512], in_=xot), [self.xs_trk[dc][G]], [t_xo], semkey="AXo%d" % ci)
                self.step(wls, body)

        for n in range(NTILE + 2):
            ga = (lambda n=n: stageA(n)) if n < NTILE else None
            gb = (lambda n=n: stageB(n - 2, (0, 2), "Bx", pt[0:5], 0)) if n >= 2 else None
            gc = (lambda n=n: stageB(n - 2, (1, 3), "By", pt[5:10], 1)) if n >= 2 else None
            self.step(None, lambda a, b, ga=ga, gb=gb, gc=gc: self.interleave([ga() if ga else None, gb() if gb else None, gc() if gc else None]))
            if n >= 2 and (n - 2) % 4 == 3:
                outproj_steps((n - 2) // 4)
        self.run_steps()

    def phase_even(self, xs_v):
        nc, s = self.nc, self.s
        A = self.ein
        sl = 1
        s.barrier()
        ident_b, ones_b, t_const = self.ident_b, self.ones_b, self.t_const
        cvs, obs = self.cvs, self.obs
        t_cvs = [Trk("cvs%d" % g) for g in range(8)]
        t_obs = [Trk("obs%d" % n) for n in range(NTILE)]
        cv = Carve(self.arena)
        xin, t_xin = cv.f32(KC * 512).rearrange("p (k t) -> p k t", k=KC), Trk("exin")
        h, t_h = cv.bf(KC * 512).rearrange("p (k t) -> p k t", k=KC), Trk("eh")
        nr = {"sq": cv.ring(2, 512, "bf", "esq"), "tmp": cv.ring(2, 512, "f32", "etmp"), "rstd": (cv.f32(512), Trk("erstd"))}
        ones_f = cv.f32(128)
        cwT = cv.f32(124).rearrange("p (c j) -> p c j", c=4)
        cvec = cv.f32(12).rearrange("p (w c) -> p w c", w=3)
        tmask = cv.bf(512)
        lbt = [cv.f32(512) for d_ in range(2)]
        oml = [cv.f32(512) for d_ in range(2)]
        hgn = cv.f32(128)
        cumM = cv.f32(1024).rearrange("p (d m t) -> p d m t", d=2, m=4)
        scm = cv.bf(128).rearrange("p (d t) -> p d t", d=2)
        cflag = cv.f32(1)
        tokm = cv.f32(4)
        eps5 = cv.f32(1)
        epsq = cv.f32(1)
        t_ec = Trk("econst")
        base = cv.off

        s.op("dve", lambda e: e.memset(ones_f, 1.0), [], [t_ec])
        s.op("dve", lambda e: e.memset(eps5, 512.0 * 1e-5), [], [t_ec])
        s.op("dve", lambda e: e.memset(epsq, EPS), [], [t_ec])
        s.dma("sp", lambda e: e.dma_start(out=cwT.rearrange("p c j -> p (c j)"), in_=A["convwT"]), [t_ec], [], semkey="EC0")
        s.dma("sp", lambda e: e.dma_start(out=cvec.rearrange("p w c -> p (w c)"), in_=A["convvec"]), [t_ec], [], semkey="EC0")
        s.dma("pool", lambda e: e.dma_start(out=tmask, in_=A["tmask"]), [t_ec], [], semkey="EC1")
        s.dma("sp", lambda e: e.dma_start(out=hgn, in_=A["hgn"]), [t_ec], [], semkey="EC0")
        s.dma("sp", lambda e: e.dma_start(out=cumM.rearrange("p d m t -> p (d m t)"), in_=A["cumM"]), [t_ec], [], semkey="EC0")
        s.dma("pool", lambda e: e.dma_start(out=scm.rearrange("p d t -> p (d t)"), in_=A["scmask"]), [t_ec], [], semkey="EC1")
        s.dma("sp", lambda e: e.dma_start(out=cflag, in_=A["cflag"]), [t_ec], [], semkey="EC0")
        s.dma("sp", lambda e: e.dma_start(out=tokm, in_=A["tokm"]), [t_ec], [], semkey="EC0")
        raw = self.arena[0:1, base:base + 1536]
        t_raw = Trk("lbraw")
        for d_ in range(2):
            s.dma("sp", lambda e, d_=d_: e.dma_start(out=raw, in_=A["lbraw"][d_:d_ + 1, :]), [t_raw], [], semkey="EC2")
            s.op("act", lambda e: e.activation(out=raw, in_=raw, func=AF.Exp), [t_raw], [t_raw])
            s.op("dve", lambda e: e.tensor_tensor(out=raw[:, 512:1024], in0=raw[:, 512:1024], in1=raw[:, 1024:1536], op=ALU.add), [t_raw], [t_raw])
            s.op("dve", lambda e: e.tensor_tensor(out=raw[:, 512:1024], in0=raw[:, 512:1024], in1=raw[:, 0:512], op=ALU.add), [t_raw], [t_raw])
            s.op("dve", lambda e: e.reciprocal(out=raw[:, 512:1024], in_=raw[:, 512:1024]), [t_raw], [t_raw])
            s.op("dve", lambda e: e.tensor_tensor(out=raw[:, 0:512], in0=raw[:, 0:512], in1=raw[:, 512:1024], op=ALU.mult), [t_raw], [t_raw])
            bk, bt = self.bank()
            s.op("pe", lambda e, bk=bk: e.matmul(bk[:, :], ones_f[0:1, :], raw[:, 0:512], start=True, stop=True), [t_raw, t_ec], [bt])
            s.op("dve", lambda e, bk=bk, d_=d_: e.tensor_copy(out=lbt[d_], in_=bk[:, :]), [bt], [t_ec])
            s.op("dve", lambda e, d_=d_: e.tensor_scalar(out=oml[d_], in0=lbt[d_], scalar1=-1.0, scalar2=1.0, op0=ALU.mult, op1=ALU.add), [t_ec], [t_ec])
        s.barrier()

        def load_norm(G, pool="all"):
            s.dma("sp", lambda e: e.dma_start(out=xin, in_=xs_v[:, :, G * 512:(G + 1) * 512]), [t_xin],
                  [self.xs_trk[kc][G] for kc in range(KC)], semkey="EXin")
            self.norm_tile(xin, t_xin, sl, h, t_h, nr, pool=pool)

        cv1 = Carve(self.arena)
        cv1.off = base
        wcv = cv1.bf(KC * 1024).rearrange("p (k c) -> p k c", k=KC)
        t_wcv = [Trk("wcv%d" % i) for i in range(2)]
        aT = cv1.bf(4 * (NT + 32)).rearrange("p (c t) -> p c t", c=4)
        t_aT = [Trk("aT%d" % g) for g in range(8)]
        t_apad = Trk("apad")
        diag = cv1.bf(4 * 31 * 128).rearrange("p (c j q) -> p c j q", c=4, j=31)
        t_diag = Trk("diag")
        yb = cv1.f32(4 * 512).rearrange("p (c t) -> p c t", c=4)
        t_yb = [Trk("yb%d" % c) for c in range(4)]
        ybf = cv1.ring(2, 512, "bf", "ybf")
        ysq = cv1.ring(2, 512, "bf", "ysq")
        sgr = cv1.ring(2, 512, "f32", "sgr")
        agr = cv1.ring(2, 512, "f32", "agr")
        mu, t_mu = cv1.f32(512), Trk("mu")
        rs, t_rs = cv1.f32(512), Trk("rs")
        zt = cv1.ring(2, 512, "f32", "zt")
        co = [cv1.bf(4 * 512).rearrange("p (c t) -> p c t", c=4) for i in range(2)]
        t_co = [Trk("co%d" % i) for i in range(2)]
        wv = A["ev_w_in"][0].rearrange("(kc p) c -> p kc c", p=128)
        for i in range(2):
            s.dma("pool", lambda e, i=i: e.dma_start(out=wcv[:, :, i * 512:(i + 1) * 512], in_=wv[:, :, i * 512:(i + 1) * 512]), [t_wcv[i]], [], semkey="EW%d" % i)
        s.op("pool", lambda e: e.memset(aT[:, :, 0:15], 0.0), [], [t_apad])
        s.op("pool", lambda e: e.memset(aT[:, :, 15 + NT:NT + 32], 0.0), [], [t_apad])
        for cc in range(4):
            for j in range(31):
                s.op("dve", lambda e, cc=cc, j=j: e.tensor_scalar(out=diag[:, cc, j, :], in0=ident_b[:, :], scalar1=cwT[:, cc, j:j + 1], scalar2=None, op0=ALU.mult),
                     [t_const, t_ec], [t_diag])
        cn = {"i": 0}

        def glu(G):
            load_norm(G, "P")
            yield
            for cc in range(4):
                ba, bat = self.bank("P")
                bg, bgt = self.bank("P")
                for kc in range(KC):
                    s.op("pe", lambda e, kc=kc, cc=cc, ba=ba: e.matmul(ba[:, :], wcv[:, kc, cc * 128:(cc + 1) * 128], h[:, kc, :], start=(kc == 0), stop=(kc == KC - 1)),
                         [t_wcv[0], t_h], [bat], signal=(kc == KC - 1))
                for kc in range(KC):
                    s.op("pe", lambda e, kc=kc, cc=cc, bg=bg: e.matmul(bg[:, :], wcv[:, kc, 512 + cc * 128:512 + (cc + 1) * 128], h[:, kc, :], start=(kc == 0), stop=(kc == KC - 1)),
                         [t_wcv[1], t_h], [bgt], signal=(kc == KC - 1))
                i = cn["i"] % 2
                cn["i"] += 1
                sg_, t_sg = sgr[i]
                ag_, t_ag = agr[i]
                s.op("act", lambda e, bg=bg, sg_=sg_: e.activation(out=sg_, in_=bg[:, :], func=AF.Sigmoid), [bgt], [t_sg])
                s.op("dve", lambda e, ba=ba, sg_=sg_, ag_=ag_: e.tensor_tensor(out=ag_, in0=ba[:, :], in1=sg_, op=ALU.mult), [bat, t_sg], [t_ag])
                dst = aT[:, cc, 15 + G * 512: 15 + (G + 1) * 512]
                s.op("pool", lambda e, ag_=ag_, dst=dst: e.tensor_tensor(out=dst, in0=ag_, in1=tmask, op=ALU.mult), [t_ag, t_ec], [t_aT[G]])
                yield

        def conv(G):
            ci = G % 2
            for cc in range(4):
                bk, bt = self.bank("S")
                rd = [t_aT[g] for g in (G - 1, G, G + 1) if 0 <= g < 8] + [t_apad, t_diag]
                for j in range(31):
                    src = aT[:, cc, G * 512 + j: G * 512 + j + 512]
                    s.op("pe", lambda e, cc=cc, j=j, src=src, bk=bk: e.matmul(bk[:, :], diag[:, cc, j, :], src, start=(j == 0), stop=(j == 30)),
                         rd, [bt], signal=(j == 30))
                s.op("act", lambda e, cc=cc, bk=bk: e.activation(out=yb[:, cc, :], in_=bk[:, :], func=AF.Identity, bias=cvec[:, 0, cc:cc + 1], scale=1.0),
                     [bt, t_ec], [t_yb[cc]])
                yield
            b1, b1t = self.bank("S")
            b2, b2t = self.bank("S")
            for cc in range(4):
                yf, t_yf = ybf[cc % 2]
                yq, t_yq = ysq[cc % 2]
                s.op("dve", lambda e, cc=cc, yf=yf: e.tensor_copy(out=yf, in_=yb[:, cc, :]), [t_yb[cc]], [t_yf])
                s.op("act", lambda e, cc=cc, yq=yq: e.activation(out=yq, in_=yb[:, cc, :], func=AF.Square), [t_yb[cc]], [t_yq])
                s.op("pe", lambda e, cc=cc, yf=yf, b1=b1: e.matmul(b1[:, :], ones_b[:, :], yf, start=(cc == 0), stop=(cc == 3)), [t_yf, t_const], [b1t])
                s.op("pe", lambda e, cc=cc, yq=yq, b2=b2: e.matmul(b2[:, :], ones_b[:, :], yq, start=(cc == 0), stop=(cc == 3)), [t_yq, t_const], [b2t])
            s.op("act", lambda e, b1=b1: e.activation(out=mu, in_=b1[:, :], func=AF.Copy, scale=1.0 / 512.0), [b1t], [t_mu])
            s.op("dve", lambda e, b1=b1: e.tensor_tensor(out=rs, in0=b1[:, :], in1=mu, op=ALU.mult), [b1t, t_mu], [t_rs])
            s.op("dve", lambda e, b2=b2: e.tensor_tensor(out=rs, in0=b2[:, :], in1=rs, op=ALU.subtract), [b2t, t_rs], [t_rs])
            s.op("act", lambda e: e.activation(out=rs, in_=rs, func=AF.Ln, bias=eps5[:, 0:1], scale=1.0), [t_rs, t_ec], [t_rs])
            s.op("act", lambda e: e.activation(out=rs, in_=rs, func=AF.Exp, scale=-0.5), [t_rs], [t_rs])
            yield
            for cc in range(4):
                z_, t_z = zt[cc % 2]
                s.op("dve", lambda e, cc=cc, z_=z_: e.tensor_tensor(out=z_, in0=yb[:, cc, :], in1=mu, op=ALU.subtract), [t_yb[cc], t_mu], [t_z])
                s.op("pool", lambda e, z_=z_: e.tensor_tensor(out=z_, in0=z_, in1=rs, op=ALU.mult), [t_z, t_rs], [t_z])
                s.op("act", lambda e, cc=cc, z_=z_, ci=ci: e.activation(out=co[ci][:, cc, :], in_=z_, func=AF.Silu, bias=cvec[:, 2, cc:cc + 1], scale=self.lng_s[:, cc:cc + 1]),
                     [t_z, t_ec], [t_co[ci]])
                yield
            s.dma("sp", lambda e, ci=ci: e.dma_start(out=cvs[:, :, G * 512:(G + 1) * 512], in_=co[ci]), [t_cvs[G]], [t_co[ci]], semkey="ECo%d" % ci)

        self.lng_s = cv1.f32(4)
        s.op("dve", lambda e: e.tensor_scalar(out=self.lng_s, in0=cvec[:, 1, :], scalar1=float(np.sqrt(512.0)), scalar2=None, op0=ALU.mult), [t_ec], [t_ec])
        if self.mode == "even_a":
            return
        for G in range(10):
            gg = (lambda G=G: glu(G)) if G < 8 else None
            gc = (lambda G=G: conv(G - 2)) if (G >= 2 and self.mode != "even_b") else None
            self.interleave([gg() if gg else None, gc() if gc else None])
        if self.mode in ("even_b", "even_c"):
            return

        s.barrier()
        cv2 = Carve(self.arena)
        cv2.off = base
        whg = cv2.bf(KC * 2048).rearrange("p (k c) -> p k c", k=KC)
        t_whg = [Trk("whg%d" % i) for i in range(4)]
        S_, t_S = cv2.f32(512).rearrange("p (h v) -> p h v", h=4), Trk("S")
        Sb, t_Sb = cv2.bf(512).rearrange("p (h v) -> p h v", h=4), Trk("Sb")
        er = cv2.ring(2, 512, "f32", "er")
        qh_t, t_qh = cv2.bf(512), Trk("qh_t")
        qt_t, t_qt = cv2.bf(512), Trk("qt_t")
        kt_t, t_kt = cv2.bf(512), Trk("kt_t")
        qtT, t_qtT = cv2.bf(512).rearrange("p (h t) -> p h t", h=4), Trk("qtT")
        ktT, t_ktT = cv2.bf(512).rearrange("p (h t) -> p h t", h=4), Trk("ktT")
        P2 = []
        for i in range(3):
            P2.append({
                "qs": (cv2.f32(512), Trk("qs%d" % i)),
                "gl": (cv2.f32(512), Trk("gl%d" % i)),
                "kk": (cv2.f32(512), Trk("kk%d" % i)),
                "ff": (cv2.f32(512), Trk("ff%d" % i)),
                "qhT": (cv2.bf(512).rearrange("p (h t) -> p h t", h=4), Trk("qhT%d" % i)),
                "scT": (cv2.bf(256), Trk("scT%d" % i)),
                "v": (cv2.bf(512), Trk("v%d" % i)),
                "kh": (cv2.bf(512), Trk("kh%d" % i)),
                "dec": (cv2.f32(8), Trk("dec%d" % i)),
                "gs": (cv2.f32(512), Trk("gs%d" % i)),
                "ob": (cv2.f32(512), Trk("ob%d" % i)),
                "ost": (cv2.f32(512), Trk("ost%d" % i)),
            })
        osum, t_osum = cv2.f32(512), Trk("osum")
        osq, t_osq = cv2.f32(512), Trk("osq")
        oss, t_oss = cv2.f32(4), Trk("oss")
        r_b, t_rb = cv2.bf(512), Trk("r_b")
        rT = cv2.bf(4 * 512).rearrange("p (c t) -> p c t", c=4)
        t_rT = [Trk("rT%d" % i) for i in range(4)]
        cvl, t_cvl = cv2.bf(4 * 512).rearrange("p (c t) -> p c t", c=4), Trk("cvl")
        xc = cv2.ring(2, 512, "f32", "exc")
        xo = cv2.ring(2, 512, "f32", "exo")
        cnx = {"i": 0}
        wo_v = A["ev_w_out"][0].rearrange("(c8 p) d -> p c8 d", p=128)

        def prepA(n, d_, P, fwd_final):
            G, p4 = n // 4, n % 4
            (qs, t_qs), (gl, t_gl), (kk, t_kk), (ff, t_ff) = P["qs"], P["gl"], P["kk"], P["ff"]
            if (d_ == 0 and p4 == 0) or (d_ == 1 and p4 == 3):
                load_norm(G, "PA")
                yield
            ncomp = 4 if fwd_final else 3
            vb, t_vb = P["v"]
            gs, t_gs = P["gs"]
            evac = [(AF.Silu, qs, t_qs), (AF.Sigmoid, ff, t_ff), (AF.Copy, vb, t_vb), (AF.Silu, gs, t_gs)]
            for c in range(ncomp):
                bk, bt = self.bank("PA")
                for kc in range(KC):
                    s.op("pe", lambda e, kc=kc, c=c, bk=bk: e.matmul(bk[:, :], h[:, kc, p4 * 128:(p4 + 1) * 128], whg[:, kc, c * 512:(c + 1) * 512],
                                                                     start=(kc == 0), stop=(kc == KC - 1)), [t_h, t_whg[c]], [bt], signal=(kc == KC - 1))
                fn_, dst_, t_dst_ = evac[c]
                s.op("act", lambda e, bk=bk, fn_=fn_, dst_=dst_: e.activation(out=dst_, in_=bk[:, :], func=fn_), [bt], [t_dst_])
                yield
            s.op("dve", lambda e: e.tensor_tensor(out=ff, in0=ff, in1=oml[d_], op=ALU.mult), [t_ff, t_ec], [t_ff])
            s.op("dve", lambda e: e.tensor_tensor(out=ff, in0=ff, in1=lbt[d_], op=ALU.add), [t_ff, t_ec], [t_ff])
            yield
            s.op("act", lambda e: e.activation(out=gl, in_=ff, func=AF.Ln), [t_ff], [t_gl])
            s.op("dve", lambda e: e.tensor_scalar(out=gl, in0=gl, scalar1=tokm[:, p4:p4 + 1], scalar2=None, op0=ALU.mult), [t_gl, t_ec], [t_gl])
            s.op("pool", lambda e: e.tensor_scalar(out=kk, in0=ff, scalar1=-1.0, scalar2=1.0, op0=ALU.mult, op1=ALU.add), [t_ff], [t_kk])
            yield

        def prepB(n, d_, P, fwd_final):
            G, p4 = n // 4, n % 4
            (qs, t_qs), (gl, t_gl), (kk, t_kk) = P["qs"], P["gl"], P["kk"]
            if self.mode == "even_e1":
                return
            bb = []
            for m in range(3):
                bk, bt = self.bank("PB")
                s.op("pe", lambda e, m=m, bk=bk: e.matmul(bk[:, :], cumM[:, d_, m, :], gl, start=True, stop=True), [t_gl, t_ec], [bt])
                bb.append((bk, bt))
                yield
            kh_, t_kh = P["kh"]
            specs = [(0, 1.0, qs, t_qs, qh_t, t_qh), (1, 1.0, qs, t_qs, qt_t, t_qt), (1, -1.0, kk, t_kk, kt_t, t_kt), (2, 1.0, kk, t_kk, kh_, t_kh)]
            for i, (m, sc_, src, t_src, dst, t_dst) in enumerate(specs):
                e_, t_e = er[i % 2]
                bk, bt = bb[m]
                s.op("act", lambda e, e_=e_, bk=bk, sc_=sc_: e.activation(out=e_, in_=bk[:, :], func=AF.Exp, scale=sc_), [bt], [t_e])
                eng = "dve" if i % 2 == 0 else "pool"
                s.op(eng, lambda e, e_=e_, src=src, dst=dst: e.tensor_tensor(out=dst, in0=src, in1=e_, op=ALU.mult), [t_e, t_src], [t_dst])
                yield
            be, bet = self.bank("PB")
            s.op("pe", lambda e, be=be: e.matmul(be[:, :], cumM[:, d_, 3, :], gl, start=True, stop=True), [t_gl, t_ec], [bet])
            ee, t_ee = er[0]
            s.op("act", lambda e, be=be, ee=ee: e.activation(out=ee, in_=be[:, :], func=AF.Exp), [bet], [t_ee])
            yield
            bd, bdt = self.bank("PB")
            for hh in range(4):
                s.op("pe", lambda e, hh=hh, bd=bd, ee=ee: e.transpose(out=bd[:, hh * 128:(hh + 1) * 128], in_=ee[:, hh * 128:(hh + 1) * 128], identity=self.ident_f[:]),
                     [t_ee, t_const], [bdt], signal=(hh == 3))
            dec, t_dec = P["dec"]
            s.op("dve", lambda e, bd=bd, dec=dec: e.tensor_copy(out=dec.rearrange("p (h c) -> p h c", h=4),
                                                               in_=bd[:, :].rearrange("p (h c t) -> p h c t", h=4, c=2)[:, :, :, 0]), [bdt], [t_dec])
            yield
            if self.mode in ("even_e2", "even_e2a"):
                return
            bA, bAt = self.bank("PB")
            bAb = bA[:, :].bitcast(BF16)
            bB, bBt = self.bank("PB")
            bBb = bB[:, :].bitcast(BF16)
            for hh in range(4):
                s.op("pe", lambda e, hh=hh: e.transpose(out=bAb[:, hh * 128:(hh + 1) * 128], in_=qh_t[:, hh * 128:(hh + 1) * 128], identity=ident_b[:]),
                     [t_qh, t_const], [bAt], signal=False)
            for hh in range(4):
                s.op("pe", lambda e, hh=hh: e.transpose(out=bAb[:, 512 + hh * 128:512 + (hh + 1) * 128], in_=qt_t[:, hh * 128:(hh + 1) * 128], identity=ident_b[:]),
                     [t_qt, t_const], [bAt], signal=(hh == 3))
            for hh in range(4):
                s.op("pe", lambda e, hh=hh: e.transpose(out=bBb[:, hh * 128:(hh + 1) * 128], in_=kt_t[:, hh * 128:(hh + 1) * 128], identity=ident_b[:]),
                     [t_kt, t_const], [bBt], signal=(hh == 3))
            qhT, t_qhT = P["qhT"]
            yield
            s.op("act", lambda e: e.activation(out=qhT.rearrange("p h t -> p (h t)"), in_=bAb[:, 0:512], func=AF.Copy), [bAt], [t_qhT])
            s.op("dve", lambda e: e.tensor_copy(out=qtT.rearrange("p h t -> p (h t)"), in_=bAb[:, 512:1024]), [bAt], [t_qtT])
            s.op("act", lambda e: e.activation(out=ktT.rearrange("p h t -> p (h t)"), in_=bBb[:, 0:512], func=AF.Copy), [bBt], [t_ktT])
            yield
            if self.mode == "even_e3":
                return
            bs, bst = self.bank("PB")
            for c in range(2):
                for hh in range(4):
                    last = (c == 1 and hh == 3)
                    s.op("pe", lambda e, c=c, hh=hh: e.matmul(bs[c * 64:(c + 1) * 64, hh * 64:(hh + 1) * 64], ktT[:, hh, c * 64:(c + 1) * 64],
                                                              qtT[:, hh, c * 64:(c + 1) * 64], start=True, stop=True),
                         [t_ktT, t_qtT], [bst], signal=last)
            scT, t_scT = P["scT"]
            s.op("dve", lambda e: e.tensor_tensor(out=scT.rearrange("p (h t) -> p h t", h=4), in0=bs[:, 0:256].rearrange("p (h t) -> p h t", h=4),
                                                  in1=scm[:, d_, :].unsqueeze(1).to_broadcast([128, 4, 64]), op=ALU.mult), [bst, t_ec], [t_scT])

        def seq(n, d_, P, fwd_final):
            G, p4 = n // 4, n % 4
            qhT, t_qhT = P["qhT"]
            scT, t_scT = P["scT"]
            vb, t_vb = P["v"]
            kh_, t_kh = P["kh"]
            dec, t_dec = P["dec"]
            bo, bot = self.bank("So")
            chunks = (0, 1) if d_ == 0 else (1, 0)
            for c in chunks:
                cg = n * 2 + c
                if (d_ == 0 and cg % 8 == 0) or (d_ == 1 and cg % 8 == 3):
                    s.op("dve", lambda e: e.tensor_scalar(out=S_.rearrange("p h v -> p (h v)"), in0=S_.rearrange("p h v -> p (h v)"), scalar1=cflag[:, 0:1],
                                                          scalar2=None, op0=ALU.mult), [t_S, t_ec], [t_S])
                    s.op("act", lambda e: e.activation(out=Sb.rearrange("p h v -> p (h v)"), in_=S_.rearrange("p h v -> p (h v)"), func=AF.Copy), [t_S], [t_Sb])
                for hh in range(4):
                    ov = bo[c * 64:(c + 1) * 64, hh * 128:(hh + 1) * 128]
                    s.op("pe", lambda e, c=c, hh=hh, ov=ov: e.matmul(ov, qhT[:, hh, c * 64:(c + 1) * 64], Sb[:, hh, :], start=True, stop=False),
                         [t_qhT, t_Sb], [bot], signal=False)
                    last = (hh == 3 and c == chunks[1])
                    s.op("pe", lambda e, c=c, hh=hh, ov=ov: e.matmul(ov, scT[c * 64:(c + 1) * 64, hh * 64:(hh + 1) * 64], vb[c * 64:(c + 1) * 64, hh * 128:(hh + 1) * 128],
                                                                    start=False, stop=True), [t_scT, t_vb], [bot], signal=(hh == 3))
                yield
                bkv, bkvt = self.bank("Sk")
                for hh in range(4):
                    s.op("pe", lambda e, c=c, hh=hh, bkv=bkv: e.matmul(bkv[:, hh * 128:(hh + 1) * 128], kh_[c * 64:(c + 1) * 64, hh * 128:(hh + 1) * 128],
                                                                      vb[c * 64:(c + 1) * 64, hh * 128:(hh + 1) * 128], start=True, stop=True),
                         [t_kh, t_vb], [bkvt], signal=(hh == 3))
                yield
                s.op("dve", lambda e, c=c: e.tensor_tensor(out=S_, in0=S_, in1=dec[:, c:8:2].unsqueeze(2).to_broadcast([128, 4, 128]), op=ALU.mult),
                     [t_S, t_dec], [t_S])
                s.op("dve", lambda e, bkv=bkv: e.tensor_tensor(out=S_.rearrange("p h v -> p (h v)"), in0=bkv[:, :], in1=S_.rearrange("p h v -> p (h v)"), op=ALU.add),
                     [t_S, bkvt], [t_S])
                s.op("act", lambda e: e.activation(out=Sb.rearrange("p h v -> p (h v)"), in_=S_.rearrange("p h v -> p (h v)"), func=AF.Copy), [t_S], [t_Sb])
                if (d_ == 0 and cg % 8 == 3) or (d_ == 1 and cg % 8 == 0):
                    slot = cg // 8
                    s.dma("sp", lambda e, slot=slot: e.dma_start(out=A["hs_out"][slot, d_].rearrange("h k v -> k h v"), in_=S_), [self.t_hs], [t_S], semkey="EHS")
            if not fwd_final:
                ost, t_ost = P["ost"]
                s.op("act", lambda e: e.activation(out=ost, in_=bo[:, :], func=AF.Copy), [bot], [t_ost])
                s.dma("sp", lambda e: e.dma_start(out=obs[n], in_=ost), [t_obs[n]], [t_ost], semkey="EOst%d" % (n % 3))
                yield
                return
            ob, t_ob = P["ob"]
            gs, t_gs = P["gs"]
            s.op("dve", lambda e: e.tensor_tensor(out=osum, in0=bo[:, :], in1=ob, op=ALU.add), [bot, t_ob], [t_osum])
            yield
            s.op("act", lambda e: e.activation(out=osq, in_=osum, func=AF.Square), [t_osum], [t_osq])
            s.op("dve", lambda e: e.tensor_reduce(out=oss, in_=osq.rearrange("p (h v) -> p h v", h=4), axis=AX.X, op=ALU.add), [t_osq], [t_oss])
            s.op("act", lambda e: e.activation(out=oss, in_=oss, func=AF.Ln, bias=epsq[:, 0:1], scale=1.0 / 128.0), [t_oss, t_ec], [t_oss])
            s.op("act", lambda e: e.activation(out=oss, in_=oss, func=AF.Exp, scale=-0.5), [t_oss], [t_oss])
            yield
            s.op("dve", lambda e: e.tensor_tensor(out=osum.rearrange("p (h v) -> p h v", h=4), in0=osum.rearrange("p (h v) -> p h v", h=4),
                                                  in1=oss.unsqueeze(2).to_broadcast([128, 4, 128]), op=ALU.mult), [t_osum, t_oss], [t_osum])
            s.op("pool", lambda e: e.tensor_tensor(out=osum.rearrange("p (h v) -> p h v", h=4), in0=osum.rearrange("p (h v) -> p h v", h=4),
                                                   in1=hgn.unsqueeze(1).to_broadcast([128, 4, 128]), op=ALU.mult), [t_osum, t_ec], [t_osum])
            s.op("dve", lambda e: e.tensor_tensor(out=r_b, in0=osum, in1=gs, op=ALU.mult), [t_osum, t_gs], [t_rb])
            yield
            bT, bTt = self.bank("Sk")
            bTb = bT[:, :].bitcast(BF16)
            for hh in range(4):
                s.op("pe", lambda e, hh=hh: e.transpose(out=bTb[:, hh * 128:(hh + 1) * 128], in_=r_b[:, hh * 128:(hh + 1) * 128], identity=ident_b[:]),
                     [t_rb, t_const], [bTt], signal=(hh == 3))
            s.op("act", lambda e: e.activation(out=rT[:, :, p4 * 128:(p4 + 1) * 128], in_=bTb[:, 0:512].rearrange("p (c t) -> p c t", c=4), func=AF.Copy),
                 [bTt], [t_rT[p4]])

        def outproj_steps(G):
            def ldc(a, b):
                s.dma("sp", lambda e: e.dma_start(out=cvl, in_=cvs[:, :, G * 512:(G + 1) * 512]), [t_cvl], [t_cvs[G]], semkey="ECl")
            self.step(None, ldc)
            for dc in range(KC):
                def wl(e, st, dc=dc):
                    return e.dma_start(out=st[:, 0:1024].rearrange("p (c8 c) -> p c8 c", c8=8), in_=wo_v[:, :, dc * 128:(dc + 1) * 128])

                def body(st, strk, dc=dc):
                    wv_ = st[:, 0:1024].rearrange("p (c8 c) -> p c8 c", c8=8)
                    ci = cnx["i"] % 2
                    cnx["i"] += 1
                    xct, t_xc = xc[ci]
                    xot, t_xo = xo[ci]
                    s.dma("sp", lambda e: e.dma_start(out=xct, in_=xs_v[:, dc, G * 512:(G + 1) * 512]), [t_xc], [self.xs_trk[dc][G]], semkey="EXc%d" % ci)
                    bk, bt = self.bank()
                    for c8 in range(8):
                        rhs = cvl[:, c8, :] if c8 < 4 else rT[:, c8 - 4, :]
                        s.op("pe", lambda e, c8=c8, bk=bk, rhs=rhs: e.matmul(bk[:, :], wv_[:, c8, :], rhs, start=(c8 == 0), stop=(c8 == 7)),
                             list(strk) + list(t_rT) + [t_cvl], [bt], signal=(c8 == 7))
                    s.op("dve", lambda e, bk=bk: e.scalar_tensor_tensor(out=xot, in0=bk[:, :], scalar=self.modG[:, sl, dc:dc + 1], in1=xct,
                                                                        op0=ALU.mult, op1=ALU.add), [bt, t_xc, self.t_modAG], [t_xo])
                    s.dma("sp", lambda e: e.dma_start(out=xs_v[:, dc, G * 512:(G + 1) * 512], in_=xot), [self.xs_trk[dc][G]], [t_xo], semkey="EXo%d" % ci)
                self.step([wl], body)

        for d_ in (1, 0):
            fwd_final = (d_ == 0)
            if fwd_final and (self.mode == "even_d" or self.mode.startswith("even_e")):
                return
            s.barrier()
            cols = [1024, 1536 + 512 * d_, 2560, 3072]
            for c in range(4 if fwd_final else 3):
                s.dma("pool", lambda e, c=c, cols=cols: e.dma_start(out=whg[:, :, c * 512:(c + 1) * 512], in_=wv[:, :, cols[c]:cols[c] + 512]), [t_whg[c]], [], semkey="EWh%d" % c)
            s.dma("sp", lambda e, d_=d_: e.dma_start(out=S_, in_=A["s0"][d_].rearrange("h k v -> k h v")), [t_S], [], semkey="ES0")
            s.op("act", lambda e: e.activation(out=Sb.rearrange("p h v -> p (h v)"), in_=S_.rearrange("p h v -> p (h v)"), func=AF.Copy), [t_S], [t_Sb])
            order = list(range(NTILE)) if d_ == 0 else list(range(NTILE - 1, -1, -1))
            nord = len(order)
            for i in range(nord + 2):
                n = order[i] if i < nord else None
                if n is not None and fwd_final:
                    ob, t_ob = P2[i % 3]["ob"]
                    self.step(None, lambda a, b, n=n, ob=ob, t_ob=t_ob, i=i: s.dma(
                        "sp", lambda e: e.dma_start(out=ob, in_=obs[n]), [t_ob], [t_obs[n]], semkey="EOb%d" % (i % 3)))
                ga = (lambda n=n, P=P2[i % 3]: prepA(n, d_, P, fwd_final)) if n is not None else None
                gb = (lambda n=order[i - 1], P=P2[(i - 1) % 3]: prepB(n, d_, P, fwd_final)) if 1 <= i <= nord else None
                gs_ = None
                if i >= 2 and not self.mode.startswith("even_e"):
                    pn = order[i - 2]
                    gs_ = (lambda pn=pn, Pp=P2[(i - 2) % 3]: seq(pn, d_, Pp, fwd_final))
                self.step(None, lambda a, b, ga=ga, gb=gb, gs_=gs_: self.interleave([g() if g else None for g in (ga, gb, gs_)]))
                if i >= 2 and not self.mode.startswith("even_e"):
                    if fwd_final and pn % 4 == 3:
                        outproj_steps(pn // 4)
            self.run_steps()

    def phase_ffn(self, l, j, ffn_w_in, ffn_w_out, xs_v, ones_b, t_const):
        nc, s = self.nc, self.s
        sl = 0 if j == 0 else 2
        TS = 1024
        NTT = TS // 512
        s.barrier()
        cv = Carve(self.arena)
        hT = cv.bf(KC * TS).rearrange("p (k t) -> p k t", k=KC)
        hT_trk = [Trk("hT%d" % i) for i in range(NTT)]
        hid = cv.bf(FC * TS).rearrange("p (f t) -> p f t", f=FC)
        hid_trk = [[Trk("hid%d_%d" % (f, i)) for i in range(NTT)] for f in range(FC)]
        xin = [cv.f32(KC * 512).rearrange("p (k t) -> p k t", k=KC) for i in range(2)]
        xin_trk = [Trk("xin%d" % i) for i in range(2)]
        nr = {"sq": cv.ring(2, 512, "bf", "sq"), "tmp": cv.ring(2, 512, "f32", "tmp"), "rstd": (cv.f32(512), Trk("rstd"))}
        sg = [cv.bf(512) for i in range(3)]
        sg_trk = [Trk("sg%d" % i) for i in range(3)]
        xc = [cv.f32(512) for i in range(3)]
        xc_trk = [Trk("xc%d" % i) for i in range(3)]
        xo = [cv.f32(512) for i in range(3)]
        xo_trk = [Trk("xo%d" % i) for i in range(3)]
        wi = ffn_w_in[l, j].rearrange("(kc p) c -> p kc c", p=128)
        wo = ffn_w_out[l, j].rearrange("(fc p) d -> p fc d", p=128)
        cnt = {"sg": 0, "xc": 0}

        NST = NT // TS
        seqN, seqA, seqB = [], [[] for _ in range(NST)], [[] for _ in range(NST)]
        for st_i in range(NT // TS):
            t0 = st_i * TS

            def norm_body(_a, _b, st_i=st_i, t0=t0):
                def load(tt):
                    g = (t0 // 512 + tt)
                    i = g % 2
                    s.dma("sp", lambda e: e.dma_start(out=xin[i], in_=xs_v[:, :, g * 512:(g + 1) * 512]), [xin_trk[i]],
                          [self.xs_trk[kc][g] for kc in range(KC)], semkey="FXin%d" % i)
                load(0)
                for tt in range(NTT):
                    if tt + 1 < NTT:
                        load(tt + 1)
                    g = (t0 // 512 + tt)
                    i = g % 2
                    self.norm_tile(xin[i], xin_trk[i], sl, hT[:, :, tt * 512:(tt + 1) * 512], hT_trk[tt], nr)
            seqN.append((None, norm_body))

            for jb in range(11):
                def wl_g(e, st, jb=jb):
                    return e.dma_start(out=st[:, 0:2048].rearrange("p (k c) -> p k c", k=KC), in_=wi[:, :, jb * 256:(jb + 1) * 256])

                def wl_u(e, st, jb=jb):
                    return e.dma_start(out=st[:, 2048:4096].rearrange("p (k c) -> p k c", k=KC),
                                       in_=wi[:, :, DFF + jb * 256: DFF + (jb + 1) * 256])

                def bodyA(st, strk, jb=jb):
                    wg = st[:, 0:2048].rearrange("p (k c) -> p k c", k=KC)
                    wu = st[:, 2048:4096].rearrange("p (k c) -> p k c", k=KC)
                    for fs in range(2):
                        f = jb * 2 + fs
                        for tt in range(NTT):
                            bg, bgt = self.bank()
                            bu, but = self.bank()
                            for kc in range(KC):
                                s.op("pe", lambda e, kc=kc, fs=fs, tt=tt, bg=bg: e.matmul(
                                    bg[:, :], wg[:, kc, fs * 128:(fs + 1) * 128], hT[:, kc, tt * 512:(tt + 1) * 512],
                                    start=(kc == 0), stop=(kc == KC - 1)), list(strk) + [hT_trk[tt]], [bgt], signal=(kc == KC - 1))
                            for kc in range(KC):
                                s.op("pe", lambda e, kc=kc, fs=fs, tt=tt, bu=bu: e.matmul(
                                    bu[:, :], wu[:, kc, fs * 128:(fs + 1) * 128], hT[:, kc, tt * 512:(tt + 1) * 512],
                                    start=(kc == 0), stop=(kc == KC - 1)), list(strk) + [hT_trk[tt]], [but], signal=(kc == KC - 1))
                            si = cnt["sg"] % 3
                            cnt["sg"] += 1
                            s.op("act", lambda e, si=si, bg=bg: e.activation(out=sg[si], in_=bg[:, :], func=AF.Silu), [bgt], [sg_trk[si]])
                            s.op("dve", lambda e, si=si, bu=bu, f=f, tt=tt: e.tensor_tensor(
                                out=hid[:, f, tt * 512:(tt + 1) * 512], in0=bu[:, :], in1=sg[si], op=ALU.mult),
                                [but, sg_trk[si]], [hid_trk[f][tt]])
                seqA[st_i].append(([wl_g, wl_u], bodyA))

            for dc in range(KC):
                def wl_o(e, st, dc=dc):
                    return e.dma_start(out=st[:, 0:FC * 128].rearrange("p (f c) -> p f c", f=FC), in_=wo[:, :, dc * 128:(dc + 1) * 128])

                def bodyB(st, strk, dc=dc, t0=t0):
                    wv = st[:, 0:FC * 128].rearrange("p (f c) -> p f c", f=FC)
                    for tt in range(NTT):
                        g = t0 // 512 + tt
                        ci = cnt["xc"] % 3
                        cnt["xc"] += 1
                        s.dma("sp", lambda e, ci=ci, g=g: e.dma_start(out=xc[ci], in_=xs_v[:, dc, g * 512:(g + 1) * 512]),
                              [xc_trk[ci]], [self.xs_trk[dc][g]], semkey="FXc%d" % ci)
                        bk, bt = self.bank()
                        for f in range(FC):
                            s.op("pe", lambda e, f=f, tt=tt, bk=bk: e.matmul(
                                bk[:, :], wv[:, f, :], hid[:, f, tt * 512:(tt + 1) * 512], start=(f == 0), stop=(f == FC - 1)),
                                list(strk) + [hid_trk[f][tt]], [bt], signal=(f == FC - 1))
                        s.op("dve", lambda e, ci=ci, bk=bk: e.scalar_tensor_tensor(
                            out=xo[ci], in0=bk[:, :], scalar=self.modG[:, sl, dc:dc + 1], in1=xc[ci], op0=ALU.mult, op1=ALU.add),
                            [bt, xc_trk[ci], self.t_modAG], [xo_trk[ci]])
                        s.dma("sp", lambda e, ci=ci, g=g: e.dma_start(out=xs_v[:, dc, g * 512:(g + 1) * 512], in_=xo[ci]),
                              [self.xs_trk[dc][g]], [xo_trk[ci]], semkey="FXo%d" % ci)
                seqB[st_i].append(([wl_o], bodyB))
        order = [seqN[0]] + seqA[0]
        for st_i in range(NST):
            if st_i + 1 < NST:
                order.append(seqN[st_i + 1])
            order += seqB[st_i]
            if st_i + 1 < NST:
                order += seqA[st_i + 1]
        self.steps.extend(order)
        self.run_steps()


_CACHE = {}


def _get_prog(mode):
    if mode not in _CACHE:
        p = Prog(mode)
        p.build()
        _CACHE[mode] = p
    return _CACHE[mode]


NEG = -30000.0


def _attn_masks():
    key = np.arange(128)[:, None]
    q = np.arange(128)[None, :]
    vis = np.zeros((128, 128), np.float32)
    hid = np.full((128, 128), NEG, np.float32)
    prev_band = np.where(key >= q, 0.0, NEG).astype(np.float32)
    next_band = np.where(key <= q, 0.0, NEG).astype(np.float32)
    am_s = np.zeros((128, 6, 2, 128), np.float32)
    am_p = np.zeros((128, 6, 2, 128), np.float32)
    for v in range(6):
        am_s[:, v, 0] = hid if v == 0 else prev_band
        am_s[:, v, 1] = hid if v == 5 else next_band
        am_p[:, v, 0] = vis if v == 2 else hid
        am_p[:, v, 1] = vis if v in (0, 1) else hid
    return np.ascontiguousarray(am_s.reshape(128, 1536)), np.ascontiguousarray(am_p.reshape(128, 1536))


def _scan_consts():
    sidx = np.arange(128)[:, None]
    tidx = np.arange(128)[None, :]
    same = (sidx // 64) == (tidx // 64)
    ls = sidx % 64
    cm = np.zeros((128, 2, 4, 128), np.float32)
    cm[:, 0, 3] = same
    cm[:, 1, 3] = same
    sm = np.zeros((128, 2, 64), np.float32)
    mb = same & (sidx <= tidx)
    sel = same & (ls <= 31)
    cm[:, 0, 0] = mb
    cm[:, 0, 1] = mb.astype(np.float32) - sel.astype(np.float32)
    cm[:, 0, 2] = same & (sidx > tidx)
    mb = same & (sidx >= tidx)
    sel = same & (ls >= 32)
    cm[:, 1, 0] = mb
    cm[:, 1, 1] = mb.astype(np.float32) - sel.astype(np.float32)
    cm[:, 1, 2] = same & (sidx < tidx)
    s_loc = (np.arange(128) % 64)[:, None]
    t_loc = np.arange(64)[None, :]
    sm[:, 0] = (s_loc <= t_loc)
    sm[:, 1] = (s_loc >= t_loc)
    return np.ascontiguousarray(cm.reshape(128, 1024)), np.ascontiguousarray(sm.reshape(128, 128))


def _rope_tables():
    t = np.arange(NT)
    row = (t // 64).astype(np.float32)
    col = (t % 64).astype(np.float32)
    inv = (10000.0 ** (-np.arange(0, 32, 2, dtype=np.float32) / 32.0)).astype(np.float32)
    ar = (row[:, None] * inv[None, :]).astype(np.float32)
    ac = (col[:, None] * inv[None, :]).astype(np.float32)
    C = np.concatenate([np.cos(ar), np.cos(ar), np.cos(ac), np.cos(ac)], axis=1).astype(np.float32)
    S = np.concatenate([-np.sin(ar), np.sin(ar), -np.sin(ac), np.sin(ac)], axis=1).astype(np.float32)
    return np.ascontiguousarray(C), np.ascontiguousarray(S)


def make_in_maps(inputs, mode="full"):
    f32 = np.float32
    xp = np.asarray(inputs["x_prompt"], f32)
    xsm = np.asarray(inputs["x_sample"], f32)
    c = np.asarray(inputs["c"], f32)
    c_ctx = np.asarray(inputs["c_ctx"], f32)
    ident = np.eye(128, dtype=f32)
    b_mod = np.asarray(inputs["b_mod"], f32)
    bmodT = np.ascontiguousarray(b_mod.reshape(2, 72, 128).transpose(0, 2, 1))
    norm_g = np.asarray(inputs["norm_g"], f32)
    normgT = np.ascontiguousarray(norm_g.reshape(2, 3, KC, 128).transpose(0, 1, 3, 2))
    shared = {
        "ident": ident,
        "w_mod": np.asarray(inputs["w_mod"], f32),
        "bmodT": bmodT,
        "normgT": normgT,
    }
    if mode in ("full", "ffn1", "l0"):
        shared["ffn_w_in"] = np.asarray(inputs["ffn_w_in"], f32)
        shared["ffn_w_out"] = np.asarray(inputs["ffn_w_out"], f32)
    if mode in ("full", "attn"):
        shared["od_w_in"] = np.asarray(inputs["od_w_in"], f32)
        shared["od_w_out"] = np.asarray(inputs["od_w_out"], f32)
        shared["qkgain"] = np.ascontiguousarray(np.broadcast_to(
            np.concatenate([np.asarray(inputs["q_norm_g"], f32)[0], np.asarray(inputs["k_norm_g"], f32)[0]])[None, :], (128, 128)))
        shared["sinks"] = np.asarray(inputs["sinks"], f32).reshape(1, 16)
        am_s, am_p = _attn_masks()
        rc, rs = _rope_tables()
    if mode in ("full", "l0") or mode.startswith("even"):
        shared["ev_w_in"] = np.asarray(inputs["ev_w_in"], f32)
        shared["ev_w_out"] = np.asarray(inputs["ev_w_out"], f32)
        cw = np.asarray(inputs["conv_w"], f32)[0]
        shared["convwT"] = np.ascontiguousarray(cw.reshape(31, 4, 128).transpose(2, 1, 0).reshape(128, 124))
        vecs = np.stack([np.asarray(inputs[k], f32)[0] for k in ("conv_b", "conv_ln_g", "conv_ln_b")])
        shared["convvec"] = np.ascontiguousarray(vecs.reshape(3, 4, 128).transpose(2, 0, 1).reshape(128, 12))
        shared["lbraw"] = np.ascontiguousarray(np.asarray(inputs["hg_lb_raw"], f32).reshape(2, 1536))
        shared["hgn"] = np.ascontiguousarray(np.broadcast_to(np.asarray(inputs["hg_norm_g"], f32)[0][None, :], (128, 128)))
        cm, sm = _scan_consts()
        shared["cumM"] = cm
        shared["scmask"] = sm
    maps = []
    for core in range(NCORES):
        m = dict(shared)
        if mode in ("full", "l0") or mode.startswith("even"):
            if core < 4:
                m["tmask"] = np.ones((128, 512), f32)
                m["cflag"] = np.ones((128, 1), f32)
                m["tokm"] = np.ones((128, 4), f32)
                m["s0"] = np.ascontiguousarray(np.asarray(inputs["state_hgrn"], f32)[core, 0])
            else:
                tm = np.zeros((128, 512), f32)
                tm[:, :256] = 1.0
                m["tmask"] = tm
                m["cflag"] = np.zeros((128, 1), f32)
                tk = np.zeros((128, 4), f32)
                tk[:, :2] = 1.0
                m["tokm"] = tk
                m["s0"] = np.zeros((2, 4, 128, 128), f32)
        if mode in ("full", "attn"):
            if core < 4:
                m["amask"] = am_s
                m["cval"] = np.ones((128, 64), f32)
                m["ctx_k"] = np.ascontiguousarray(np.asarray(inputs["cache_k"], f32)[core, 0].reshape(256, 256))
                m["ctx_v"] = np.ascontiguousarray(np.asarray(inputs["cache_v"], f32)[core, 0].reshape(256, 256))
                m["ropeC"], m["ropeS"] = rc, rs
            else:
                m["amask"] = am_p
                m["cval"] = np.zeros((128, 64), f32)
                m["ctx_k"] = np.zeros((256, 256), f32)
                m["ctx_v"] = np.zeros((256, 256), f32)
                m["ropeC"] = np.ones((NT, 64), f32)
                m["ropeS"] = np.zeros((NT, 64), f32)
        if core < 4:
            m["x_in"] = np.ascontiguousarray(xsm[core])
            cv = c[core]
        else:
            xi = np.zeros((NT, D), f32)
            for sl in range(8):
                xi[sl * 512: sl * 512 + 256] = xp[(core - 4) * 8 + sl]
            m["x_in"] = xi
            cv = c_ctx
        m["cond"] = np.ascontiguousarray(cv.reshape(KC, 128).T)
        maps.append(m)
    return maps


def run(inputs, mode="full", trace=False):
    p = _get_prog(mode)
    maps = make_in_maps(inputs, mode)
    res = run_bass_kernel_spmd(p.nc, maps, core_ids=list(range(NCORES)), trace=trace)
    return res


def kernel(**inputs):
    res = run(inputs, "full")
    r = res.results
    y_sample = np.stack([np.asarray(r[i]["y"], np.float32) for i in range(4)], axis=0)
    y_prompt = np.zeros((32, 256, D), np.float32)
    hs = np.zeros((32, 1, 2, 4, 128, 128), np.float32)
    nk = np.zeros((32, 1, 256, 4, 64), np.float32)
    nv = np.zeros((32, 1, 256, 4, 64), np.float32)
    for core in range(4, 8):
        y = np.asarray(r[core]["y"], np.float32)
        for sl in range(8):
            b = (core - 4) * 8 + sl
            y_prompt[b] = y[sl * 512: sl * 512 + 256]
            hs[b, 0] = r[core]["hs_out"][sl]
            nk[b, 0] = np.asarray(r[core]["nk"][sl]).reshape(256, 4, 64)
            nv[b, 0] = np.asarray(r[core]["nv"][sl]).reshape(256, 4, 64)
    return (y_prompt, y_sample, hs, nk, nv)
```

```python
import numpy as np
import concourse.bass as bass
import concourse.mybir as mybir
from concourse.bass_utils import run_bass_kernel_spmd

F32 = mybir.dt.float32
BF16 = mybir.dt.bfloat16
AF = mybir.ActivationFunctionType
ALU = mybir.AluOpType
AX = mybir.AxisListType

NCORES = 8
D = 1024
KC = 8
DFF = 2816
FC = 22
NT = 4096
NTILE = NT // 128
EPS = 1e-6


class Trk:
    __slots__ = ("name", "w", "r", "excl")

    def __init__(self, name="", excl=False):
        self.name = name
        self.w = None
        self.r = []
        self.excl = excl


class Sched:
    ENGS = ("pe", "act", "dve", "pool", "sp")

    def __init__(self, nc):
        self.nc = nc
        self.ops = {e: [] for e in self.ENGS}
        self.cnt = {}
        self.waited = {e: {} for e in self.ENGS}
        self.sems = {}
        self.pend = {e: ([], []) for e in self.ENGS}
        self._semctx = []
        for e in self.ENGS:
            self._mksem("E_" + e)
        self.nops = 0

    def _mksem(self, key):
        ctx = self.nc.semaphore(key)
        h = ctx.__enter__()
        self._semctx.append(ctx)
        self.sems[key] = h
        self.cnt[key] = 0
        return h

    def _deps(self, eng, reads, writes):
        need = {}

        def add(ev):
            if ev is None:
                return
            k, v = ev
            if need.get(k, 0) < v:
                need[k] = v
        for t in reads:
            add(t.w)
        for t in writes:
            add(t.w)
            for ev in t.r:
                add(ev)
        for e2 in self.ENGS:
            if e2 == eng:
                continue
            p = self.pend[e2]
            if p[0] or p[1]:
                ids = set(id(t) for t in p[1])
                idr = set(id(t) for t in p[0])
                for t in reads:
                    assert id(t) not in ids, ("pending unsignaled writer", e2, t.name)
                for t in writes:
                    assert id(t) not in ids and id(t) not in idr, ("pending unsignaled access", e2, t.name)
        out = []
        wd = self.waited[eng]
        for k, v in need.items():
            if wd.get(k, 0) >= v:
                continue
            wd[k] = v
            out.append((k, v))
        return out

    def op(self, eng, fn, reads=(), writes=(), signal=True):
        ex = [t for t in reads if t.excl]
        if ex:
            writes = list(writes) + ex
            reads = [t for t in reads if not t.excl]
        waits = self._deps(eng, reads, writes)
        pr, pw = self.pend[eng]
        pr.extend(reads)
        pw.extend(writes)
        key = "E_" + eng
        if signal:
            self.cnt[key] += 1
            ev = (key, self.cnt[key])
            for t in pw:
                t.w = ev
                t.r = []
            for t in pr:
                if t.w is not ev:
                    t.r.append(ev)
            self.pend[eng] = ([], [])
        self.ops[eng].append((waits, fn, (key, 1) if signal else None))
        self.nops += 1

    def dma(self, eng, fn, dsts, srcs=(), semkey=None):
        if semkey not in self.sems:
            self._mksem(semkey)
        waits = self._deps(eng, list(srcs), list(dsts))
        self.cnt[semkey] += 16
        ev = (semkey, self.cnt[semkey])
        for t in dsts:
            t.w = ev
            t.r = []
        for t in srcs:
            t.r.append(ev)
        self.ops[eng].append((waits, fn, (semkey, 16)))
        self.nops += 1

    def barrier(self):
        for e in self.ENGS:
            assert not self.pend[e][0] and not self.pend[e][1]
        for e in self.ENGS:
            wd = self.waited[e]
            waits = []
            for k, v in self.cnt.items():
                if v > 0 and wd.get(k, 0) < v:
                    wd[k] = v
                    waits.append((k, v))
            self.ops[e].append((waits, None, None))

    def emit(self):
        nc = self.nc
        sems = self.sems
        engobj = {"pe": "tensor", "act": "scalar", "dve": "vector", "pool": "gpsimd", "sp": "sync"}
        with nc.Block() as block:
            for e in self.ENGS:
                lst = self.ops[e]

                def body(eng, lst=lst):
                    for waits, fn, inc in lst:
                        for k, v in waits:
                            eng.wait_ge(sems[k], v)
                        if fn is not None:
                            ins = fn(eng)
                            if inc is not None:
                                ins.then_inc(sems[inc[0]], inc[1])
                getattr(block, engobj[e])(body)


class Carve:
    def __init__(self, arena):
        self.ar = arena
        self.off = 0

    def f32(self, n):
        a = self.ar[:, self.off:self.off + n]
        self.off += n
        assert self.off <= self.ar.shape[1], ("arena overflow", self.off)
        return a

    def bf(self, n):
        assert n % 2 == 0
        a = self.ar[:, self.off:self.off + n // 2].bitcast(BF16)
        self.off += n // 2
        assert self.off <= self.ar.shape[1], ("arena overflow", self.off)
        return a

    def ring(self, n, width, kind, name):
        return [((self.f32(width) if kind == "f32" else self.bf(width)), Trk("%s%d" % (name, i))) for i in range(n)]


class Ring:
    def __init__(self, nc, name, n, shape, dtype):
        self.n = n
        self.name = name
        self.t = [nc.alloc_sbuf_tensor("%s%d" % (name, i), shape, dtype) for i in range(n)]
        self.trk = [Trk("%s%d" % (name, i)) for i in range(n)]
        self.i = 0

    def next(self):
        i = self.i % self.n
        self.i += 1
        return i, self.t[i], self.trk[i]


class Prog:
    def __init__(self, mode="full"):
        self.mode = mode
        nc = bass.Bass("TRN2", target_bir_lowering=False)
        self.nc = nc
        self.s = Sched(nc)
        self.inp = {}
        self.steps = []

    def din(self, name, shape, dt=F32):
        ap = self.nc.dram_tensor(name, list(shape), dt, kind="ExternalInput").ap()
        self.inp[name] = ap
        return ap

    def dout(self, name, shape, dt=F32):
        return self.nc.dram_tensor(name, list(shape), dt, kind="ExternalOutput").ap()

    def dscr(self, name, shape, dt=F32):
        return self.nc.dram_tensor(name, list(shape), dt, kind="Internal").ap()

    POOLS = {"all": list(range(8)), "P": [0, 1, 2, 3, 4], "S": [5, 6, 7], "A3": [0, 1, 2], "B5": [3, 4, 5, 6, 7],
             "A2": [0, 1], "Bx": [2, 3, 4], "By": [5, 6, 7]}

    def bank(self, pool="all"):
        lst = self.POOLS[pool]
        c = self.bank_cnt.get(pool, 0)
        self.bank_cnt[pool] = c + 1
        i = lst[c % len(lst)]
        return self.banks[i], self.bank_trk[i]

    @staticmethod
    def interleave(gens):
        gens = [g for g in gens if g is not None]
        while gens:
            for g in list(gens):
                try:
                    next(g)
                except StopIteration:
                    gens.remove(g)

    def step(self, wloads, body):
        self.steps.append((wloads, body))

    def run_steps(self, lookahead=3):
        s = self.s
        steps = self.steps
        self.steps = []
        slots = {}
        ring = self.wring
        pos = [0]

        def issue(i):
            wl = steps[i][0]
            if wl is None:
                return
            si, st, strk = ring.next()
            slots[i] = (st, strk)
            for k, fn in enumerate(wl):
                s.dma("pool", (lambda e, fn=fn, st=st: fn(e, st)), [self.wpart_trk[si][k]], [], semkey="W%d_%d" % (si, k))
        n = len(steps)
        for i in range(min(lookahead, n)):
            issue(i)
        for i in range(n):
            if i + lookahead < n:
                issue(i + lookahead)
            wl, body = steps[i]
            if wl is None:
                body(None, None)
            else:
                st, strk = slots.pop(i)
                si = ring.t.index(st)
                body(st, list(self.wpart_trk[si]))

    def build(self):
        nc, s = self.nc, self.s
        mode = self.mode
        x_in = self.din("x_in", [NT, D])
        cond = self.din("cond", [128, KC])
        ident_d = self.din("ident", [128, 128])
        w_mod = self.din("w_mod", [2, D, 9 * D])
        bmodT = self.din("bmodT", [2, 128, 72])
        normgT = self.din("normgT", [2, 3, 128, KC])
        ffn_w_in = ffn_w_out = None
        if mode in ("full", "ffn1", "l0"):
            ffn_w_in = self.din("ffn_w_in", [2, 2, D, 2 * DFF])
            ffn_w_out = self.din("ffn_w_out", [2, 2, DFF, D])
        y_out = self.dout("y", [NT, D])
        A = {}
        if mode in ("full", "attn"):
            A["od_w_in"] = self.din("od_w_in", [1, D, 1536])
            A["od_w_out"] = self.din("od_w_out", [1, D, D])
            A["amask"] = self.din("amask", [128, 1536])
            A["cval"] = self.din("cval", [128, 64])
            A["qkgain"] = self.din("qkgain", [128, 128])
            A["sinks"] = self.din("sinks", [1, 16])
            A["ctx_k"] = self.din("ctx_k", [256, 256])
            A["ctx_v"] = self.din("ctx_v", [256, 256])
            A["ropeC"] = self.din("ropeC", [NT, 64])
            A["ropeS"] = self.din("ropeS", [NT, 64])
            A["nk"] = self.dout("nk", [8, 256, 256])
            A["nv"] = self.dout("nv", [8, 256, 256])
            self.t_nk, self.t_nv = Trk("nk"), Trk("nv")
        self.ain = A
        E = {}
        if mode in ("full", "l0") or mode.startswith("even"):
            E["ev_w_in"] = self.din("ev_w_in", [1, D, 3584])
            E["ev_w_out"] = self.din("ev_w_out", [1, D, D])
            E["convwT"] = self.din("convwT", [128, 124])
            E["convvec"] = self.din("convvec", [128, 12])
            E["tmask"] = self.din("tmask", [128, 512])
            E["lbraw"] = self.din("lbraw", [2, 1536])
            E["hgn"] = self.din("hgn", [128, 128])
            E["cumM"] = self.din("cumM", [128, 1024])
            E["scmask"] = self.din("scmask", [128, 128])
            E["cflag"] = self.din("cflag", [128, 1])
            E["tokm"] = self.din("tokm", [128, 4])
            E["s0"] = self.din("s0", [2, 4, 128, 128])
            E["hs_out"] = self.dout("hs_out", [8, 2, 4, 128, 128])
            self.t_hs = Trk("hs")
            self.cvs = self.dscr("cvs", [128, 4, NT], BF16)
            self.obs = self.dscr("obs", [NTILE, 128, 512])
        self.ein = E
        xs = self.dscr("xs", [D, NT])
        xs_v = xs.rearrange("(kc p) t -> p kc t", p=128)
        self.xs_trk = [[Trk("xs_%d_%d" % (kc, tt)) for tt in range(NT // 512)] for kc in range(KC)]

        self.banks = [nc.alloc_psum_tensor("bank%d" % i, [128, 512], F32) for i in range(8)]
        self.bank_trk = [Trk("bank%d" % i, excl=True) for i in range(8)]
        self.bank_cnt = {}
        NW = 4
        self.wring = Ring(nc, "wr", NW, [128, 4096], BF16)
        self.wpart_trk = [[Trk("wr%d_%d" % (i, k)) for k in range(4)] for i in range(NW)]
        ident_f = nc.alloc_sbuf_tensor("ident_f", [128, 128], F32)
        ident_b = nc.alloc_sbuf_tensor("ident_b", [128, 128], BF16)
        ones_b = nc.alloc_sbuf_tensor("ones_b", [128, 128], BF16)
        condt = nc.alloc_sbuf_tensor("condt", [128, KC], F32)
        sc_b = nc.alloc_sbuf_tensor("sc_b", [128, KC], BF16)
        bmt = nc.alloc_sbuf_tensor("bmt", [128, 2, 72], F32)
        ngt = nc.alloc_sbuf_tensor("ngt", [128, 2, 3, KC], F32)
        t_const, t_cond, t_mod, t_modAG = Trk("const"), Trk("cond"), Trk("modT"), Trk("modAG")
        self.ones_b, self.t_const, self.ident_b, self.ident_f = ones_b, t_const, ident_b, ident_f
        self.modL = {}
        self.mod_args = (w_mod, sc_b, t_cond, bmt, ngt)
        self.pending_mod = None
        ARENA_F32 = 42 * 1024
        arena = nc.alloc_sbuf_tensor("arena", [128, ARENA_F32], F32)
        self.arena = arena

        s.dma("sp", lambda e: e.dma_start(out=ident_f[:], in_=ident_d), [t_const], [], semkey="C0")
        s.dma("sp", lambda e: e.dma_start(out=condt[:], in_=cond), [t_cond], [], semkey="C1")
        s.dma("sp", lambda e: e.dma_start(out=bmt[:], in_=bmodT.rearrange("l p j -> p l j")), [t_cond], [], semkey="C2")
        s.dma("sp", lambda e: e.dma_start(out=ngt[:], in_=normgT.rearrange("l s p k -> p l s k")), [t_cond], [], semkey="C3")
        s.op("dve", lambda e: e.tensor_copy(out=ident_b[:], in_=ident_f[:]), [t_const], [t_const])
        s.op("dve", lambda e: e.memset(ones_b[:], 1.0), [], [t_const])
        self.epsc = nc.alloc_sbuf_tensor("epsc", [128, 1], F32)
        s.op("dve", lambda e: e.memset(self.epsc[:], 1024.0 * EPS), [], [t_const])
        s.op("act", lambda e: e.activation(out=sc_b[:], in_=condt[:], func=AF.Silu), [t_cond], [t_cond])

        self.phase_T0(x_in, xs_v, ident_f, t_const)
        if mode == "ffn1":
            self.phase_mod(0, w_mod, sc_b, t_cond, bmt, ngt)
            self.phase_ffn(0, 0, ffn_w_in, ffn_w_out, xs_v, ones_b, t_const)
        elif mode == "attn":
            self.phase_mod(1, w_mod, sc_b, t_cond, bmt, ngt)
            self.phase_attn(xs_v)
        elif mode.startswith("even"):
            self.phase_mod(0, w_mod, sc_b, t_cond, bmt, ngt)
            self.phase_even(xs_v)
        elif mode in ("full", "l0"):
            for l in ((0, 1) if mode == "full" else (0,)):
                if l == 0:
                    self.phase_mod(l, w_mod, sc_b, t_cond, bmt, ngt)
                    if mode == "full":
                        self.pending_mod = self.make_mod(1, w_mod, sc_b, t_cond, bmt, ngt)
                else:
                    assert self.pending_mod is None
                    self.use_mod(1)
                self.phase_ffn(l, 0, ffn_w_in, ffn_w_out, xs_v, ones_b, t_const)
                if l == 0:
                    self.phase_even(xs_v)
                else:
                    self.phase_attn(xs_v)
                last = (mode == "full" and l == 1)
                self.phase_ffn(l, 1, ffn_w_in, ffn_w_out, xs_v, ones_b, t_const, t1_out=(y_out if last else None))
                self.t1_done = last
        if not getattr(self, "t1_done", False):
            self.phase_T1(xs_v, y_out, ident_f, t_const)
        s.barrier()
        s.emit()
        return nc

    def phase_T0(self, x_in, xs_v, ident_f, t_const):
        nc, s = self.nc, self.s
        s.barrier()
        ar = self.arena
        tin = [ar[:, i * 1024:(i + 1) * 1024] for i in range(3)]
        tin_trk = [Trk("t0in%d" % i) for i in range(3)]
        stg = [ar[:, 3072 + i * 4096: 3072 + (i + 1) * 4096].rearrange("p (k t) -> p k t", k=KC) for i in range(2)]
        stg_trk = [[Trk("t0stg%d_%d" % (i, h)) for h in range(2)] for i in range(2)]
        x_v = x_in.rearrange("(n p) d -> n p d", p=128)

        def load(n):
            i = n % 3
            s.dma("sp", lambda e: e.dma_start(out=tin[i], in_=x_v[n]), [tin_trk[i]], [], semkey="T0in%d" % i)
        load(0)
        load(1)
        for n in range(NTILE):
            if n + 2 < NTILE:
                load(n + 2)
            i = n % 3
            g = n // 4
            si = g % 2
            for half in range(2):
                bk, bt = self.bank()
                for q in range(4):
                    kc = half * 4 + q
                    s.op("pe", lambda e, kc=kc, q=q, bk=bk, i=i: e.transpose(
                        out=bk[:, q * 128:(q + 1) * 128], in_=tin[i][:, kc * 128:(kc + 1) * 128], identity=ident_f[:]),
                        [tin_trk[i], t_const], [bt], signal=(q == 3))
                dst = stg[si][:, half * 4:(half + 1) * 4, (n % 4) * 128:(n % 4 + 1) * 128]
                src = bk[:, :].rearrange("p (k t) -> p k t", k=4)
                eng = "dve" if half == 0 else "act"
                if eng == "dve":
                    s.op("dve", lambda e, dst=dst, src=src: e.tensor_copy(out=dst, in_=src), [bt], [stg_trk[si][half]])
                else:
                    s.op("act", lambda e, dst=dst, src=src: e.activation(out=dst, in_=src, func=AF.Copy), [bt], [stg_trk[si][half]])
            if n % 4 == 3:
                tt = n // 4
                s.dma("sp", lambda e, si=si, tt=tt: e.dma_start(out=xs_v[:, :, tt * 512:(tt + 1) * 512], in_=stg[si]),
                      [self.xs_trk[kc][tt] for kc in range(KC)], list(stg_trk[si]), semkey="T0st%d" % si)

    def t1_setup(self, xs_v, y_out, off):
        s = self.s
        ar = self.arena
        ident_f, t_const = self.ident_f, self.t_const
        fin = [ar[:, off + i * 4096: off + (i + 1) * 4096].rearrange("p (k t) -> p k t", k=KC) for i in range(2)]
        fin_trk = [Trk("t1in%d" % i) for i in range(2)]
        tout = [ar[:, off + 8192 + i * 1024: off + 8192 + (i + 1) * 1024] for i in range(3)]
        tout_trk = [Trk("t1out%d" % i) for i in range(3)]
        assert off + 8192 + 3072 <= ar.shape[1], "arena overflow (t1)"
        y_v = y_out.rearrange("(n p) d -> n p d", p=128)
        ytrk = Trk("y")

        def load(tt):
            i = tt % 2
            s.dma("sp", lambda e: e.dma_start(out=fin[i], in_=xs_v[:, :, tt * 512:(tt + 1) * 512]), [fin_trk[i]],
                  [self.xs_trk[kc][tt] for kc in range(KC)], semkey="T1in%d" % i)

        def comp(tt):
            i = tt % 2
            for q4 in range(4):
                n = tt * 4 + q4
                oi = n % 3
                for half in range(2):
                    bk, bt = self.bank()
                    for q in range(4):
                        kc = half * 4 + q
                        s.op("pe", lambda e, kc=kc, q=q, bk=bk, i=i, q4=q4: e.transpose(
                            out=bk[:, q * 128:(q + 1) * 128], in_=fin[i][:, kc, q4 * 128:(q4 + 1) * 128], identity=ident_f[:]),
                            [fin_trk[i], t_const], [bt], signal=(q == 3))
                    dst = tout[oi][:, half * 512:(half + 1) * 512]
                    if half == 0:
                        s.op("dve", lambda e, dst=dst, bk=bk: e.tensor_copy(out=dst, in_=bk[:, :]), [bt], [tout_trk[oi]])
                    else:
                        s.op("act", lambda e, dst=dst, bk=bk: e.activation(out=dst, in_=bk[:, :], func=AF.Copy), [bt], [tout_trk[oi]])
                s.dma("sp", lambda e, oi=oi, n=n: e.dma_start(out=y_v[n], in_=tout[oi]), [ytrk], [tout_trk[oi]],
                      semkey="T1st%d" % oi)
        return load, comp

    def phase_T1(self, xs_v, y_out, ident_f, t_const):
        self.s.barrier()
        load, comp = self.t1_setup(xs_v, y_out, 0)
        load(0)
        for tt in range(NT // 512):
            if tt + 1 < NT // 512:
                load(tt + 1)
            comp(tt)

    def make_mod(self, l, w_mod, sc_b, t_cond, bmt, ngt):
        nc, s = self.nc, self.s
        modT = nc.alloc_sbuf_tensor("modT_%d" % l, [128, 72], F32)
        macc = nc.alloc_sbuf_tensor("macc_%d" % l, [128, 72], F32)
        modA = nc.alloc_sbuf_tensor("modA_%d" % l, [128, 3, KC], F32)
        modG = nc.alloc_sbuf_tensor("modG_%d" % l, [128, 3, KC], F32)
        t_mod, t_modAG, t_macc = Trk("modT%d" % l), Trk("modAG%d" % l), Trk("macc%d" % l)
        self.modL[l] = (modT, modA, modG, t_mod, t_modAG)
        wv = w_mod[l].rearrange("(kc p) c -> p kc c", p=128)
        steps = []
        for cb in range(18):
            def wl(e, st, cb=cb):
                return e.dma_start(out=st[:, :].rearrange("p (k c) -> p k c", k=KC), in_=wv[:, :, cb * 512:(cb + 1) * 512])

            def body(st, strk, cb=cb):
                stv = st[:, :].rearrange("p (k c) -> p k c", k=KC)
                bk, bt = self.bank()
                for q in range(4):
                    for kc in range(KC):
                        s.op("pe", lambda e, q=q, kc=kc, stv=stv, bk=bk: e.matmul(
                            bk[:, q:q + 1], stv[:, kc, q * 128:(q + 1) * 128], sc_b[:, kc:kc + 1],
                            start=(kc == 0), stop=(kc == KC - 1)),
                            list(strk) + [t_cond], [bt], signal=(kc == KC - 1))
                s.op("dve", lambda e, bk=bk, cb=cb: e.tensor_copy(out=macc[:, cb * 4:(cb + 1) * 4], in_=bk[:, 0:4]), [bt], [t_macc])
            steps.append(([wl], body))

        def finalize():
            s.op("dve", lambda e: e.tensor_tensor(out=modT[:], in0=macc[:], in1=bmt[:, l, :], op=ALU.add), [t_macc, t_cond], [t_mod])
            mv = modT[:, :].rearrange("p (v k) -> p v k", k=KC)
            for sl in range(3):
                s.op("dve", lambda e, sl=sl: e.scalar_tensor_tensor(
                    out=modA[:, sl, :], in0=mv[:, 3 * sl + 1, :], scalar=1.0, in1=ngt[:, l, sl, :], op0=ALU.add, op1=ALU.mult),
                    [t_mod, t_cond], [t_modAG])
                s.op("dve", lambda e, sl=sl: e.tensor_scalar(
                    out=modA[:, sl, :], in0=modA[:, sl, :], scalar1=32.0, scalar2=None, op0=ALU.mult), [t_modAG], [t_modAG])
                gsc = 1.0 if sl == 1 else 0.5
                s.op("dve", lambda e, sl=sl, gsc=gsc: e.tensor_scalar(
                    out=modG[:, sl, :], in0=mv[:, 3 * sl + 2, :], scalar1=gsc, scalar2=None, op0=ALU.mult), [t_mod], [t_modAG])
        return steps, finalize

    def use_mod(self, l):
        self.modT, self.modA, self.modG, self.t_mod, self.t_modAG = self.modL[l]

    def phase_mod(self, l, w_mod, sc_b, t_cond, bmt, ngt):
        self.s.barrier()
        steps, fin = self.make_mod(l, w_mod, sc_b, t_cond, bmt, ngt)
        self.steps.extend(steps)
        self.run_steps()
        fin()
        self.use_mod(l)

    def norm_tile(self, xt, xt_trk, sl, hdst, hdst_trk, nr, W=512, pool="all"):
        s = self.s
        modT, modA = self.modT, self.modA
        ones_b, t_const = self.ones_b, self.t_const
        bk, bt = self.bank(pool)
        for kc in range(KC):
            sq, sq_trk = nr["sq"][kc % len(nr["sq"])]
            s.op("act", lambda e, kc=kc, sq=sq: e.activation(out=sq, in_=xt[:, kc, :], func=AF.Square), [xt_trk], [sq_trk])
            s.op("pe", lambda e, kc=kc, sq=sq: e.matmul(bk[:, 0:W], ones_b[:, :], sq, start=(kc == 0), stop=(kc == KC - 1)),
                 [sq_trk, t_const], [bt], signal=True)
        rstd, rstd_trk = nr["rstd"]
        s.op("act", lambda e: e.activation(out=rstd, in_=bk[:, 0:W], func=AF.Ln, bias=self.epsc[:, 0:1], scale=1.0), [bt, t_const], [rstd_trk])
        s.op("act", lambda e: e.activation(out=rstd, in_=rstd, func=AF.Exp, scale=-0.5), [rstd_trk], [rstd_trk])
        for kc in range(KC):
            tmp, tmp_trk = nr["tmp"][kc % len(nr["tmp"])]
            s.op("dve", lambda e, kc=kc, tmp=tmp: e.tensor_tensor(out=tmp, in0=xt[:, kc, :], in1=rstd, op=ALU.mult),
                 [xt_trk, rstd_trk], [tmp_trk])
            s.op("act", lambda e, kc=kc, tmp=tmp: e.activation(out=hdst[:, kc, :], in_=tmp, func=AF.Identity,
                                                               bias=modT[:, 3 * sl * KC + kc: 3 * sl * KC + kc + 1],
                                                               scale=modA[:, sl, kc:kc + 1]),
                 [tmp_trk, self.t_mod, self.t_modAG], [hdst_trk])

    def phase_attn(self, xs_v):
        nc, s = self.nc, self.s
        A = self.ain
        sl = 1
        s.barrier()
        cv = Carve(self.arena)
        ident_b, ones_b, t_const = self.ident_b, self.ones_b, self.t_const
        win = cv.bf(KC * 1536).rearrange("p (k c) -> p k c", k=KC)
        t_win = [Trk("awin%d" % i) for i in range(3)]
        kT = cv.bf(2 * NT).rearrange("p (r t) -> p r t", r=2)
        t_kT = [Trk("kT%d" % n) for n in range(NTILE)]
        Vv = cv.bf(NTILE * 256).rearrange("p (n f) -> p n f", n=NTILE)
        t_V = [Trk("V%d" % n) for n in range(NTILE)]
        kTc = cv.bf(512).rearrange("p (r t) -> p r t", r=2)
        Vc = cv.bf(512).rearrange("p (c f) -> p c f", c=2)
        t_ctx = Trk("ctx")
        msk = cv.bf(1536).rearrange("p (v j q) -> p v j q", v=6, j=2)
        cval = cv.bf(64)
        t_msk = Trk("msk")
        gain = cv.f32(128)
        t_gain = Trk("gain")
        epsq = cv.f32(1)
        xin = cv.f32(KC * 512).rearrange("p (k t) -> p k t", k=KC)
        t_xin = Trk("axin")
        nr = {"sq": cv.ring(2, 512, "bf", "asq"), "tmp": cv.ring(2, 512, "f32", "atmp"), "rstd": (cv.f32(512), Trk("arstd"))}
        h = cv.bf(KC * 512).rearrange("p (k t) -> p k t", k=KC)
        t_h = Trk("ah")
        qkv, t_qkv = cv.f32(1536), Trk("qkv")
        bufA, t_bufA = cv.f32(1280), Trk("bufA")
        bufB, t_bufB = cv.f32(1280), Trk("bufB")
        ss, t_ss = cv.f32(20), Trk("ss")
        rinv, t_rinv = cv.f32(20), Trk("rinv")
        ropeC = cv.ring(2, 64, "f32", "ropeC")
        ropeS = cv.ring(2, 64, "f32", "ropeS")
        qb, t_qb = cv.bf(1024), Trk("qb")
        kb, t_kb = cv.bf(256), Trk("kb")
        qT = cv.ring(3, 1024, "bf", "qT")
        pt = cv.ring(10, 512, "bf", "pt")
        rec = cv.ring(2, 512, "f32", "rec")
        at_g = cv.bf(8 * 512).rearrange("p (c t) -> p c t", c=8)
        t_at = [Trk("at%d" % i) for i in range(4)]
        xc = cv.ring(2, 512, "f32", "axc")
        xo = cv.ring(2, 512, "f32", "axo")
        esink = cv.bf(2048)
        sk, t_sk = cv.f32(16), Trk("sk")
        ctxs = cv.f32(512)
        t_ctxs = Trk("ctxs")
        ctxb = cv.bf(512)
        cnt = {"pt": 0, "rec": 0, "xc": 0}

        wv = A["od_w_in"][0].rearrange("(kc p) c -> p kc c", p=128)
        for c3 in range(3):
            s.dma("pool", lambda e, c3=c3: e.dma_start(out=win[:, :, c3 * 512:(c3 + 1) * 512], in_=wv[:, :, c3 * 512:(c3 + 1) * 512]),
                  [t_win[c3]], [], semkey="AW%d" % c3)
        s.dma("pool", lambda e: e.dma_start(out=msk.rearrange("p v j q -> p (v j q)"), in_=A["amask"]), [t_msk], [], semkey="AM")
        s.dma("pool", lambda e: e.dma_start(out=cval, in_=A["cval"]), [t_msk], [], semkey="AM")
        s.dma("sp", lambda e: e.dma_start(out=gain, in_=A["qkgain"]), [t_gain], [], semkey="AG")
        s.op("dve", lambda e: e.tensor_scalar(out=gain[:, 0:64], in0=gain[:, 0:64], scalar1=0.125, scalar2=None, op0=ALU.mult), [t_gain], [t_gain])
        s.op("dve", lambda e: e.memset(epsq, EPS), [], [t_gain])
        s.dma("sp", lambda e: e.dma_start(out=sk[0:1, :], in_=A["sinks"]), [t_sk], [], semkey="ASK")
        s.op("act", lambda e: e.activation(out=sk[0:1, :], in_=sk[0:1, :], func=AF.Exp), [t_sk], [t_sk])
        s.op("dve", lambda e: e.tensor_copy(out=esink[0:1, :].rearrange("p (h q) -> p h q", h=16),
                                            in_=sk[0:1, :].unsqueeze(2).to_broadcast([1, 16, 128])), [t_sk], [t_sk])
        s.dma("sp", lambda e: e.dma_start(out=ctxs.rearrange("p (c f) -> p c f", c=2), in_=A["ctx_k"].rearrange("(c p) f -> p c f", p=128)),
              [t_ctxs], [], semkey="ACX")
        s.op("dve", lambda e: e.tensor_copy(out=ctxb, in_=ctxs), [t_ctxs], [t_ctxs])
        bk, bt = self.bank()
        bkb = bk[:, :].bitcast(BF16)
        for c in range(2):
            for pr in range(2):
                i4 = c * 2 + pr
                s.op("pe", lambda e, c=c, pr=pr, i4=i4: e.transpose(out=bkb[:, i4 * 128:(i4 + 1) * 128],
                                                                    in_=ctxb[:, c * 256 + pr * 128: c * 256 + (pr + 1) * 128], identity=ident_b[:]),
                     [t_ctxs, t_const], [bt], signal=(i4 == 3))
        s.op("dve", lambda e: e.tensor_copy(out=kTc.rearrange("p r (c t) -> p c r t", c=2),
                                            in_=bkb[:, 0:512].rearrange("p (c r t) -> p c r t", c=2, r=2)), [bt], [t_ctx])
        s.dma("sp", lambda e: e.dma_start(out=ctxs.rearrange("p (c f) -> p c f", c=2), in_=A["ctx_v"].rearrange("(c p) f -> p c f", p=128)),
              [t_ctxs], [], semkey="ACX")
        s.op("dve", lambda e: e.tensor_copy(out=Vc.rearrange("p c f -> p (c f)"), in_=ctxs), [t_ctxs], [t_ctxs, t_ctx])

        def mvar(n):
            return 0 if n == 0 else (5 if n == NTILE - 1 else 1 + n % 4)

        def stageA(n):
            G, p4 = n // 4, n % 4
            if p4 == 0:
                s.dma("sp", lambda e: e.dma_start(out=xin, in_=xs_v[:, :, G * 512:(G + 1) * 512]), [t_xin],
                      [self.xs_trk[kc][G] for kc in range(KC)], semkey="AXin")
                self.norm_tile(xin, t_xin, sl, h, t_h, nr, pool="A2")
                yield
            rC, t_rC = ropeC[n % 2]
            rS, t_rS = ropeS[n % 2]
            s.dma("sp", lambda e: e.dma_start(out=rC, in_=A["ropeC"][n * 128:(n + 1) * 128, :]), [t_rC], [], semkey="ARC%d" % (n % 2))
            s.dma("sp", lambda e: e.dma_start(out=rS, in_=A["ropeS"][n * 128:(n + 1) * 128, :]), [t_rS], [], semkey="ARS%d" % (n % 2))
            for c3 in range(3):
                bk, bt = self.bank("A2")
                for kc in range(KC):
                    s.op("pe", lambda e, kc=kc, c3=c3, bk=bk: e.matmul(bk[:, :], h[:, kc, p4 * 128:(p4 + 1) * 128], win[:, kc, c3 * 512:(c3 + 1) * 512],
                                                                       start=(kc == 0), stop=(kc == KC - 1)),
                         [t_h, t_win[c3]], [bt], signal=(kc == KC - 1))
                if c3 < 2:
                    s.op("act", lambda e, c3=c3, bk=bk: e.activation(out=qkv[:, c3 * 512:(c3 + 1) * 512], in_=bk[:, :], func=AF.Copy), [bt], [t_qkv])
                else:
                    s.op("dve", lambda e, c3=c3, bk=bk: e.tensor_copy(out=qkv[:, c3 * 512:(c3 + 1) * 512], in_=bk[:, :]), [bt], [t_qkv])
                yield
            s.op("act", lambda e: e.activation(out=bufA, in_=qkv[:, 0:1280], func=AF.Square), [t_qkv], [t_bufA])
            s.op("dve", lambda e: e.tensor_reduce(out=ss, in_=bufA.rearrange("p (h d) -> p h d", d=64), axis=AX.X, op=ALU.add), [t_bufA], [t_ss])
            s.op("act", lambda e: e.activation(out=rinv, in_=ss, func=AF.Ln, bias=epsq[:, 0:1], scale=1.0 / 64.0), [t_ss, t_gain], [t_rinv])
            s.op("act", lambda e: e.activation(out=rinv, in_=rinv, func=AF.Exp, scale=-0.5), [t_rinv], [t_rinv])
            yield
            s.op("dve", lambda e: e.tensor_tensor(out=bufB.rearrange("p (h d) -> p h d", d=64), in0=qkv[:, 0:1280].rearrange("p (h d) -> p h d", d=64),
                                                  in1=rinv.unsqueeze(2).to_broadcast([128, 20, 64]), op=ALU.mult), [t_qkv, t_rinv], [t_bufB])
            s.op("pool", lambda e: e.tensor_tensor(out=bufB[:, 0:1024].rearrange("p (h d) -> p h d", d=64), in0=bufB[:, 0:1024].rearrange("p (h d) -> p h d", d=64),
                                                   in1=gain[:, 0:64].unsqueeze(1).to_broadcast([128, 16, 64]), op=ALU.mult), [t_bufB, t_gain], [t_bufB])
            s.op("pool", lambda e: e.tensor_tensor(out=bufB[:, 1024:1280].rearrange("p (h d) -> p h d", d=64), in0=bufB[:, 1024:1280].rearrange("p (h d) -> p h d", d=64),
                                                   in1=gain[:, 64:128].unsqueeze(1).to_broadcast([128, 4, 64]), op=ALU.mult), [t_bufB, t_gain], [t_bufB])
            yield
            s.op("dve", lambda e: e.tensor_tensor(out=bufA.rearrange("p (h d) -> p h d", d=64), in0=bufB.rearrange("p (h d) -> p h d", d=64),
                                                  in1=rC.unsqueeze(1).to_broadcast([128, 20, 64]), op=ALU.mult), [t_bufB, t_rC], [t_bufA])
            xv = bufB.rearrange("p (h a f e) -> p h a f e", a=2, f=2, e=16)
            tv = qkv[:, 0:1280].rearrange("p (h a f e) -> p h a f e", a=2, f=2, e=16)
            sv = rS.rearrange("p (a f e) -> p a f e", a=2, f=2)
            for f in range(2):
                s.op("pool", lambda e, f=f: e.tensor_tensor(out=tv[:, :, :, f, :], in0=xv[:, :, :, 1 - f, :],
                                                            in1=sv[:, :, f, :].unsqueeze(1).to_broadcast([128, 20, 2, 16]), op=ALU.mult),
                     [t_bufB, t_rS], [t_qkv])
            s.op("dve", lambda e: e.tensor_tensor(out=bufB, in0=bufA, in1=qkv[:, 0:1280], op=ALU.add), [t_bufA, t_qkv], [t_bufB])
            yield
            for pr in range(2):
                s.op("act", lambda e, pr=pr: e.activation(
                    out=qb[:, pr * 512:(pr + 1) * 512].rearrange("p (g two d) -> p g two d", g=4, two=2),
                    in_=bufB[:, pr * 512:(pr + 1) * 512].rearrange("p (two g d) -> p g two d", two=2, g=4), func=AF.Copy), [t_bufB], [t_qb])
            s.op("act", lambda e: e.activation(out=kb, in_=bufB[:, 1024:1280], func=AF.Copy), [t_bufB], [t_kb])
            s.op("pool", lambda e: e.tensor_copy(out=Vv[:, n, :], in_=qkv[:, 1280:1536]), [t_qkv], [t_V[n]])
            yield
            if p4 < 2:
                s.dma("sp", lambda e: e.dma_start(out=A["nk"][G, p4 * 128:(p4 + 1) * 128, :], in_=bufB[:, 1024:1280]), [self.t_nk], [t_bufB], semkey="ANK")
                s.dma("sp", lambda e: e.dma_start(out=A["nv"][G, p4 * 128:(p4 + 1) * 128, :], in_=qkv[:, 1280:1536]), [self.t_nv], [t_qkv], semkey="ANV")
            bk, bt = self.bank("A2")
            bkb = bk[:, :].bitcast(BF16)
            for pr in range(2):
                for g in range(4):
                    i8 = pr * 4 + g
                    s.op("pe", lambda e, pr=pr, g=g, i8=i8, bkb=bkb: e.transpose(out=bkb[:, i8 * 128:(i8 + 1) * 128],
                                                                                 in_=qb[:, i8 * 128:(i8 + 1) * 128], identity=ident_b[:]),
                         [t_qb, t_const], [bt], signal=(i8 == 7))
            qTt, t_qT = qT[n % 3]
            s.op("act", lambda e, bkb=bkb, qTt=qTt: e.activation(out=qTt, in_=bkb[:, :], func=AF.Copy), [bt], [t_qT])
            yield
            bk2, bt2 = self.bank("A2")
            bk2b = bk2[:, :].bitcast(BF16)
            for pr in range(2):
                s.op("pe", lambda e, pr=pr, bk2b=bk2b: e.transpose(out=bk2b[:, pr * 128:(pr + 1) * 128], in_=kb[:, pr * 128:(pr + 1) * 128], identity=ident_b[:]),
                     [t_kb, t_const], [bt2], signal=(pr == 1))
            s.op("dve", lambda e, bk2b=bk2b: e.tensor_copy(out=kT[:, :, n * 128:(n + 1) * 128], in_=bk2b[:, 0:256].rearrange("p (r t) -> p r t", r=2)),
                 [bt2], [t_kT[n]])

        def stageB(n, khs, bpool, ptr, rci):
            G, p4 = n // 4, n % 4
            var = mvar(n)
            qTt, t_qT = qT[n % 3]
            qTv = qTt.rearrange("p (r gq) -> p r gq", r=2)
            loc = [max(n - 1, 0), n, min(n + 1, NTILE - 1)]
            for kh in khs:
                pr, lo = kh // 2, (kh % 2) * 64
                pts = []
                for j in range(5):
                    if j < 2:
                        lk = kTc[lo:lo + 64, pr, j * 128:(j + 1) * 128]
                        lv = Vc[:, j, kh * 64:(kh + 1) * 64]
                        rd = [t_ctx]
                    else:
                        m = loc[j - 2]
                        lk = kT[lo:lo + 64, pr, m * 128:(m + 1) * 128]
                        lv = Vv[:, m, kh * 64:(kh + 1) * 64]
                        rd = [t_kT[m], t_V[m]]
                    bk, bt = self.bank(bpool)
                    rq = qTv[lo:lo + 64, pr, :]
                    if j in (2, 4):
                        s.op("pe", lambda e, lk=lk, bk=bk, rq=rq: e.matmul(bk[:, :], lk, rq, start=True, stop=False),
                             rd + [t_qT], [bt], signal=False)
                        mk = msk[:, var, (j - 2) // 2, :]
                        for g in range(4):
                            s.op("pe", lambda e, mk=mk, bk=bk, g=g: e.matmul(bk[:, g * 128:(g + 1) * 128], ident_b[:, :], mk, start=False, stop=(g == 3)),
                                 [t_msk, t_const], [bt], signal=(g == 3))
                    else:
                        s.op("pe", lambda e, lk=lk, bk=bk, rq=rq: e.matmul(bk[:, :], lk, rq, start=True, stop=True),
                             rd + [t_qT], [bt], signal=True)
                    ptt, t_pt = ptr[j]
                    s.op("act", lambda e, bk=bk, ptt=ptt: e.activation(out=ptt, in_=bk[:, :], func=AF.Exp), [bt], [t_pt])
                    pts.append((ptt, t_pt, lv, rd))
                    yield
                bo, bot = self.bank(bpool)
                bd, bdt = self.bank(bpool)
                bo_v = bo[lo:lo + 64, :]
                bd_v = bd[lo:lo + 64, :]
                for j, (ptt, t_pt, lv, rd) in enumerate(pts):
                    s.op("pe", lambda e, ptt=ptt, lv=lv, j=j, bo_v=bo_v: e.matmul(bo_v, lv, ptt, start=(j == 0), stop=(j == 4)),
                         rd + [t_pt], [bot], signal=(j == 4))
                yield
                es = esink[0:1, kh * 512:(kh + 1) * 512]
                s.op("pe", lambda e, bd_v=bd_v, es=es: e.matmul(bd_v, ones_b[0:1, 0:64], es, start=True, stop=False),
                     [t_sk, t_const], [bdt], signal=False)
                for j, (ptt, t_pt, lv, rd) in enumerate(pts):
                    dl = cval[:, 0:64] if j < 2 else ones_b[:, 0:64]
                    s.op("pe", lambda e, ptt=ptt, j=j, bd_v=bd_v, dl=dl: e.matmul(bd_v, dl, ptt, start=False, stop=(j == 4)),
                         [t_pt, t_const, t_msk], [bdt], signal=(j == 4))
                yield
                rc, t_rc = rec[rci]
                rc_v = rc[lo:lo + 64, :]
                at_v = at_g[lo:lo + 64, pr * 4:(pr + 1) * 4, p4 * 128:(p4 + 1) * 128]
                s.op("act", lambda e, rc_v=rc_v, bd_v=bd_v: e.activation(out=rc_v, in_=bd_v, func=AF.Ln), [bdt], [t_rc])
                s.op("act", lambda e, rc_v=rc_v: e.activation(out=rc_v, in_=rc_v, func=AF.Exp, scale=-1.0), [t_rc], [t_rc])
                s.op("dve", lambda e, rc_v=rc_v, bo_v=bo_v, at_v=at_v: e.tensor_tensor(
                    out=at_v, in0=bo_v.rearrange("p (g q) -> p g q", g=4), in1=rc_v.rearrange("p (g q) -> p g q", g=4), op=ALU.mult),
                    [bot, t_rc], [t_at[p4]])
                yield

        wo_v = A["od_w_out"][0].rearrange("(r two g d) c -> two d r g c", r=2, two=2, g=4, d=64)

        def outproj_steps(G):
            for dc in range(KC):
                def mk_wl(two, r, dc=dc):
                    def wl(e, st):
                        return e.dma_start(out=st[two * 64:(two + 1) * 64, r * 512:(r + 1) * 512].rearrange("p (g c) -> p g c", g=4),
                                           in_=wo_v[two, :, r, :, dc * 128:(dc + 1) * 128])
                    return wl
                wls = [mk_wl(two, r) for two in range(2) for r in range(2)]

                def body(st, strk, dc=dc, G=G):
                    wv_ = st[:, 0:1024].rearrange("p (c8 c) -> p c8 c", c8=8)
                    ci = cnt["xc"] % 2
                    cnt["xc"] += 1
                    xct, t_xc = xc[ci]
                    xot, t_xo = xo[ci]
                    s.dma("sp", lambda e: e.dma_start(out=xct, in_=xs_v[:, dc, G * 512:(G + 1) * 512]), [t_xc], [self.xs_trk[dc][G]], semkey="AXc%d" % ci)
                    bk, bt = self.bank()
                    for c8 in range(8):
                        s.op("pe", lambda e, c8=c8, bk=bk: e.matmul(bk[:, :], wv_[:, c8, :], at_g[:, c8, :], start=(c8 == 0), stop=(c8 == 7)),
                             list(strk) + list(t_at), [bt], signal=(c8 == 7))
                    s.op("dve", lambda e, bk=bk, mg=self.modG[:, sl, dc:dc + 1]: e.scalar_tensor_tensor(out=xot, in0=bk[:, :], scalar=mg, in1=xct,
                                                                        op0=ALU.mult, op1=ALU.add), [bt, t_xc, self.t_modAG], [t_xo])
                    s.dma("sp", lambda e: e.dma_start(out=xs_v[:, dc, G * 512:(G + 1) * 512], in_=xot), [self.xs_trk[dc][G]], [t_xo], semkey="AXo%d" % ci)
                self.step(wls, body)

        for n in range(NTILE + 2):
            ga = (lambda n=n: stageA(n)) if n < NTILE else None
            gb = (lambda n=n: stageB(n - 2, (0, 2), "Bx", pt[0:5], 0)) if n >= 2 else None
            gc = (lambda n=n: stageB(n - 2, (1, 3), "By", pt[5:10], 1)) if n >= 2 else None
            self.step(None, lambda a, b, ga=ga, gb=gb, gc=gc: self.interleave([ga() if ga else None, gb() if gb else None, gc() if gc else None]))
            if n >= 2 and (n - 2) % 4 == 3:
                outproj_steps((n - 2) // 4)
        self.run_steps()

    def phase_even(self, xs_v):
        nc, s = self.nc, self.s
        A = self.ein
        sl = 1
        s.barrier()
        ident_b, ones_b, t_const = self.ident_b, self.ones_b, self.t_const
        cvs, obs = self.cvs, self.obs
        t_cvs = [Trk("cvs%d" % g) for g in range(8)]
        t_obs = [Trk("obs%d" % n) for n in range(NTILE)]
        cv = Carve(self.arena)
        xin, t_xin = cv.f32(KC * 512).rearrange("p (k t) -> p k t", k=KC), Trk("exin")
        h, t_h = cv.bf(KC * 512).rearrange("p (k t) -> p k t", k=KC), Trk("eh")
        nr = {"sq": cv.ring(2, 512, "bf", "esq"), "tmp": cv.ring(2, 512, "f32", "etmp"), "rstd": (cv.f32(512), Trk("erstd"))}
        ones_f = cv.f32(128)
        cwT = cv.f32(124).rearrange("p (c j) -> p c j", c=4)
        cvec = cv.f32(12).rearrange("p (w c) -> p w c", w=3)
        tmask = cv.bf(512)
        lbt = [cv.f32(512) for d_ in range(2)]
        oml = [cv.f32(512) for d_ in range(2)]
        hgn = cv.f32(128)
        cumM = cv.f32(1024).rearrange("p (d m t) -> p d m t", d=2, m=4)
        scm = cv.bf(128).rearrange("p (d t) -> p d t", d=2)
        cflag = cv.f32(1)
        tokm = cv.f32(4)
        eps5 = cv.f32(1)
        epsq = cv.f32(1)
        t_ec = Trk("econst")
        base = cv.off

        s.op("dve", lambda e: e.memset(ones_f, 1.0), [], [t_ec])
        s.op("dve", lambda e: e.memset(eps5, 512.0 * 1e-5), [], [t_ec])
        s.op("dve", lambda e: e.memset(epsq, EPS), [], [t_ec])
        s.dma("sp", lambda e: e.dma_start(out=cwT.rearrange("p c j -> p (c j)"), in_=A["convwT"]), [t_ec], [], semkey="EC0")
        s.dma("sp", lambda e: e.dma_start(out=cvec.rearrange("p w c -> p (w c)"), in_=A["convvec"]), [t_ec], [], semkey="EC0")
        s.dma("pool", lambda e: e.dma_start(out=tmask, in_=A["tmask"]), [t_ec], [], semkey="EC1")
        s.dma("sp", lambda e: e.dma_start(out=hgn, in_=A["hgn"]), [t_ec], [], semkey="EC0")
        s.dma("sp", lambda e: e.dma_start(out=cumM.rearrange("p d m t -> p (d m t)"), in_=A["cumM"]), [t_ec], [], semkey="EC0")
        s.dma("pool", lambda e: e.dma_start(out=scm.rearrange("p d t -> p (d t)"), in_=A["scmask"]), [t_ec], [], semkey="EC1")
        s.dma("sp", lambda e: e.dma_start(out=cflag, in_=A["cflag"]), [t_ec], [], semkey="EC0")
        s.dma("sp", lambda e: e.dma_start(out=tokm, in_=A["tokm"]), [t_ec], [], semkey="EC0")
        raw = self.arena[0:1, base:base + 1536]
        t_raw = Trk("lbraw")
        for d_ in range(2):
            s.dma("sp", lambda e, d_=d_: e.dma_start(out=raw, in_=A["lbraw"][d_:d_ + 1, :]), [t_raw], [], semkey="EC2")
            s.op("act", lambda e: e.activation(out=raw, in_=raw, func=AF.Exp), [t_raw], [t_raw])
            s.op("dve", lambda e: e.tensor_tensor(out=raw[:, 512:1024], in0=raw[:, 512:1024], in1=raw[:, 1024:1536], op=ALU.add), [t_raw], [t_raw])
            s.op("dve", lambda e: e.tensor_tensor(out=raw[:, 512:1024], in0=raw[:, 512:1024], in1=raw[:, 0:512], op=ALU.add), [t_raw], [t_raw])
            s.op("dve", lambda e: e.reciprocal(out=raw[:, 512:1024], in_=raw[:, 512:1024]), [t_raw], [t_raw])
            s.op("dve", lambda e: e.tensor_tensor(out=raw[:, 0:512], in0=raw[:, 0:512], in1=raw[:, 512:1024], op=ALU.mult), [t_raw], [t_raw])
            bk, bt = self.bank()
            s.op("pe", lambda e, bk=bk: e.matmul(bk[:, :], ones_f[0:1, :], raw[:, 0:512], start=True, stop=True), [t_raw, t_ec], [bt])
            s.op("dve", lambda e, bk=bk, d_=d_: e.tensor_copy(out=lbt[d_], in_=bk[:, :]), [bt], [t_ec])
            s.op("dve", lambda e, d_=d_: e.tensor_scalar(out=oml[d_], in0=lbt[d_], scalar1=-1.0, scalar2=1.0, op0=ALU.mult, op1=ALU.add), [t_ec], [t_ec])
        s.barrier()

        def load_norm(G, pool="all"):
            s.dma("sp", lambda e: e.dma_start(out=xin, in_=xs_v[:, :, G * 512:(G + 1) * 512]), [t_xin],
                  [self.xs_trk[kc][G] for kc in range(KC)], semkey="EXin")
            self.norm_tile(xin, t_xin, sl, h, t_h, nr, pool=pool)

        cv1 = Carve(self.arena)
        cv1.off = base
        wcv = cv1.bf(KC * 1024).rearrange("p (k c) -> p k c", k=KC)
        t_wcv = [Trk("wcv%d" % i) for i in range(2)]
        aT = cv1.bf(4 * (NT + 32)).rearrange("p (c t) -> p c t", c=4)
        t_aT = [Trk("aT%d" % g) for g in range(8)]
        t_apad = Trk("apad")
        diag = cv1.bf(4 * 31 * 128).rearrange("p (c j q) -> p c j q", c=4, j=31)
        t_diag = Trk("diag")
        yb = cv1.f32(4 * 512).rearrange("p (c t) -> p c t", c=4)
        t_yb = [Trk("yb%d" % c) for c in range(4)]
        ybf = cv1.ring(2, 512, "bf", "ybf")
        ysq = cv1.ring(2, 512, "bf", "ysq")
        sgr = cv1.ring(2, 512, "f32", "sgr")
        agr = cv1.ring(2, 512, "f32", "agr")
        mu, t_mu = cv1.f32(512), Trk("mu")
        rs, t_rs = cv1.f32(512), Trk("rs")
        zt = cv1.ring(2, 512, "f32", "zt")
        co = [cv1.bf(4 * 512).rearrange("p (c t) -> p c t", c=4) for i in range(2)]
        t_co = [Trk("co%d" % i) for i in range(2)]
        wv = A["ev_w_in"][0].rearrange("(kc p) c -> p kc c", p=128)
        for i in range(2):
            s.dma("pool", lambda e, i=i: e.dma_start(out=wcv[:, :, i * 512:(i + 1) * 512], in_=wv[:, :, i * 512:(i + 1) * 512]), [t_wcv[i]], [], semkey="EW%d" % i)
        s.op("pool", lambda e: e.memset(aT[:, :, 0:15], 0.0), [], [t_apad])
        s.op("pool", lambda e: e.memset(aT[:, :, 15 + NT:NT + 32], 0.0), [], [t_apad])
        for cc in range(4):
            for j in range(31):
                s.op("dve", lambda e, cc=cc, j=j: e.tensor_scalar(out=diag[:, cc, j, :], in0=ident_b[:, :], scalar1=cwT[:, cc, j:j + 1], scalar2=None, op0=ALU.mult),
                     [t_const, t_ec], [t_diag])
        cn = {"i": 0}

        def glu(G):
            load_norm(G, "P")
            yield
            for cc in range(4):
                ba, bat = self.bank("P")
                bg, bgt = self.bank("P")
                for kc in range(KC):
                    s.op("pe", lambda e, kc=kc, cc=cc, ba=ba: e.matmul(ba[:, :], wcv[:, kc, cc * 128:(cc + 1) * 128], h[:, kc, :], start=(kc == 0), stop=(kc == KC - 1)),
                         [t_wcv[0], t_h], [bat], signal=(kc == KC - 1))
                for kc in range(KC):
                    s.op("pe", lambda e, kc=kc, cc=cc, bg=bg: e.matmul(bg[:, :], wcv[:, kc, 512 + cc * 128:512 + (cc + 1) * 128], h[:, kc, :], start=(kc == 0), stop=(kc == KC - 1)),
                         [t_wcv[1], t_h], [bgt], signal=(kc == KC - 1))
                i = cn["i"] % 2
                cn["i"] += 1
                sg_, t_sg = sgr[i]
                ag_, t_ag = agr[i]
                s.op("act", lambda e, bg=bg, sg_=sg_: e.activation(out=sg_, in_=bg[:, :], func=AF.Sigmoid), [bgt], [t_sg])
                s.op("dve", lambda e, ba=ba, sg_=sg_, ag_=ag_: e.tensor_tensor(out=ag_, in0=ba[:, :], in1=sg_, op=ALU.mult), [bat, t_sg], [t_ag])
                dst = aT[:, cc, 15 + G * 512: 15 + (G + 1) * 512]
                s.op("pool", lambda e, ag_=ag_, dst=dst: e.tensor_tensor(out=dst, in0=ag_, in1=tmask, op=ALU.mult), [t_ag, t_ec], [t_aT[G]])
                yield

        def conv(G):
            ci = G % 2
            for cc in range(4):
                bk, bt = self.bank("S")
                rd = [t_aT[g] for g in (G - 1, G, G + 1) if 0 <= g < 8] + [t_apad, t_diag]
                for j in range(31):
                    src = aT[:, cc, G * 512 + j: G * 512 + j + 512]
                    s.op("pe", lambda e, cc=cc, j=j, src=src, bk=bk: e.matmul(bk[:, :], diag[:, cc, j, :], src, start=(j == 0), stop=(j == 30)),
                         rd, [bt], signal=(j == 30))
                s.op("act", lambda e, cc=cc, bk=bk: e.activation(out=yb[:, cc, :], in_=bk[:, :], func=AF.Identity, bias=cvec[:, 0, cc:cc + 1], scale=1.0),
                     [bt, t_ec], [t_yb[cc]])
                yield
            b1, b1t = self.bank("S")
            b2, b2t = self.bank("S")
            for cc in range(4):
                yf, t_yf = ybf[cc % 2]
                yq, t_yq = ysq[cc % 2]
                s.op("dve", lambda e, cc=cc, yf=yf: e.tensor_copy(out=yf, in_=yb[:, cc, :]), [t_yb[cc]], [t_yf])
                s.op("act", lambda e, cc=cc, yq=yq: e.activation(out=yq, in_=yb[:, cc, :], func=AF.Square), [t_yb[cc]], [t_yq])
                s.op("pe", lambda e, cc=cc, yf=yf, b1=b1: e.matmul(b1[:, :], ones_b[:, :], yf, start=(cc == 0), stop=(cc == 3)), [t_yf, t_const], [b1t])
                s.op("pe", lambda e, cc=cc, yq=yq, b2=b2: e.matmul(b2[:, :], ones_b[:, :], yq, start=(cc == 0), stop=(cc == 3)), [t_yq, t_const], [b2t])
            s.op("act", lambda e, b1=b1: e.activation(out=mu, in_=b1[:, :], func=AF.Copy, scale=1.0 / 512.0), [b1t], [t_mu])
            s.op("dve", lambda e, b1=b1: e.tensor_tensor(out=rs, in0=b1[:, :], in1=mu, op=ALU.mult), [b1t, t_mu], [t_rs])
            s.op("dve", lambda e, b2=b2: e.tensor_tensor(out=rs, in0=b2[:, :], in1=rs, op=ALU.subtract), [b2t, t_rs], [t_rs])
            s.op("act", lambda e: e.activation(out=rs, in_=rs, func=AF.Ln, bias=eps5[:, 0:1], scale=1.0), [t_rs, t_ec], [t_rs])
            s.op("act", lambda e: e.activation(out=rs, in_=rs, func=AF.Exp, scale=-0.5), [t_rs], [t_rs])
            yield
            for cc in range(4):
                z_, t_z = zt[cc % 2]
                s.op("dve", lambda e, cc=cc, z_=z_: e.tensor_tensor(out=z_, in0=yb[:, cc, :], in1=mu, op=ALU.subtract), [t_yb[cc], t_mu], [t_z])
                s.op("pool", lambda e, z_=z_: e.tensor_tensor(out=z_, in0=z_, in1=rs, op=ALU.mult), [t_z, t_rs], [t_z])
                s.op("act", lambda e, cc=cc, z_=z_, ci=ci: e.activation(out=co[ci][:, cc, :], in_=z_, func=AF.Silu, bias=cvec[:, 2, cc:cc + 1], scale=self.lng_s[:, cc:cc + 1]),
                     [t_z, t_ec], [t_co[ci]])
                yield
            s.dma("sp", lambda e, ci=ci: e.dma_start(out=cvs[:, :, G * 512:(G + 1) * 512], in_=co[ci]), [t_cvs[G]], [t_co[ci]], semkey="ECo%d" % ci)

        self.lng_s = cv1.f32(4)
        s.op("dve", lambda e: e.tensor_scalar(out=self.lng_s, in0=cvec[:, 1, :], scalar1=float(np.sqrt(512.0)), scalar2=None, op0=ALU.mult), [t_ec], [t_ec])
        if self.mode == "even_a":
            return
        for G in range(10):
            gg = (lambda G=G: glu(G)) if G < 8 else None
            gc = (lambda G=G: conv(G - 2)) if (G >= 2 and self.mode != "even_b") else None
            self.interleave([gg() if gg else None, gc() if gc else None])
        if self.mode in ("even_b", "even_c"):
            return

        s.barrier()
        cv2 = Carve(self.arena)
        cv2.off = base
        whg = cv2.bf(KC * 2048).rearrange("p (k c) -> p k c", k=KC)
        t_whg = [Trk("whg%d" % i) for i in range(4)]
        S_, t_S = cv2.f32(512).rearrange("p (h v) -> p h v", h=4), Trk("S")
        Sb, t_Sb = cv2.bf(512).rearrange("p (h v) -> p h v", h=4), Trk("Sb")
        qs, t_qs = cv2.f32(512), Trk("qs")
        ff, t_ff = cv2.f32(512), Trk("ff")
        gl, t_gl = cv2.f32(512), Trk("gl")
        kk, t_kk = cv2.f32(512), Trk("kk")
        er = cv2.ring(2, 512, "f32", "er")
        qh_t, t_qh = cv2.bf(512), Trk("qh_t")
        qt_t, t_qt = cv2.bf(512), Trk("qt_t")
        kt_t, t_kt = cv2.bf(512), Trk("kt_t")
        qtT, t_qtT = cv2.bf(512).rearrange("p (h t) -> p h t", h=4), Trk("qtT")
        ktT, t_ktT = cv2.bf(512).rearrange("p (h t) -> p h t", h=4), Trk("ktT")
        P2 = []
        for i in range(2):
            P2.append({
                "qhT": (cv2.bf(512).rearrange("p (h t) -> p h t", h=4), Trk("qhT%d" % i)),
                "scT": (cv2.bf(256), Trk("scT%d" % i)),
                "v": (cv2.bf(512), Trk("v%d" % i)),
                "kh": (cv2.bf(512), Trk("kh%d" % i)),
                "dec": (cv2.f32(8), Trk("dec%d" % i)),
                "gs": (cv2.f32(512), Trk("gs%d" % i)),
                "ob": (cv2.f32(512), Trk("ob%d" % i)),
                "ost": (cv2.f32(512), Trk("ost%d" % i)),
            })
        osum, t_osum = cv2.f32(512), Trk("osum")
        osq, t_osq = cv2.f32(512), Trk("osq")
        oss, t_oss = cv2.f32(4), Trk("oss")
        r_b, t_rb = cv2.bf(512), Trk("r_b")
        rT = cv2.bf(4 * 512).rearrange("p (c t) -> p c t", c=4)
        t_rT = [Trk("rT%d" % i) for i in range(4)]
        cvl, t_cvl = cv2.bf(4 * 512).rearrange("p (c t) -> p c t", c=4), Trk("cvl")
        xc = cv2.ring(2, 512, "f32", "exc")
        xo = cv2.ring(2, 512, "f32", "exo")
        cnx = {"i": 0}
        wo_v = A["ev_w_out"][0].rearrange("(c8 p) d -> p c8 d", p=128)

        def prep(n, d_, P, fwd_final):
            G, p4 = n // 4, n % 4
            if (d_ == 0 and p4 == 0) or (d_ == 1 and p4 == 3):
                load_norm(G, "P")
                yield
            ncomp = 4 if fwd_final else 3
            bks = []
            for c in range(ncomp):
                bk, bt = self.bank("P")
                for kc in range(KC):
                    s.op("pe", lambda e, kc=kc, c=c, bk=bk: e.matmul(bk[:, :], h[:, kc, p4 * 128:(p4 + 1) * 128], whg[:, kc, c * 512:(c + 1) * 512],
                                                                     start=(kc == 0), stop=(kc == KC - 1)), [t_h, t_whg[c]], [bt], signal=(kc == KC - 1))
                bks.append((bk, bt))
            (bq, bqt), (bz, bzt), (bv, bvt) = bks[0], bks[1], bks[2]
            yield
            s.op("act", lambda e: e.activation(out=qs, in_=bq[:, :], func=AF.Silu), [bqt], [t_qs])
            s.op("act", lambda e: e.activation(out=ff, in_=bz[:, :], func=AF.Sigmoid), [bzt], [t_ff])
            vb, t_vb = P["v"]
            s.op("act", lambda e: e.activation(out=vb, in_=bv[:, :], func=AF.Copy), [bvt], [t_vb])
            if fwd_final:
                gs, t_gs = P["gs"]
                bg, bgt = bks[3]
                s.op("act", lambda e: e.activation(out=gs, in_=bg[:, :], func=AF.Silu), [bgt], [t_gs])
            yield
            s.op("dve", lambda e: e.tensor_tensor(out=ff, in0=ff, in1=oml[d_], op=ALU.mult), [t_ff, t_ec], [t_ff])
            s.op("dve", lambda e: e.tensor_tensor(out=ff, in0=ff, in1=lbt[d_], op=ALU.add), [t_ff, t_ec], [t_ff])
            yield
            s.op("act", lambda e: e.activation(out=gl, in_=ff, func=AF.Ln), [t_ff], [t_gl])
            s.op("dve", lambda e: e.tensor_scalar(out=gl, in0=gl, scalar1=tokm[:, p4:p4 + 1], scalar2=None, op0=ALU.mult), [t_gl, t_ec], [t_gl])
            s.op("pool", lambda e: e.tensor_scalar(out=kk, in0=ff, scalar1=-1.0, scalar2=1.0, op0=ALU.mult, op1=ALU.add), [t_ff], [t_kk])
            yield
            if self.mode == "even_e1":
                return
            bb = []
            for m in range(3):
                bk, bt = self.bank("P")
                s.op("pe", lambda e, m=m, bk=bk: e.matmul(bk[:, :], cumM[:, d_, m, :], gl, start=True, stop=True), [t_gl, t_ec], [bt])
                bb.append((bk, bt))
                yield
            be, bet = self.bank("P")
            s.op("pe", lambda e, be=be: e.matmul(be[:, :], cumM[:, d_, 3, :], gl, start=True, stop=True), [t_gl, t_ec], [bet])
            ee, t_ee = er[0]
            s.op("act", lambda e, be=be, ee=ee: e.activation(out=ee, in_=be[:, :], func=AF.Exp), [bet], [t_ee])
            yield
            bd, bdt = self.bank("P")
            for hh in range(4):
                s.op("pe", lambda e, hh=hh, bd=bd, ee=ee: e.transpose(out=bd[:, hh * 128:(hh + 1) * 128], in_=ee[:, hh * 128:(hh + 1) * 128], identity=self.ident_f[:]),
                     [t_ee, t_const], [bdt], signal=(hh == 3))
            dec, t_dec = P["dec"]
            s.op("dve", lambda e, bd=bd, dec=dec: e.tensor_copy(out=dec.rearrange("p (h c) -> p h c", h=4),
                                                               in_=bd[:, :].rearrange("p (h c t) -> p h c t", h=4, c=2)[:, :, :, 0]), [bdt], [t_dec])
            yield
            kh_, t_kh = P["kh"]
            specs = [(0, 1.0, qs, t_qs, qh_t, t_qh), (1, 1.0, qs, t_qs, qt_t, t_qt), (1, -1.0, kk, t_kk, kt_t, t_kt), (2, 1.0, kk, t_kk, kh_, t_kh)]
            for i, (m, sc_, src, t_src, dst, t_dst) in enumerate(specs):
                e_, t_e = er[i % 2]
                bk, bt = bb[m]
                s.op("act", lambda e, e_=e_, bk=bk, sc_=sc_: e.activation(out=e_, in_=bk[:, :], func=AF.Exp, scale=sc_), [bt], [t_e])
                eng = "dve" if i % 2 == 0 else "pool"
                s.op(eng, lambda e, e_=e_, src=src, dst=dst: e.tensor_tensor(out=dst, in0=src, in1=e_, op=ALU.mult), [t_e, t_src], [t_dst])
                yield
            if self.mode in ("even_e2", "even_e2a"):
                return
            bA, bAt = self.bank("P")
            bAb = bA[:, :].bitcast(BF16)
            bB, bBt = self.bank("P")
            bBb = bB[:, :].bitcast(BF16)
            for hh in range(4):
                s.op("pe", lambda e, hh=hh: e.transpose(out=bAb[:, hh * 128:(hh + 1) * 128], in_=qh_t[:, hh * 128:(hh + 1) * 128], identity=ident_b[:]),
                     [t_qh, t_const], [bAt], signal=False)
            for hh in range(4):
                s.op("pe", lambda e, hh=hh: e.transpose(out=bAb[:, 512 + hh * 128:512 + (hh + 1) * 128], in_=qt_t[:, hh * 128:(hh + 1) * 128], identity=ident_b[:]),
                     [t_qt, t_const], [bAt], signal=(hh == 3))
            for hh in range(4):
                s.op("pe", lambda e, hh=hh: e.transpose(out=bBb[:, hh * 128:(hh + 1) * 128], in_=kt_t[:, hh * 128:(hh + 1) * 128], identity=ident_b[:]),
                     [t_kt, t_const], [bBt], signal=(hh == 3))
            qhT, t_qhT = P["qhT"]
            yield
            s.op("act", lambda e: e.activation(out=qhT.rearrange("p h t -> p (h t)"), in_=bAb[:, 0:512], func=AF.Copy), [bAt], [t_qhT])
            s.op("dve", lambda e: e.tensor_copy(out=qtT.rearrange("p h t -> p (h t)"), in_=bAb[:, 512:1024]), [bAt], [t_qtT])
            s.op("act", lambda e: e.activation(out=ktT.rearrange("p h t -> p (h t)"), in_=bBb[:, 0:512], func=AF.Copy), [bBt], [t_ktT])
            yield
            if self.mode == "even_e3":
                return
            bs, bst = self.bank("P")
            for c in range(2):
                for hh in range(4):
                    last = (c == 1 and hh == 3)
                    s.op("pe", lambda e, c=c, hh=hh: e.matmul(bs[c * 64:(c + 1) * 64, hh * 64:(hh + 1) * 64], ktT[:, hh, c * 64:(c + 1) * 64],
                                                              qtT[:, hh, c * 64:(c + 1) * 64], start=True, stop=True),
                         [t_ktT, t_qtT], [bst], signal=last)
            scT, t_scT = P["scT"]
            s.op("dve", lambda e: e.tensor_tensor(out=scT.rearrange("p (h t) -> p h t", h=4), in0=bs[:, 0:256].rearrange("p (h t) -> p h t", h=4),
                                                  in1=scm[:, d_, :].unsqueeze(1).to_broadcast([128, 4, 64]), op=ALU.mult), [bst, t_ec], [t_scT])

        def seq(n, d_, P, fwd_final):
            G, p4 = n // 4, n % 4
            qhT, t_qhT = P["qhT"]
            scT, t_scT = P["scT"]
            vb, t_vb = P["v"]
            kh_, t_kh = P["kh"]
            dec, t_dec = P["dec"]
            bo, bot = self.bank("S")
            chunks = (0, 1) if d_ == 0 else (1, 0)
            for c in chunks:
                cg = n * 2 + c
                if (d_ == 0 and cg % 8 == 0) or (d_ == 1 and cg % 8 == 3):
                    s.op("dve", lambda e: e.tensor_scalar(out=S_.rearrange("p h v -> p (h v)"), in0=S_.rearrange("p h v -> p (h v)"), scalar1=cflag[:, 0:1],
                                                          scalar2=None, op0=ALU.mult), [t_S, t_ec], [t_S])
                    s.op("act", lambda e: e.activation(out=Sb.rearrange("p h v -> p (h v)"), in_=S_.rearrange("p h v -> p (h v)"), func=AF.Copy), [t_S], [t_Sb])
                for hh in range(4):
                    ov = bo[c * 64:(c + 1) * 64, hh * 128:(hh + 1) * 128]
                    s.op("pe", lambda e, c=c, hh=hh, ov=ov: e.matmul(ov, qhT[:, hh, c * 64:(c + 1) * 64], Sb[:, hh, :], start=True, stop=False),
                         [t_qhT, t_Sb], [bot], signal=False)
                    last = (hh == 3 and c == chunks[1])
                    s.op("pe", lambda e, c=c, hh=hh, ov=ov: e.matmul(ov, scT[c * 64:(c + 1) * 64, hh * 64:(hh + 1) * 64], vb[c * 64:(c + 1) * 64, hh * 128:(hh + 1) * 128],
                                                                    start=False, stop=True), [t_scT, t_vb], [bot], signal=(hh == 3))
                yield
                bkv, bkvt = self.bank("S")
                for hh in range(4):
                    s.op("pe", lambda e, c=c, hh=hh, bkv=bkv: e.matmul(bkv[:, hh * 128:(hh + 1) * 128], kh_[c * 64:(c + 1) * 64, hh * 128:(hh + 1) * 128],
                                                                      vb[c * 64:(c + 1) * 64, hh * 128:(hh + 1) * 128], start=True, stop=True),
                         [t_kh, t_vb], [bkvt], signal=(hh == 3))
                yield
                s.op("dve", lambda e, c=c: e.tensor_tensor(out=S_, in0=S_, in1=dec[:, c:8:2].unsqueeze(2).to_broadcast([128, 4, 128]), op=ALU.mult),
                     [t_S, t_dec], [t_S])
                s.op("dve", lambda e, bkv=bkv: e.tensor_tensor(out=S_.rearrange("p h v -> p (h v)"), in0=bkv[:, :], in1=S_.rearrange("p h v -> p (h v)"), op=ALU.add),
                     [t_S, bkvt], [t_S])
                s.op("act", lambda e: e.activation(out=Sb.rearrange("p h v -> p (h v)"), in_=S_.rearrange("p h v -> p (h v)"), func=AF.Copy), [t_S], [t_Sb])
                if (d_ == 0 and cg % 8 == 3) or (d_ == 1 and cg % 8 == 0):
                    slot = cg // 8
                    s.dma("sp", lambda e, slot=slot: e.dma_start(out=A["hs_out"][slot, d_].rearrange("h k v -> k h v"), in_=S_), [self.t_hs], [t_S], semkey="EHS")
            if not fwd_final:
                ost, t_ost = P["ost"]
                s.op("act", lambda e: e.activation(out=ost, in_=bo[:, :], func=AF.Copy), [bot], [t_ost])
                s.dma("sp", lambda e: e.dma_start(out=obs[n], in_=ost), [t_obs[n]], [t_ost], semkey="EOst%d" % (n % 2))
                yield
                return
            ob, t_ob = P["ob"]
            gs, t_gs = P["gs"]
            s.op("dve", lambda e: e.tensor_tensor(out=osum, in0=bo[:, :], in1=ob, op=ALU.add), [bot, t_ob], [t_osum])
            yield
            s.op("act", lambda e: e.activation(out=osq, in_=osum, func=AF.Square), [t_osum], [t_osq])
            s.op("dve", lambda e: e.tensor_reduce(out=oss, in_=osq.rearrange("p (h v) -> p h v", h=4), axis=AX.X, op=ALU.add), [t_osq], [t_oss])
            s.op("act", lambda e: e.activation(out=oss, in_=oss, func=AF.Ln, bias=epsq[:, 0:1], scale=1.0 / 128.0), [t_oss, t_ec], [t_oss])
            s.op("act", lambda e: e.activation(out=oss, in_=oss, func=AF.Exp, scale=-0.5), [t_oss], [t_oss])
            yield
            s.op("dve", lambda e: e.tensor_tensor(out=osum.rearrange("p (h v) -> p h v", h=4), in0=osum.rearrange("p (h v) -> p h v", h=4),
                                                  in1=oss.unsqueeze(2).to_broadcast([128, 4, 128]), op=ALU.mult), [t_osum, t_oss], [t_osum])
            s.op("pool", lambda e: e.tensor_tensor(out=osum.rearrange("p (h v) -> p h v", h=4), in0=osum.rearrange("p (h v) -> p h v", h=4),
                                                   in1=hgn.unsqueeze(1).to_broadcast([128, 4, 128]), op=ALU.mult), [t_osum, t_ec], [t_osum])
            s.op("dve", lambda e: e.tensor_tensor(out=r_b, in0=osum, in1=gs, op=ALU.mult), [t_osum, t_gs], [t_rb])
            yield
            bT, bTt = self.bank("S")
            bTb = bT[:, :].bitcast(BF16)
            for hh in range(4):
                s.op("pe", lambda e, hh=hh: e.transpose(out=bTb[:, hh * 128:(hh + 1) * 128], in_=r_b[:, hh * 128:(hh + 1) * 128], identity=ident_b[:]),
                     [t_rb, t_const], [bTt], signal=(hh == 3))
            s.op("act", lambda e: e.activation(out=rT[:, :, p4 * 128:(p4 + 1) * 128], in_=bTb[:, 0:512].rearrange("p (c t) -> p c t", c=4), func=AF.Copy),
                 [bTt], [t_rT[p4]])

        def outproj_steps(G):
            def ldc(a, b):
                s.dma("sp", lambda e: e.dma_start(out=cvl, in_=cvs[:, :, G * 512:(G + 1) * 512]), [t_cvl], [t_cvs[G]], semkey="ECl")
            self.step(None, ldc)
            for dc in range(KC):
                def wl(e, st, dc=dc):
                    return e.dma_start(out=st[:, 0:1024].rearrange("p (c8 c) -> p c8 c", c8=8), in_=wo_v[:, :, dc * 128:(dc + 1) * 128])

                def body(st, strk, dc=dc):
                    wv_ = st[:, 0:1024].rearrange("p (c8 c) -> p c8 c", c8=8)
                    ci = cnx["i"] % 2
                    cnx["i"] += 1
                    xct, t_xc = xc[ci]
                    xot, t_xo = xo[ci]
                    s.dma("sp", lambda e: e.dma_start(out=xct, in_=xs_v[:, dc, G * 512:(G + 1) * 512]), [t_xc], [self.xs_trk[dc][G]], semkey="EXc%d" % ci)
                    bk, bt = self.bank()
                    for c8 in range(8):
                        rhs = cvl[:, c8, :] if c8 < 4 else rT[:, c8 - 4, :]
                        s.op("pe", lambda e, c8=c8, bk=bk, rhs=rhs: e.matmul(bk[:, :], wv_[:, c8, :], rhs, start=(c8 == 0), stop=(c8 == 7)),
                             list(strk) + list(t_rT) + [t_cvl], [bt], signal=(c8 == 7))
                    s.op("dve", lambda e, bk=bk, mg=self.modG[:, sl, dc:dc + 1]: e.scalar_tensor_tensor(out=xot, in0=bk[:, :], scalar=mg, in1=xct,
                                                                        op0=ALU.mult, op1=ALU.add), [bt, t_xc, self.t_modAG], [t_xo])
                    s.dma("sp", lambda e: e.dma_start(out=xs_v[:, dc, G * 512:(G + 1) * 512], in_=xot), [self.xs_trk[dc][G]], [t_xo], semkey="EXo%d" % ci)
                self.step([wl], body)

        for d_ in (1, 0):
            fwd_final = (d_ == 0)
            if fwd_final and (self.mode == "even_d" or self.mode.startswith("even_e")):
                return
            s.barrier()
            cols = [1024, 1536 + 512 * d_, 2560, 3072]
            for c in range(4 if fwd_final else 3):
                s.dma("pool", lambda e, c=c, cols=cols: e.dma_start(out=whg[:, :, c * 512:(c + 1) * 512], in_=wv[:, :, cols[c]:cols[c] + 512]), [t_whg[c]], [], semkey="EWh%d" % c)
            s.dma("sp", lambda e, d_=d_: e.dma_start(out=S_, in_=A["s0"][d_].rearrange("h k v -> k h v")), [t_S], [], semkey="ES0")
            s.op("act", lambda e: e.activation(out=Sb.rearrange("p h v -> p (h v)"), in_=S_.rearrange("p h v -> p (h v)"), func=AF.Copy), [t_S], [t_Sb])
            order = list(range(NTILE)) if d_ == 0 else list(range(NTILE - 1, -1, -1))
            for i, n in enumerate(order + [None]):
                if n is not None:
                    P = P2[i % 2]
                    if fwd_final:
                        ob, t_ob = P["ob"]
                        self.step(None, lambda a, b, n=n, ob=ob, t_ob=t_ob, i=i: s.dma(
                            "sp", lambda e: e.dma_start(out=ob, in_=obs[n]), [t_ob], [t_obs[n]], semkey="EOb%d" % (i % 2)))
                gp = (lambda n=n, P=P: prep(n, d_, P, fwd_final)) if n is not None else None
                gs_ = None
                if i >= 1 and not self.mode.startswith("even_e"):
                    pn = order[i - 1]
                    gs_ = (lambda pn=pn, Pp=P2[(i - 1) % 2]: seq(pn, d_, Pp, fwd_final))
                self.step(None, lambda a, b, gp=gp, gs_=gs_: self.interleave([gp() if gp else None, gs_() if gs_ else None]))
                if d_ == 1 and self.pending_mod is not None and self.pending_mod[0] and i % 2 == 1:
                    self.steps.append(self.pending_mod[0].pop(0))
                if i >= 1 and not self.mode.startswith("even_e"):
                    if fwd_final and pn % 4 == 3:
                        outproj_steps(pn // 4)
            if d_ == 1 and self.pending_mod is not None:
                self.steps.extend(self.pending_mod[0])
                self.run_steps()
                self.pending_mod[1]()
                self.pending_mod = None
            self.run_steps()

    def phase_ffn(self, l, j, ffn_w_in, ffn_w_out, xs_v, ones_b, t_const, t1_out=None):
        nc, s = self.nc, self.s
        sl = 0 if j == 0 else 2
        TS = 1024
        NTT = TS // 512
        s.barrier()
        cv = Carve(self.arena)
        hT = cv.bf(KC * TS).rearrange("p (k t) -> p k t", k=KC)
        hT_trk = [Trk("hT%d" % i) for i in range(NTT)]
        hid = cv.bf(FC * TS).rearrange("p (f t) -> p f t", f=FC)
        hid_trk = [[Trk("hid%d_%d" % (f, i)) for i in range(NTT)] for f in range(FC)]
        xin = [cv.f32(KC * 512).rearrange("p (k t) -> p k t", k=KC) for i in range(2)]
        xin_trk = [Trk("xin%d" % i) for i in range(2)]
        nr = {"sq": cv.ring(2, 512, "bf", "sq"), "tmp": cv.ring(2, 512, "f32", "tmp"), "rstd": (cv.f32(512), Trk("rstd"))}
        sg = [cv.bf(512) for i in range(3)]
        sg_trk = [Trk("sg%d" % i) for i in range(3)]
        xc = [cv.f32(512) for i in range(3)]
        xc_trk = [Trk("xc%d" % i) for i in range(3)]
        xo = [cv.f32(512) for i in range(3)]
        xo_trk = [Trk("xo%d" % i) for i in range(3)]
        wi = ffn_w_in[l, j].rearrange("(kc p) c -> p kc c", p=128)
        wo = ffn_w_out[l, j].rearrange("(fc p) d -> p fc d", p=128)
        cnt = {"sg": 0, "xc": 0}

        NST = NT // TS
        seqN, seqA, seqB = [], [[] for _ in range(NST)], [[] for _ in range(NST)]
        for st_i in range(NT // TS):
            t0 = st_i * TS

            def norm_body(_a, _b, st_i=st_i, t0=t0):
                def load(tt):
                    g = (t0 // 512 + tt)
                    i = g % 2
                    s.dma("sp", lambda e: e.dma_start(out=xin[i], in_=xs_v[:, :, g * 512:(g + 1) * 512]), [xin_trk[i]],
                          [self.xs_trk[kc][g] for kc in range(KC)], semkey="FXin%d" % i)
                load(0)
                for tt in range(NTT):
                    if tt + 1 < NTT:
                        load(tt + 1)
                    g = (t0 // 512 + tt)
                    i = g % 2
                    self.norm_tile(xin[i], xin_trk[i], sl, hT[:, :, tt * 512:(tt + 1) * 512], hT_trk[tt], nr)
            seqN.append((None, norm_body))

            for jb in range(11):
                def wl_g(e, st, jb=jb):
                    return e.dma_start(out=st[:, 0:2048].rearrange("p (k c) -> p k c", k=KC), in_=wi[:, :, jb * 256:(jb + 1) * 256])

                def wl_u(e, st, jb=jb):
                    return e.dma_start(out=st[:, 2048:4096].rearrange("p (k c) -> p k c", k=KC),
                                       in_=wi[:, :, DFF + jb * 256: DFF + (jb + 1) * 256])

                def bodyA(st, strk, jb=jb):
                    wg = st[:, 0:2048].rearrange("p (k c) -> p k c", k=KC)
                    wu = st[:, 2048:4096].rearrange("p (k c) -> p k c", k=KC)
                    for fs in range(2):
                        f = jb * 2 + fs
                        for tt in range(NTT):
                            bg, bgt = self.bank()
                            bu, but = self.bank()
                            for kc in range(KC):
                                s.op("pe", lambda e, kc=kc, fs=fs, tt=tt, bg=bg: e.matmul(
                                    bg[:, :], wg[:, kc, fs * 128:(fs + 1) * 128], hT[:, kc, tt * 512:(tt + 1) * 512],
                                    start=(kc == 0), stop=(kc == KC - 1)), list(strk) + [hT_trk[tt]], [bgt], signal=(kc == KC - 1))
                            for kc in range(KC):
                                s.op("pe", lambda e, kc=kc, fs=fs, tt=tt, bu=bu: e.matmul(
                                    bu[:, :], wu[:, kc, fs * 128:(fs + 1) * 128], hT[:, kc, tt * 512:(tt + 1) * 512],
                                    start=(kc == 0), stop=(kc == KC - 1)), list(strk) + [hT_trk[tt]], [but], signal=(kc == KC - 1))
                            si = cnt["sg"] % 3
                            cnt["sg"] += 1
                            s.op("act", lambda e, si=si, bg=bg: e.activation(out=sg[si], in_=bg[:, :], func=AF.Silu), [bgt], [sg_trk[si]])
                            s.op("dve", lambda e, si=si, bu=bu, f=f, tt=tt: e.tensor_tensor(
                                out=hid[:, f, tt * 512:(tt + 1) * 512], in0=bu[:, :], in1=sg[si], op=ALU.mult),
                                [but, sg_trk[si]], [hid_trk[f][tt]])
                seqA[st_i].append(([wl_g, wl_u], bodyA))

            for dc in range(KC):
                def wl_o(e, st, dc=dc):
                    return e.dma_start(out=st[:, 0:FC * 128].rearrange("p (f c) -> p f c", f=FC), in_=wo[:, :, dc * 128:(dc + 1) * 128])

                def bodyB(st, strk, dc=dc, t0=t0):
                    wv = st[:, 0:FC * 128].rearrange("p (f c) -> p f c", f=FC)
                    for tt in range(NTT):
                        g = t0 // 512 + tt
                        ci = cnt["xc"] % 3
                        cnt["xc"] += 1
                        s.dma("sp", lambda e, ci=ci, g=g: e.dma_start(out=xc[ci], in_=xs_v[:, dc, g * 512:(g + 1) * 512]),
                              [xc_trk[ci]], [self.xs_trk[dc][g]], semkey="FXc%d" % ci)
                        bk, bt = self.bank()
                        for f in range(FC):
                            s.op("pe", lambda e, f=f, tt=tt, bk=bk: e.matmul(
                                bk[:, :], wv[:, f, :], hid[:, f, tt * 512:(tt + 1) * 512], start=(f == 0), stop=(f == FC - 1)),
                                list(strk) + [hid_trk[f][tt]], [bt], signal=(f == FC - 1))
                        s.op("dve", lambda e, ci=ci, bk=bk, mg=self.modG[:, sl, dc:dc + 1]: e.scalar_tensor_tensor(
                            out=xo[ci], in0=bk[:, :], scalar=mg, in1=xc[ci], op0=ALU.mult, op1=ALU.add),
                            [bt, xc_trk[ci], self.t_modAG], [xo_trk[ci]])
                        s.dma("sp", lambda e, ci=ci, g=g: e.dma_start(out=xs_v[:, dc, g * 512:(g + 1) * 512], in_=xo[ci]),
                              [self.xs_trk[dc][g]], [xo_trk[ci]], semkey="FXo%d" % ci)
                seqB[st_i].append(([wl_o], bodyB))
        t1l = t1c = None
        if t1_out is not None:
            t1l, t1c = self.t1_setup(xs_v, t1_out, cv.off)
        gps = TS // 512
        order = [seqN[0]] + seqA[0]
        for st_i in range(NST):
            if st_i + 1 < NST:
                order.append(seqN[st_i + 1])
            order += seqB[st_i]
            if t1l is not None:
                order.append((None, lambda a, b, st_i=st_i: [t1l(st_i * gps + g) for g in range(gps)]))
            if st_i + 1 < NST:
                nxt = list(seqA[st_i + 1])
                if t1c is not None:
                    for g in range(gps):
                        pos = min(len(nxt), 3 + 4 * g + g)
                        nxt.insert(pos, (None, lambda a, b, tt=st_i * gps + g: t1c(tt)))
                order += nxt
            elif t1c is not None:
                for g in range(gps):
                    order.append((None, lambda a, b, tt=st_i * gps + g: t1c(tt)))
        self.steps.extend(order)
        self.run_steps()


_CACHE = {}


def _get_prog(mode):
    if mode not in _CACHE:
        p = Prog(mode)
        p.build()
        _CACHE[mode] = p
    return _CACHE[mode]


NEG = -30000.0


def _attn_masks():
    key = np.arange(128)[:, None]
    q = np.arange(128)[None, :]
    vis = np.zeros((128, 128), np.float32)
    hid = np.full((128, 128), NEG, np.float32)
    prev_band = np.where(key >= q, 0.0, NEG).astype(np.float32)
    next_band = np.where(key <= q, 0.0, NEG).astype(np.float32)
    am_s = np.zeros((128, 6, 2, 128), np.float32)
    am_p = np.zeros((128, 6, 2, 128), np.float32)
    for v in range(6):
        am_s[:, v, 0] = hid if v == 0 else prev_band
        am_s[:, v, 1] = hid if v == 5 else next_band
        am_p[:, v, 0] = vis if v == 2 else hid
        am_p[:, v, 1] = vis if v in (0, 1) else hid
    return np.ascontiguousarray(am_s.reshape(128, 1536)), np.ascontiguousarray(am_p.reshape(128, 1536))


def _scan_consts():
    sidx = np.arange(128)[:, None]
    tidx = np.arange(128)[None, :]
    same = (sidx // 64) == (tidx // 64)
    ls = sidx % 64
    cm = np.zeros((128, 2, 4, 128), np.float32)
    cm[:, 0, 3] = same
    cm[:, 1, 3] = same
    sm = np.zeros((128, 2, 64), np.float32)
    mb = same & (sidx <= tidx)
    sel = same & (ls <= 31)
    cm[:, 0, 0] = mb
    cm[:, 0, 1] = mb.astype(np.float32) - sel.astype(np.float32)
    cm[:, 0, 2] = same & (sidx > tidx)
    mb = same & (sidx >= tidx)
    sel = same & (ls >= 32)
    cm[:, 1, 0] = mb
    cm[:, 1, 1] = mb.astype(np.float32) - sel.astype(np.float32)
    cm[:, 1, 2] = same & (sidx < tidx)
    s_loc = (np.arange(128) % 64)[:, None]
    t_loc = np.arange(64)[None, :]
    sm[:, 0] = (s_loc <= t_loc)
    sm[:, 1] = (s_loc >= t_loc)
    return np.ascontiguousarray(cm.reshape(128, 1024)), np.ascontiguousarray(sm.reshape(128, 128))


def _rope_tables():
    t = np.arange(NT)
    row = (t // 64).astype(np.float32)
    col = (t % 64).astype(np.float32)
    inv = (10000.0 ** (-np.arange(0, 32, 2, dtype=np.float32) / 32.0)).astype(np.float32)
    ar = (row[:, None] * inv[None, :]).astype(np.float32)
    ac = (col[:, None] * inv[None, :]).astype(np.float32)
    C = np.concatenate([np.cos(ar), np.cos(ar), np.cos(ac), np.cos(ac)], axis=1).astype(np.float32)
    S = np.concatenate([-np.sin(ar), np.sin(ar), -np.sin(ac), np.sin(ac)], axis=1).astype(np.float32)
    return np.ascontiguousarray(C), np.ascontiguousarray(S)


def make_in_maps(inputs, mode="full"):
    f32 = np.float32
    xp = np.asarray(inputs["x_prompt"], f32)
    xsm = np.asarray(inputs["x_sample"], f32)
    c = np.asarray(inputs["c"], f32)
    c_ctx = np.asarray(inputs["c_ctx"], f32)
    ident = np.eye(128, dtype=f32)
    b_mod = np.asarray(inputs["b_mod"], f32)
    bmodT = np.ascontiguousarray(b_mod.reshape(2, 72, 128).transpose(0, 2, 1))
    norm_g = np.asarray(inputs["norm_g"], f32)
    normgT = np.ascontiguousarray(norm_g.reshape(2, 3, KC, 128).transpose(0, 1, 3, 2))
    shared = {
        "ident": ident,
        "w_mod": np.asarray(inputs["w_mod"], f32),
        "bmodT": bmodT,
        "normgT": normgT,
    }
    if mode in ("full", "ffn1", "l0"):
        shared["ffn_w_in"] = np.asarray(inputs["ffn_w_in"], f32)
        shared["ffn_w_out"] = np.asarray(inputs["ffn_w_out"], f32)
    if mode in ("full", "attn"):
        shared["od_w_in"] = np.asarray(inputs["od_w_in"], f32)
        shared["od_w_out"] = np.asarray(inputs["od_w_out"], f32)
        shared["qkgain"] = np.ascontiguousarray(np.broadcast_to(
            np.concatenate([np.asarray(inputs["q_norm_g"], f32)[0], np.asarray(inputs["k_norm_g"], f32)[0]])[None, :], (128, 128)))
        shared["sinks"] = np.asarray(inputs["sinks"], f32).reshape(1, 16)
        am_s, am_p = _attn_masks()
        rc, rs = _rope_tables()
    if mode in ("full", "l0") or mode.startswith("even"):
        shared["ev_w_in"] = np.asarray(inputs["ev_w_in"], f32)
        shared["ev_w_out"] = np.asarray(inputs["ev_w_out"], f32)
        cw = np.asarray(inputs["conv_w"], f32)[0]
        shared["convwT"] = np.ascontiguousarray(cw.reshape(31, 4, 128).transpose(2, 1, 0).reshape(128, 124))
        vecs = np.stack([np.asarray(inputs[k], f32)[0] for k in ("conv_b", "conv_ln_g", "conv_ln_b")])
        shared["convvec"] = np.ascontiguousarray(vecs.reshape(3, 4, 128).transpose(2, 0, 1).reshape(128, 12))
        shared["lbraw"] = np.ascontiguousarray(np.asarray(inputs["hg_lb_raw"], f32).reshape(2, 1536))
        shared["hgn"] = np.ascontiguousarray(np.broadcast_to(np.asarray(inputs["hg_norm_g"], f32)[0][None, :], (128, 128)))
        cm, sm = _scan_consts()
        shared["cumM"] = cm
        shared["scmask"] = sm
    maps = []
    for core in range(NCORES):
        m = dict(shared)
        if mode in ("full", "l0") or mode.startswith("even"):
            if core < 4:
                m["tmask"] = np.ones((128, 512), f32)
                m["cflag"] = np.ones((128, 1), f32)
                m["tokm"] = np.ones((128, 4), f32)
                m["s0"] = np.ascontiguousarray(np.asarray(inputs["state_hgrn"], f32)[core, 0])
            else:
                tm = np.zeros((128, 512), f32)
                tm[:, :256] = 1.0
                m["tmask"] = tm
                m["cflag"] = np.zeros((128, 1), f32)
                tk = np.zeros((128, 4), f32)
                tk[:, :2] = 1.0
                m["tokm"] = tk
                m["s0"] = np.zeros((2, 4, 128, 128), f32)
        if mode in ("full", "attn"):
            if core < 4:
                m["amask"] = am_s
                m["cval"] = np.ones((128, 64), f32)
                m["ctx_k"] = np.ascontiguousarray(np.asarray(inputs["cache_k"], f32)[core, 0].reshape(256, 256))
                m["ctx_v"] = np.ascontiguousarray(np.asarray(inputs["cache_v"], f32)[core, 0].reshape(256, 256))
                m["ropeC"], m["ropeS"] = rc, rs
            else:
                m["amask"] = am_p
                m["cval"] = np.zeros((128, 64), f32)
                m["ctx_k"] = np.zeros((256, 256), f32)
                m["ctx_v"] = np.zeros((256, 256), f32)
                m["ropeC"] = np.ones((NT, 64), f32)
                m["ropeS"] = np.zeros((NT, 64), f32)
        if core < 4:
            m["x_in"] = np.ascontiguousarray(xsm[core])
            cv = c[core]
        else:
            xi = np.zeros((NT, D), f32)
            for sl in range(8):
                xi[sl * 512: sl * 512 + 256] = xp[(core - 4) * 8 + sl]
            m["x_in"] = xi
            cv = c_ctx
        m["cond"] = np.ascontiguousarray(cv.reshape(KC, 128).T)
        maps.append(m)
    return maps


def run(inputs, mode="full", trace=False):
    p = _get_prog(mode)
    maps = make_in_maps(inputs, mode)
    res = run_bass_kernel_spmd(p.nc, maps, core_ids=list(range(NCORES)), trace=trace)
    return res


def kernel(**inputs):
    res = run(inputs, "full")
    r = res.results
    y_sample = np.stack([np.asarray(r[i]["y"], np.float32) for i in range(4)], axis=0)
    y_prompt = np.zeros((32, 256, D), np.float32)
    hs = np.zeros((32, 1, 2, 4, 128, 128), np.float32)
    nk = np.zeros((32, 1, 256, 4, 64), np.float32)
    nv = np.zeros((32, 1, 256, 4, 64), np.float32)
    for core in range(4, 8):
        y = np.asarray(r[core]["y"], np.float32)
        for sl in range(8):
            b = (core - 4) * 8 + sl
            y_prompt[b] = y[sl * 512: sl * 512 + 256]
            hs[b, 0] = r[core]["hs_out"][sl]
            nk[b, 0] = np.asarray(r[core]["nk"][sl]).reshape(256, 4, 64)
            nv[b, 0] = np.asarray(r[core]["nv"][sl]).reshape(256, 4, 64)
    return (y_prompt, y_sample, hs, nk, nv)
```

```python
import numpy as np
import concourse.bass as bass
import concourse.mybir as mybir
from concourse.bass_utils import run_bass_kernel_spmd

F32 = mybir.dt.float32
BF16 = mybir.dt.bfloat16
AF = mybir.ActivationFunctionType
ALU = mybir.AluOpType
AX = mybir.AxisListType

NCORES = 8
D = 1024
KC = 8
DFF = 2816
FC = 22
NT = 4096
NTILE = NT // 128
EPS = 1e-6


class Trk:
    __slots__ = ("name", "w", "r", "excl")

    def __init__(self, name="", excl=False):
        self.name = name
        self.w = None
        self.r = []
        self.excl = excl


class Sched:
    ENGS = ("pe", "act", "dve", "pool", "sp")

    def __init__(self, nc):
        self.nc = nc
        self.ops = {e: [] for e in self.ENGS}
        self.cnt = {}
        self.waited = {e: {} for e in self.ENGS}
        self.sems = {}
        self.pend = {e: ([], []) for e in self.ENGS}
        self._semctx = []
        for e in self.ENGS:
            self._mksem("E_" + e)
        self.nops = 0

    def _mksem(self, key):
        ctx = self.nc.semaphore(key)
        h = ctx.__enter__()
        self._semctx.append(ctx)
        self.sems[key] = h
        self.cnt[key] = 0
        return h

    def _deps(self, eng, reads, writes):
        need = {}

        def add(ev):
            if ev is None:
                return
            k, v = ev
            if need.get(k, 0) < v:
                need[k] = v
        for t in reads:
            add(t.w)
        for t in writes:
            add(t.w)
            for ev in t.r:
                add(ev)
        for e2 in self.ENGS:
            if e2 == eng:
                continue
            p = self.pend[e2]
            if p[0] or p[1]:
                ids = set(id(t) for t in p[1])
                idr = set(id(t) for t in p[0])
                for t in reads:
                    assert id(t) not in ids, ("pending unsignaled writer", e2, t.name)
                for t in writes:
                    assert id(t) not in ids and id(t) not in idr, ("pending unsignaled access", e2, t.name)
        out = []
        wd = self.waited[eng]
        for k, v in need.items():
            if wd.get(k, 0) >= v:
                continue
            wd[k] = v
            out.append((k, v))
        return out

    def op(self, eng, fn, reads=(), writes=(), signal=True):
        ex = [t for t in reads if t.excl]
        if ex:
            writes = list(writes) + ex
            reads = [t for t in reads if not t.excl]
        waits = self._deps(eng, reads, writes)
        pr, pw = self.pend[eng]
        pr.extend(reads)
        pw.extend(writes)
        key = "E_" + eng
        if signal:
            self.cnt[key] += 1
            ev = (key, self.cnt[key])
            for t in pw:
                t.w = ev
                t.r = []
            for t in pr:
                if t.w is not ev:
                    t.r.append(ev)
            self.pend[eng] = ([], [])
        self.ops[eng].append((waits, fn, (key, 1) if signal else None))
        self.nops += 1

    def dma(self, eng, fn, dsts, srcs=(), semkey=None):
        if semkey not in self.sems:
            self._mksem(semkey)
        waits = self._deps(eng, list(srcs), list(dsts))
        self.cnt[semkey] += 16
        ev = (semkey, self.cnt[semkey])
        for t in dsts:
            t.w = ev
            t.r = []
        for t in srcs:
            t.r.append(ev)
        self.ops[eng].append((waits, fn, (semkey, 16)))
        self.nops += 1

    def barrier(self):
        for e in self.ENGS:
            assert not self.pend[e][0] and not self.pend[e][1]
        for e in self.ENGS:
            wd = self.waited[e]
            waits = []
            for k, v in self.cnt.items():
                if v > 0 and wd.get(k, 0) < v:
                    wd[k] = v
                    waits.append((k, v))
            self.ops[e].append((waits, None, None))

    def emit(self):
        nc = self.nc
        sems = self.sems
        engobj = {"pe": "tensor", "act": "scalar", "dve": "vector", "pool": "gpsimd", "sp": "sync"}
        with nc.Block() as block:
            for e in self.ENGS:
                lst = self.ops[e]

                def body(eng, lst=lst):
                    for waits, fn, inc in lst:
                        for k, v in waits:
                            eng.wait_ge(sems[k], v)
                        if fn is not None:
                            ins = fn(eng)
                            if inc is not None:
                                ins.then_inc(sems[inc[0]], inc[1])
                getattr(block, engobj[e])(body)


class Carve:
    def __init__(self, arena):
        self.ar = arena
        self.off = 0

    def f32(self, n):
        a = self.ar[:, self.off:self.off + n]
        self.off += n
        assert self.off <= self.ar.shape[1], ("arena overflow", self.off)
        return a

    def bf(self, n):
        assert n % 2 == 0
        a = self.ar[:, self.off:self.off + n // 2].bitcast(BF16)
        self.off += n // 2
        assert self.off <= self.ar.shape[1], ("arena overflow", self.off)
        return a

    def ring(self, n, width, kind, name):
        return [((self.f32(width) if kind == "f32" else self.bf(width)), Trk("%s%d" % (name, i))) for i in range(n)]


class Ring:
    def __init__(self, nc, name, n, shape, dtype):
        self.n = n
        self.name = name
        self.t = [nc.alloc_sbuf_tensor("%s%d" % (name, i), shape, dtype) for i in range(n)]
        self.trk = [Trk("%s%d" % (name, i)) for i in range(n)]
        self.i = 0

    def next(self):
        i = self.i % self.n
        self.i += 1
        return i, self.t[i], self.trk[i]


class Prog:
    def __init__(self, mode="full"):
        self.mode = mode
        nc = bass.Bass("TRN2", target_bir_lowering=False)
        self.nc = nc
        self.s = Sched(nc)
        self.inp = {}
        self.steps = []

    def din(self, name, shape, dt=F32):
        ap = self.nc.dram_tensor(name, list(shape), dt, kind="ExternalInput").ap()
        self.inp[name] = ap
        return ap

    def dout(self, name, shape, dt=F32):
        return self.nc.dram_tensor(name, list(shape), dt, kind="ExternalOutput").ap()

    def dscr(self, name, shape, dt=F32):
        return self.nc.dram_tensor(name, list(shape), dt, kind="Internal").ap()

    POOLS = {"all": list(range(8)), "P": [0, 1, 2, 3, 4], "S": [5, 6, 7], "A3": [0, 1, 2], "B5": [3, 4, 5, 6, 7],
             "A2": [0, 1], "Bx": [2, 3, 4], "By": [5, 6, 7]}

    def bank(self, pool="all"):
        lst = self.POOLS[pool]
        c = self.bank_cnt.get(pool, 0)
        self.bank_cnt[pool] = c + 1
        i = lst[c % len(lst)]
        return self.banks[i], self.bank_trk[i]

    @staticmethod
    def interleave(gens):
        gens = [g for g in gens if g is not None]
        while gens:
            for g in list(gens):
                try:
                    next(g)
                except StopIteration:
                    gens.remove(g)

    def step(self, wloads, body):
        self.steps.append((wloads, body))

    def run_steps(self, lookahead=3):
        s = self.s
        steps = self.steps
        self.steps = []
        slots = {}
        ring = self.wring
        pos = [0]

        def issue(i):
            wl = steps[i][0]
            if wl is None:
                return
            si, st, strk = ring.next()
            slots[i] = (st, strk)
            for k, fn in enumerate(wl):
                s.dma("pool", (lambda e, fn=fn, st=st: fn(e, st)), [self.wpart_trk[si][k]], [], semkey="W%d_%d" % (si, k))
        n = len(steps)
        for i in range(min(lookahead, n)):
            issue(i)
        for i in range(n):
            if i + lookahead < n:
                issue(i + lookahead)
            wl, body = steps[i]
            if wl is None:
                body(None, None)
            else:
                st, strk = slots.pop(i)
                si = ring.t.index(st)
                body(st, list(self.wpart_trk[si]))

    def build(self):
        nc, s = self.nc, self.s
        mode = self.mode
        x_in = self.din("x_in", [NT, D])
        cond = self.din("cond", [128, KC])
        ident_d = self.din("ident", [128, 128])
        w_mod = self.din("w_mod", [2, D, 9 * D])
        bmodT = self.din("bmodT", [2, 128, 72])
        normgT = self.din("normgT", [2, 3, 128, KC])
        ffn_w_in = ffn_w_out = None
        if mode in ("full", "ffn1", "l0"):
            ffn_w_in = self.din("ffn_w_in", [2, 2, D, 2 * DFF])
            ffn_w_out = self.din("ffn_w_out", [2, 2, DFF, D])
        y_out = self.dout("y", [NT, D])
        A = {}
        if mode in ("full", "attn"):
            A["od_w_in"] = self.din("od_w_in", [1, D, 1536])
            A["od_w_out"] = self.din("od_w_out", [1, D, D])
            A["amask"] = self.din("amask", [128, 1536])
            A["cval"] = self.din("cval", [128, 64])
            A["qkgain"] = self.din("qkgain", [128, 128])
            A["sinks"] = self.din("sinks", [1, 16])
            A["ctx_k"] = self.din("ctx_k", [256, 256])
            A["ctx_v"] = self.din("ctx_v", [256, 256])
            A["ropeC"] = self.din("ropeC", [NT, 64])
            A["ropeS"] = self.din("ropeS", [NT, 64])
            A["nk"] = self.dout("nk", [8, 256, 256])
            A["nv"] = self.dout("nv", [8, 256, 256])
            self.t_nk, self.t_nv = Trk("nk"), Trk("nv")
        self.ain = A
        E = {}
        if mode in ("full", "l0") or mode.startswith("even"):
            E["ev_w_in"] = self.din("ev_w_in", [1, D, 3584])
            E["ev_w_out"] = self.din("ev_w_out", [1, D, D])
            E["convwT"] = self.din("convwT", [128, 124])
            E["convvec"] = self.din("convvec", [128, 12])
            E["tmask"] = self.din("tmask", [128, 512])
            E["lbraw"] = self.din("lbraw", [2, 1536])
            E["hgn"] = self.din("hgn", [128, 128])
            E["cumM"] = self.din("cumM", [128, 1024])
            E["scmask"] = self.din("scmask", [128, 128])
            E["cflag"] = self.din("cflag", [128, 1])
            E["tokm"] = self.din("tokm", [128, 4])
            E["s0"] = self.din("s0", [2, 4, 128, 128])
            E["hs_out"] = self.dout("hs_out", [8, 2, 4, 128, 128])
            self.t_hs = Trk("hs")
            self.cvs = self.dscr("cvs", [128, 4, NT], BF16)
            self.obs = self.dscr("obs", [NTILE, 128, 512])
        self.ein = E
        xs = self.dscr("xs", [D, NT])
        xs_v = xs.rearrange("(kc p) t -> p kc t", p=128)
        self.xs_trk = [[Trk("xs_%d_%d" % (kc, tt)) for tt in range(NT // 512)] for kc in range(KC)]

        self.banks = [nc.alloc_psum_tensor("bank%d" % i, [128, 512], F32) for i in range(8)]
        self.bank_trk = [Trk("bank%d" % i, excl=True) for i in range(8)]
        self.bank_cnt = {}
        NW = 4
        self.wring = Ring(nc, "wr", NW, [128, 4096], BF16)
        self.wpart_trk = [[Trk("wr%d_%d" % (i, k)) for k in range(4)] for i in range(NW)]
        ident_f = nc.alloc_sbuf_tensor("ident_f", [128, 128], F32)
        ident_b = nc.alloc_sbuf_tensor("ident_b", [128, 128], BF16)
        ones_b = nc.alloc_sbuf_tensor("ones_b", [128, 128], BF16)
        condt = nc.alloc_sbuf_tensor("condt", [128, KC], F32)
        sc_b = nc.alloc_sbuf_tensor("sc_b", [128, KC], BF16)
        bmt = nc.alloc_sbuf_tensor("bmt", [128, 2, 72], F32)
        ngt = nc.alloc_sbuf_tensor("ngt", [128, 2, 3, KC], F32)
        t_const, t_cond, t_mod, t_modAG = Trk("const"), Trk("cond"), Trk("modT"), Trk("modAG")
        self.ones_b, self.t_const, self.ident_b, self.ident_f = ones_b, t_const, ident_b, ident_f
        self.modL = {}
        self.mod_args = (w_mod, sc_b, t_cond, bmt, ngt)
        self.pending_mod = None
        ARENA_F32 = 42 * 1024
        arena = nc.alloc_sbuf_tensor("arena", [128, ARENA_F32], F32)
        self.arena = arena

        s.dma("sp", lambda e: e.dma_start(out=ident_f[:], in_=ident_d), [t_const], [], semkey="C0")
        s.dma("sp", lambda e: e.dma_start(out=condt[:], in_=cond), [t_cond], [], semkey="C1")
        s.dma("sp", lambda e: e.dma_start(out=bmt[:], in_=bmodT.rearrange("l p j -> p l j")), [t_cond], [], semkey="C2")
        s.dma("sp", lambda e: e.dma_start(out=ngt[:], in_=normgT.rearrange("l s p k -> p l s k")), [t_cond], [], semkey="C3")
        s.op("dve", lambda e: e.tensor_copy(out=ident_b[:], in_=ident_f[:]), [t_const], [t_const])
        s.op("dve", lambda e: e.memset(ones_b[:], 1.0), [], [t_const])
        self.epsc = nc.alloc_sbuf_tensor("epsc", [128, 1], F32)
        s.op("dve", lambda e: e.memset(self.epsc[:], 1024.0 * EPS), [], [t_const])
        s.op("act", lambda e: e.activation(out=sc_b[:], in_=condt[:], func=AF.Silu), [t_cond], [t_cond])

        if mode in ("full", "l0"):
            m0steps, m0fin = self.make_mod(0, w_mod, sc_b, t_cond, bmt, ngt)
            self.phase_T0(x_in, xs_v, ident_f, t_const, extra_steps=m0steps)
            m0fin()
            self.use_mod(0)
            self.mod0_done = True
        else:
            self.phase_T0(x_in, xs_v, ident_f, t_const)
        if mode == "ffn1":
            self.phase_mod(0, w_mod, sc_b, t_cond, bmt, ngt)
            self.phase_ffn(0, 0, ffn_w_in, ffn_w_out, xs_v, ones_b, t_const)
        elif mode == "attn":
            self.phase_mod(1, w_mod, sc_b, t_cond, bmt, ngt)
            self.phase_attn(xs_v)
        elif mode.startswith("even"):
            self.phase_mod(0, w_mod, sc_b, t_cond, bmt, ngt)
            self.phase_even(xs_v)
        elif mode in ("full", "l0"):
            for l in ((0, 1) if mode == "full" else (0,)):
                if l == 0:
                    if not getattr(self, "mod0_done", False):
                        self.phase_mod(l, w_mod, sc_b, t_cond, bmt, ngt)
                    if mode == "full":
                        self.pending_mod = self.make_mod(1, w_mod, sc_b, t_cond, bmt, ngt)
                else:
                    assert self.pending_mod is None
                    self.use_mod(1)
                self.phase_ffn(l, 0, ffn_w_in, ffn_w_out, xs_v, ones_b, t_const)
                if l == 0:
                    self.phase_even(xs_v)
                else:
                    self.phase_attn(xs_v)
                last = (mode == "full" and l == 1)
                self.phase_ffn(l, 1, ffn_w_in, ffn_w_out, xs_v, ones_b, t_const, t1_out=(y_out if last else None))
                self.t1_done = last
        if not getattr(self, "t1_done", False):
            self.phase_T1(xs_v, y_out, ident_f, t_const)
        s.barrier()
        s.emit()
        return nc

    def phase_T0(self, x_in, xs_v, ident_f, t_const, extra_steps=None):
        nc, s = self.nc, self.s
        s.barrier()
        ar = self.arena
        tin = [ar[:, i * 1024:(i + 1) * 1024] for i in range(3)]
        tin_trk = [Trk("t0in%d" % i) for i in range(3)]
        stg = [ar[:, 3072 + i * 4096: 3072 + (i + 1) * 4096].rearrange("p (k t) -> p k t", k=KC) for i in range(2)]
        stg_trk = [[Trk("t0stg%d_%d" % (i, h)) for h in range(2)] for i in range(2)]
        x_v = x_in.rearrange("(n p) d -> n p d", p=128)

        def load(n):
            i = n % 3
            s.dma("sp", lambda e: e.dma_start(out=tin[i], in_=x_v[n]), [tin_trk[i]], [], semkey="T0in%d" % i)
        load(0)
        load(1)

        def tile(n):
            if n + 2 < NTILE:
                load(n + 2)
            i = n % 3
            g = n // 4
            si = g % 2
            for half in range(2):
                bk, bt = self.bank()
                for q in range(4):
                    kc = half * 4 + q
                    s.op("pe", lambda e, kc=kc, q=q, bk=bk, i=i: e.transpose(
                        out=bk[:, q * 128:(q + 1) * 128], in_=tin[i][:, kc * 128:(kc + 1) * 128], identity=ident_f[:]),
                        [tin_trk[i], t_const], [bt], signal=(q == 3))
                dst = stg[si][:, half * 4:(half + 1) * 4, (n % 4) * 128:(n % 4 + 1) * 128]
                src = bk[:, :].rearrange("p (k t) -> p k t", k=4)
                eng = "dve" if half == 0 else "act"
                if eng == "dve":
                    s.op("dve", lambda e, dst=dst, src=src: e.tensor_copy(out=dst, in_=src), [bt], [stg_trk[si][half]])
                else:
                    s.op("act", lambda e, dst=dst, src=src: e.activation(out=dst, in_=src, func=AF.Copy), [bt], [stg_trk[si][half]])
            if n % 4 == 3:
                tt = n // 4
                s.dma("sp", lambda e, si=si, tt=tt: e.dma_start(out=xs_v[:, :, tt * 512:(tt + 1) * 512], in_=stg[si]),
                      [self.xs_trk[kc][tt] for kc in range(KC)], list(stg_trk[si]), semkey="T0st%d" % si)

        extra = list(extra_steps) if extra_steps else []
        for n in range(NTILE):
            self.step(None, lambda a, b, n=n: tile(n))
            if extra and n % 2 == 1:
                self.steps.append(extra.pop(0))
        self.steps.extend(extra)
        self.run_steps()

    def t1_setup(self, xs_v, y_out, off):
        s = self.s
        ar = self.arena
        ident_f, t_const = self.ident_f, self.t_const
        fin = [ar[:, off + i * 4096: off + (i + 1) * 4096].rearrange("p (k t) -> p k t", k=KC) for i in range(2)]
        fin_trk = [Trk("t1in%d" % i) for i in range(2)]
        tout = [ar[:, off + 8192 + i * 1024: off + 8192 + (i + 1) * 1024] for i in range(3)]
        tout_trk = [Trk("t1out%d" % i) for i in range(3)]
        assert off + 8192 + 3072 <= ar.shape[1], "arena overflow (t1)"
        y_v = y_out.rearrange("(n p) d -> n p d", p=128)
        ytrk = Trk("y")

        def load(tt):
            i = tt % 2
            s.dma("sp", lambda e: e.dma_start(out=fin[i], in_=xs_v[:, :, tt * 512:(tt + 1) * 512]), [fin_trk[i]],
                  [self.xs_trk[kc][tt] for kc in range(KC)], semkey="T1in%d" % i)

        def comp(tt):
            i = tt % 2
            for q4 in range(4):
                n = tt * 4 + q4
                oi = n % 3
                for half in range(2):
                    bk, bt = self.bank()
                    for q in range(4):
                        kc = half * 4 + q
                        s.op("pe", lambda e, kc=kc, q=q, bk=bk, i=i, q4=q4: e.transpose(
                            out=bk[:, q * 128:(q + 1) * 128], in_=fin[i][:, kc, q4 * 128:(q4 + 1) * 128], identity=ident_f[:]),
                            [fin_trk[i], t_const], [bt], signal=(q == 3))
                    dst = tout[oi][:, half * 512:(half + 1) * 512]
                    if half == 0:
                        s.op("dve", lambda e, dst=dst, bk=bk: e.tensor_copy(out=dst, in_=bk[:, :]), [bt], [tout_trk[oi]])
                    else:
                        s.op("act", lambda e, dst=dst, bk=bk: e.activation(out=dst, in_=bk[:, :], func=AF.Copy), [bt], [tout_trk[oi]])
                s.dma("sp", lambda e, oi=oi, n=n: e.dma_start(out=y_v[n], in_=tout[oi]), [ytrk], [tout_trk[oi]],
                      semkey="T1st%d" % oi)
        return load, comp

    def phase_T1(self, xs_v, y_out, ident_f, t_const):
        self.s.barrier()
        load, comp = self.t1_setup(xs_v, y_out, 0)
        load(0)
        for tt in range(NT // 512):
            if tt + 1 < NT // 512:
                load(tt + 1)
            comp(tt)

    def make_mod(self, l, w_mod, sc_b, t_cond, bmt, ngt):
        nc, s = self.nc, self.s
        modT = nc.alloc_sbuf_tensor("modT_%d" % l, [128, 72], F32)
        macc = nc.alloc_sbuf_tensor("macc_%d" % l, [128, 72], F32)
        modA = nc.alloc_sbuf_tensor("modA_%d" % l, [128, 3, KC], F32)
        modG = nc.alloc_sbuf_tensor("modG_%d" % l, [128, 3, KC], F32)
        t_mod, t_modAG, t_macc = Trk("modT%d" % l), Trk("modAG%d" % l), Trk("macc%d" % l)
        self.modL[l] = (modT, modA, modG, t_mod, t_modAG)
        wv = w_mod[l].rearrange("(kc p) c -> p kc c", p=128)
        steps = []
        for cb in range(18):
            def wl(e, st, cb=cb):
                return e.dma_start(out=st[:, :].rearrange("p (k c) -> p k c", k=KC), in_=wv[:, :, cb * 512:(cb + 1) * 512])

            def body(st, strk, cb=cb):
                stv = st[:, :].rearrange("p (k c) -> p k c", k=KC)
                bk, bt = self.bank()
                for q in range(4):
                    for kc in range(KC):
                        s.op("pe", lambda e, q=q, kc=kc, stv=stv, bk=bk: e.matmul(
                            bk[:, q:q + 1], stv[:, kc, q * 128:(q + 1) * 128], sc_b[:, kc:kc + 1],
                            start=(kc == 0), stop=(kc == KC - 1)),
                            list(strk) + [t_cond], [bt], signal=(kc == KC - 1))
                s.op("dve", lambda e, bk=bk, cb=cb: e.tensor_copy(out=macc[:, cb * 4:(cb + 1) * 4], in_=bk[:, 0:4]), [bt], [t_macc])
            steps.append(([wl], body))

        def finalize():
            s.op("dve", lambda e: e.tensor_tensor(out=modT[:], in0=macc[:], in1=bmt[:, l, :], op=ALU.add), [t_macc, t_cond], [t_mod])
            mv = modT[:, :].rearrange("p (v k) -> p v k", k=KC)
            for sl in range(3):
                s.op("dve", lambda e, sl=sl: e.scalar_tensor_tensor(
                    out=modA[:, sl, :], in0=mv[:, 3 * sl + 1, :], scalar=1.0, in1=ngt[:, l, sl, :], op0=ALU.add, op1=ALU.mult),
                    [t_mod, t_cond], [t_modAG])
                s.op("dve", lambda e, sl=sl: e.tensor_scalar(
                    out=modA[:, sl, :], in0=modA[:, sl, :], scalar1=32.0, scalar2=None, op0=ALU.mult), [t_modAG], [t_modAG])
                gsc = 1.0 if sl == 1 else 0.5
                s.op("dve", lambda e, sl=sl, gsc=gsc: e.tensor_scalar(
                    out=modG[:, sl, :], in0=mv[:, 3 * sl + 2, :], scalar1=gsc, scalar2=None, op0=ALU.mult), [t_mod], [t_modAG])
        return steps, finalize

    def use_mod(self, l):
        self.modT, self.modA, self.modG, self.t_mod, self.t_modAG = self.modL[l]

    def phase_mod(self, l, w_mod, sc_b, t_cond, bmt, ngt):
        self.s.barrier()
        steps, fin = self.make_mod(l, w_mod, sc_b, t_cond, bmt, ngt)
        self.steps.extend(steps)
        self.run_steps()
        fin()
        self.use_mod(l)

    def norm_tile(self, xt, xt_trk, sl, hdst, hdst_trk, nr, W=512, pool="all"):
        s = self.s
        modT, modA = self.modT, self.modA
        ones_b, t_const = self.ones_b, self.t_const
        bk, bt = self.bank(pool)
        for kc in range(KC):
            sq, sq_trk = nr["sq"][kc % len(nr["sq"])]
            s.op("act", lambda e, kc=kc, sq=sq: e.activation(out=sq, in_=xt[:, kc, :], func=AF.Square), [xt_trk], [sq_trk])
            s.op("pe", lambda e, kc=kc, sq=sq: e.matmul(bk[:, 0:W], ones_b[:, :], sq, start=(kc == 0), stop=(kc == KC - 1)),
                 [sq_trk, t_const], [bt], signal=True)
        rstd, rstd_trk = nr["rstd"]
        s.op("act", lambda e: e.activation(out=rstd, in_=bk[:, 0:W], func=AF.Ln, bias=self.epsc[:, 0:1], scale=1.0), [bt, t_const], [rstd_trk])
        s.op("act", lambda e: e.activation(out=rstd, in_=rstd, func=AF.Exp, scale=-0.5), [rstd_trk], [rstd_trk])
        for kc in range(KC):
            tmp, tmp_trk = nr["tmp"][kc % len(nr["tmp"])]
            s.op("dve", lambda e, kc=kc, tmp=tmp: e.tensor_tensor(out=tmp, in0=xt[:, kc, :], in1=rstd, op=ALU.mult),
                 [xt_trk, rstd_trk], [tmp_trk])
            s.op("act", lambda e, kc=kc, tmp=tmp: e.activation(out=hdst[:, kc, :], in_=tmp, func=AF.Identity,
                                                               bias=modT[:, 3 * sl * KC + kc: 3 * sl * KC + kc + 1],
                                                               scale=modA[:, sl, kc:kc + 1]),
                 [tmp_trk, self.t_mod, self.t_modAG], [hdst_trk])

    def phase_attn(self, xs_v):
        nc, s = self.nc, self.s
        A = self.ain
        sl = 1
        s.barrier()
        cv = Carve(self.arena)
        ident_b, ones_b, t_const = self.ident_b, self.ones_b, self.t_const
        win = cv.bf(KC * 1536).rearrange("p (k c) -> p k c", k=KC)
        t_win = [Trk("awin%d" % i) for i in range(3)]
        kT = cv.bf(2 * NT).rearrange("p (r t) -> p r t", r=2)
        t_kT = [Trk("kT%d" % n) for n in range(NTILE)]
        Vv = cv.bf(NTILE * 256).rearrange("p (n f) -> p n f", n=NTILE)
        t_V = [Trk("V%d" % n) for n in range(NTILE)]
        kTc = cv.bf(512).rearrange("p (r t) -> p r t", r=2)
        Vc = cv.bf(512).rearrange("p (c f) -> p c f", c=2)
        t_ctx = Trk("ctx")
        msk = cv.bf(1536).rearrange("p (v j q) -> p v j q", v=6, j=2)
        cval = cv.bf(64)
        t_msk = Trk("msk")
        gain = cv.f32(128)
        t_gain = Trk("gain")
        epsq = cv.f32(1)
        xin = cv.f32(KC * 512).rearrange("p (k t) -> p k t", k=KC)
        t_xin = Trk("axin")
        nr = {"sq": cv.ring(2, 512, "bf", "asq"), "tmp": cv.ring(2, 512, "f32", "atmp"), "rstd": (cv.f32(512), Trk("arstd"))}
        h = cv.bf(KC * 512).rearrange("p (k t) -> p k t", k=KC)
        t_h = Trk("ah")
        qkv, t_qkv = cv.f32(1536), Trk("qkv")
        bufA, t_bufA = cv.f32(1280), Trk("bufA")
        bufB, t_bufB = cv.f32(1280), Trk("bufB")
        ss, t_ss = cv.f32(20), Trk("ss")
        rinv, t_rinv = cv.f32(20), Trk("rinv")
        ropeC = cv.ring(2, 64, "f32", "ropeC")
        ropeS = cv.ring(2, 64, "f32", "ropeS")
        qb, t_qb = cv.bf(1024), Trk("qb")
        kb, t_kb = cv.bf(256), Trk("kb")
        qT = cv.ring(3, 1024, "bf", "qT")
        pt = cv.ring(10, 512, "bf", "pt")
        rec = cv.ring(2, 512, "f32", "rec")
        at_g = cv.bf(8 * 512).rearrange("p (c t) -> p c t", c=8)
        t_at = [Trk("at%d" % i) for i in range(4)]
        xc = cv.ring(2, 512, "f32", "axc")
        xo = cv.ring(2, 512, "f32", "axo")
        esink = cv.bf(2048)
        sk, t_sk = cv.f32(16), Trk("sk")
        ctxs = cv.f32(512)
        t_ctxs = Trk("ctxs")
        ctxb = cv.bf(512)
        cnt = {"pt": 0, "rec": 0, "xc": 0}

        wv = A["od_w_in"][0].rearrange("(kc p) c -> p kc c", p=128)
        for c3 in range(3):
            s.dma("pool", lambda e, c3=c3: e.dma_start(out=win[:, :, c3 * 512:(c3 + 1) * 512], in_=wv[:, :, c3 * 512:(c3 + 1) * 512]),
                  [t_win[c3]], [], semkey="AW%d" % c3)
        s.dma("pool", lambda e: e.dma_start(out=msk.rearrange("p v j q -> p (v j q)"), in_=A["amask"]), [t_msk], [], semkey="AM")
        s.dma("pool", lambda e: e.dma_start(out=cval, in_=A["cval"]), [t_msk], [], semkey="AM")
        s.dma("sp", lambda e: e.dma_start(out=gain, in_=A["qkgain"]), [t_gain], [], semkey="AG")
        s.op("dve", lambda e: e.tensor_scalar(out=gain[:, 0:64], in0=gain[:, 0:64], scalar1=0.125, scalar2=None, op0=ALU.mult), [t_gain], [t_gain])
        s.op("dve", lambda e: e.memset(epsq, EPS), [], [t_gain])
        s.dma("sp", lambda e: e.dma_start(out=sk[0:1, :], in_=A["sinks"]), [t_sk], [], semkey="ASK")
        s.op("act", lambda e: e.activation(out=sk[0:1, :], in_=sk[0:1, :], func=AF.Exp), [t_sk], [t_sk])
        s.op("dve", lambda e: e.tensor_copy(out=esink[0:1, :].rearrange("p (h q) -> p h q", h=16),
                                            in_=sk[0:1, :].unsqueeze(2).to_broadcast([1, 16, 128])), [t_sk], [t_sk])
        s.dma("sp", lambda e: e.dma_start(out=ctxs.rearrange("p (c f) -> p c f", c=2), in_=A["ctx_k"].rearrange("(c p) f -> p c f", p=128)),
              [t_ctxs], [], semkey="ACX")
        s.op("dve", lambda e: e.tensor_copy(out=ctxb, in_=ctxs), [t_ctxs], [t_ctxs])
        bk, bt = self.bank()
        bkb = bk[:, :].bitcast(BF16)
        for c in range(2):
            for pr in range(2):
                i4 = c * 2 + pr
                s.op("pe", lambda e, c=c, pr=pr, i4=i4: e.transpose(out=bkb[:, i4 * 128:(i4 + 1) * 128],
                                                                    in_=ctxb[:, c * 256 + pr * 128: c * 256 + (pr + 1) * 128], identity=ident_b[:]),
                     [t_ctxs, t_const], [bt], signal=(i4 == 3))
        s.op("dve", lambda e: e.tensor_copy(out=kTc.rearrange("p r (c t) -> p c r t", c=2),
                                            in_=bkb[:, 0:512].rearrange("p (c r t) -> p c r t", c=2, r=2)), [bt], [t_ctx])
        s.dma("sp", lambda e: e.dma_start(out=ctxs.rearrange("p (c f) -> p c f", c=2), in_=A["ctx_v"].rearrange("(c p) f -> p c f", p=128)),
              [t_ctxs], [], semkey="ACX")
        s.op("dve", lambda e: e.tensor_copy(out=Vc.rearrange("p c f -> p (c f)"), in_=ctxs), [t_ctxs], [t_ctxs, t_ctx])

        def mvar(n):
            return 0 if n == 0 else (5 if n == NTILE - 1 else 1 + n % 4)

        def stageA(n):
            G, p4 = n // 4, n % 4
            if p4 == 0:
                s.dma("sp", lambda e: e.dma_start(out=xin, in_=xs_v[:, :, G * 512:(G + 1) * 512]), [t_xin],
                      [self.xs_trk[kc][G] for kc in range(KC)], semkey="AXin")
                self.norm_tile(xin, t_xin, sl, h, t_h, nr, pool="A2")
                yield
            rC, t_rC = ropeC[n % 2]
            rS, t_rS = ropeS[n % 2]
            s.dma("sp", lambda e: e.dma_start(out=rC, in_=A["ropeC"][n * 128:(n + 1) * 128, :]), [t_rC], [], semkey="ARC%d" % (n % 2))
            s.dma("sp", lambda e: e.dma_start(out=rS, in_=A["ropeS"][n * 128:(n + 1) * 128, :]), [t_rS], [], semkey="ARS%d" % (n % 2))
            for c3 in range(3):
                bk, bt = self.bank("A2")
                for kc in range(KC):
                    s.op("pe", lambda e, kc=kc, c3=c3, bk=bk: e.matmul(bk[:, :], h[:, kc, p4 * 128:(p4 + 1) * 128], win[:, kc, c3 * 512:(c3 + 1) * 512],
                                                                       start=(kc == 0), stop=(kc == KC - 1)),
                         [t_h, t_win[c3]], [bt], signal=(kc == KC - 1))
                if c3 < 2:
                    s.op("act", lambda e, c3=c3, bk=bk: e.activation(out=qkv[:, c3 * 512:(c3 + 1) * 512], in_=bk[:, :], func=AF.Copy), [bt], [t_qkv])
                else:
                    s.op("dve", lambda e, c3=c3, bk=bk: e.tensor_copy(out=qkv[:, c3 * 512:(c3 + 1) * 512], in_=bk[:, :]), [bt], [t_qkv])
                yield
            s.op("act", lambda e: e.activation(out=bufA, in_=qkv[:, 0:1280], func=AF.Square), [t_qkv], [t_bufA])
            s.op("dve", lambda e: e.tensor_reduce(out=ss, in_=bufA.rearrange("p (h d) -> p h d", d=64), axis=AX.X, op=ALU.add), [t_bufA], [t_ss])
            s.op("act", lambda e: e.activation(out=rinv, in_=ss, func=AF.Ln, bias=epsq[:, 0:1], scale=1.0 / 64.0), [t_ss, t_gain], [t_rinv])
            s.op("act", lambda e: e.activation(out=rinv, in_=rinv, func=AF.Exp, scale=-0.5), [t_rinv], [t_rinv])
            yield
            s.op("dve", lambda e: e.tensor_tensor(out=bufB.rearrange("p (h d) -> p h d", d=64), in0=qkv[:, 0:1280].rearrange("p (h d) -> p h d", d=64),
                                                  in1=rinv.unsqueeze(2).to_broadcast([128, 20, 64]), op=ALU.mult), [t_qkv, t_rinv], [t_bufB])
            s.op("pool", lambda e: e.tensor_tensor(out=bufB[:, 0:1024].rearrange("p (h d) -> p h d", d=64), in0=bufB[:, 0:1024].rearrange("p (h d) -> p h d", d=64),
                                                   in1=gain[:, 0:64].unsqueeze(1).to_broadcast([128, 16, 64]), op=ALU.mult), [t_bufB, t_gain], [t_bufB])
            s.op("pool", lambda e: e.tensor_tensor(out=bufB[:, 1024:1280].rearrange("p (h d) -> p h d", d=64), in0=bufB[:, 1024:1280].rearrange("p (h d) -> p h d", d=64),
                                                   in1=gain[:, 64:128].unsqueeze(1).to_broadcast([128, 4, 64]), op=ALU.mult), [t_bufB, t_gain], [t_bufB])
            yield
            s.op("dve", lambda e: e.tensor_tensor(out=bufA.rearrange("p (h d) -> p h d", d=64), in0=bufB.rearrange("p (h d) -> p h d", d=64),
                                                  in1=rC.unsqueeze(1).to_broadcast([128, 20, 64]), op=ALU.mult), [t_bufB, t_rC], [t_bufA])
            xv = bufB.rearrange("p (h a f e) -> p h a f e", a=2, f=2, e=16)
            tv = qkv[:, 0:1280].rearrange("p (h a f e) -> p h a f e", a=2, f=2, e=16)
            sv = rS.rearrange("p (a f e) -> p a f e", a=2, f=2)
            for f in range(2):
                s.op("pool", lambda e, f=f: e.tensor_tensor(out=tv[:, :, :, f, :], in0=xv[:, :, :, 1 - f, :],
                                                            in1=sv[:, :, f, :].unsqueeze(1).to_broadcast([128, 20, 2, 16]), op=ALU.mult),
                     [t_bufB, t_rS], [t_qkv])
            s.op("dve", lambda e: e.tensor_tensor(out=bufB, in0=bufA, in1=qkv[:, 0:1280], op=ALU.add), [t_bufA, t_qkv], [t_bufB])
            yield
            for pr in range(2):
                s.op("act", lambda e, pr=pr: e.activation(
                    out=qb[:, pr * 512:(pr + 1) * 512].rearrange("p (g two d) -> p g two d", g=4, two=2),
                    in_=bufB[:, pr * 512:(pr + 1) * 512].rearrange("p (two g d) -> p g two d", two=2, g=4), func=AF.Copy), [t_bufB], [t_qb])
            s.op("act", lambda e: e.activation(out=kb, in_=bufB[:, 1024:1280], func=AF.Copy), [t_bufB], [t_kb])
            s.op("pool", lambda e: e.tensor_copy(out=Vv[:, n, :], in_=qkv[:, 1280:1536]), [t_qkv], [t_V[n]])
            yield
            if p4 < 2:
                s.dma("sp", lambda e: e.dma_start(out=A["nk"][G, p4 * 128:(p4 + 1) * 128, :], in_=bufB[:, 1024:1280]), [self.t_nk], [t_bufB], semkey="ANK")
                s.dma("sp", lambda e: e.dma_start(out=A["nv"][G, p4 * 128:(p4 + 1) * 128, :], in_=qkv[:, 1280:1536]), [self.t_nv], [t_qkv], semkey="ANV")
            bk, bt = self.bank("A2")
            bkb = bk[:, :].bitcast(BF16)
            for pr in range(2):
                for g in range(4):
                    i8 = pr * 4 + g
                    s.op("pe", lambda e, pr=pr, g=g, i8=i8, bkb=bkb: e.transpose(out=bkb[:, i8 * 128:(i8 + 1) * 128],
                                                                                 in_=qb[:, i8 * 128:(i8 + 1) * 128], identity=ident_b[:]),
                         [t_qb, t_const], [bt], signal=(i8 == 7))
            qTt, t_qT = qT[n % 3]
            s.op("act", lambda e, bkb=bkb, qTt=qTt: e.activation(out=qTt, in_=bkb[:, :], func=AF.Copy), [bt], [t_qT])
            yield
            bk2, bt2 = self.bank("A2")
            bk2b = bk2[:, :].bitcast(BF16)
            for pr in range(2):
                s.op("pe", lambda e, pr=pr, bk2b=bk2b: e.transpose(out=bk2b[:, pr * 128:(pr + 1) * 128], in_=kb[:, pr * 128:(pr + 1) * 128], identity=ident_b[:]),
                     [t_kb, t_const], [bt2], signal=(pr == 1))
            s.op("dve", lambda e, bk2b=bk2b: e.tensor_copy(out=kT[:, :, n * 128:(n + 1) * 128], in_=bk2b[:, 0:256].rearrange("p (r t) -> p r t", r=2)),
                 [bt2], [t_kT[n]])

        def stageB(n, khs, bpool, ptr, rci):
            G, p4 = n // 4, n % 4
            var = mvar(n)
            qTt, t_qT = qT[n % 3]
            qTv = qTt.rearrange("p (r gq) -> p r gq", r=2)
            loc = [max(n - 1, 0), n, min(n + 1, NTILE - 1)]
            for kh in khs:
                pr, lo = kh // 2, (kh % 2) * 64
                pts = []
                for j in range(5):
                    if j < 2:
                        lk = kTc[lo:lo + 64, pr, j * 128:(j + 1) * 128]
                        lv = Vc[:, j, kh * 64:(kh + 1) * 64]
                        rd = [t_ctx]
                    else:
                        m = loc[j - 2]
                        lk = kT[lo:lo + 64, pr, m * 128:(m + 1) * 128]
                        lv = Vv[:, m, kh * 64:(kh + 1) * 64]
                        rd = [t_kT[m], t_V[m]]
                    bk, bt = self.bank(bpool)
                    rq = qTv[lo:lo + 64, pr, :]
                    if j in (2, 4):
                        s.op("pe", lambda e, lk=lk, bk=bk, rq=rq: e.matmul(bk[:, :], lk, rq, start=True, stop=False),
                             rd + [t_qT], [bt], signal=False)
                        mk = msk[:, var, (j - 2) // 2, :]
                        for g in range(4):
                            s.op("pe", lambda e, mk=mk, bk=bk, g=g: e.matmul(bk[:, g * 128:(g + 1) * 128], ident_b[:, :], mk, start=False, stop=(g == 3)),
                                 [t_msk, t_const], [bt], signal=(g == 3))
                    else:
                        s.op("pe", lambda e, lk=lk, bk=bk, rq=rq: e.matmul(bk[:, :], lk, rq, start=True, stop=True),
                             rd + [t_qT], [bt], signal=True)
                    ptt, t_pt = ptr[j]
                    s.op("act", lambda e, bk=bk, ptt=ptt: e.activation(out=ptt, in_=bk[:, :], func=AF.Exp), [bt], [t_pt])
                    pts.append((ptt, t_pt, lv, rd))
                    yield
                bo, bot = self.bank(bpool)
                bd, bdt = self.bank(bpool)
                bo_v = bo[lo:lo + 64, :]
                bd_v = bd[lo:lo + 64, :]
                for j, (ptt, t_pt, lv, rd) in enumerate(pts):
                    s.op("pe", lambda e, ptt=ptt, lv=lv, j=j, bo_v=bo_v: e.matmul(bo_v, lv, ptt, start=(j == 0), stop=(j == 4)),
                         rd + [t_pt], [bot], signal=(j == 4))
                yield
                es = esink[0:1, kh * 512:(kh + 1) * 512]
                s.op("pe", lambda e, bd_v=bd_v, es=es: e.matmul(bd_v, ones_b[0:1, 0:64], es, start=True, stop=False),
                     [t_sk, t_const], [bdt], signal=False)
                for j, (ptt, t_pt, lv, rd) in enumerate(pts):
                    dl = cval[:, 0:64] if j < 2 else ones_b[:, 0:64]
                    s.op("pe", lambda e, ptt=ptt, j=j, bd_v=bd_v, dl=dl: e.matmul(bd_v, dl, ptt, start=False, stop=(j == 4)),
                         [t_pt, t_const, t_msk], [bdt], signal=(j == 4))
                yield
                rc, t_rc = rec[rci]
                rc_v = rc[lo:lo + 64, :]
                at_v = at_g[lo:lo + 64, pr * 4:(pr + 1) * 4, p4 * 128:(p4 + 1) * 128]
                s.op("act", lambda e, rc_v=rc_v, bd_v=bd_v: e.activation(out=rc_v, in_=bd_v, func=AF.Ln), [bdt], [t_rc])
                s.op("act", lambda e, rc_v=rc_v: e.activation(out=rc_v, in_=rc_v, func=AF.Exp, scale=-1.0), [t_rc], [t_rc])
                s.op("dve", lambda e, rc_v=rc_v, bo_v=bo_v, at_v=at_v: e.tensor_tensor(
                    out=at_v, in0=bo_v.rearrange("p (g q) -> p g q", g=4), in1=rc_v.rearrange("p (g q) -> p g q", g=4), op=ALU.mult),
                    [bot, t_rc], [t_at[p4]])
                yield

        wo_v = A["od_w_out"][0].rearrange("(r two g d) c -> two d r g c", r=2, two=2, g=4, d=64)

        def outproj_steps(G):
            for dc in range(KC):
                def mk_wl(two, r, dc=dc):
                    def wl(e, st):
                        return e.dma_start(out=st[two * 64:(two + 1) * 64, r * 512:(r + 1) * 512].rearrange("p (g c) -> p g c", g=4),
                                           in_=wo_v[two, :, r, :, dc * 128:(dc + 1) * 128])
                    return wl
                wls = [mk_wl(two, r) for two in range(2) for r in range(2)]

                def body(st, strk, dc=dc, G=G):
                    wv_ = st[:, 0:1024].rearrange("p (c8 c) -> p c8 c", c8=8)
                    ci = cnt["xc"] % 2
                    cnt["xc"] += 1
                    xct, t_xc = xc[ci]
                    xot, t_xo = xo[ci]
                    s.dma("sp", lambda e: e.dma_start(out=xct, in_=xs_v[:, dc, G * 512:(G + 1) * 512]), [t_xc], [self.xs_trk[dc][G]], semkey="AXc%d" % ci)
                    bk, bt = self.bank()
                    for c8 in range(8):
                        s.op("pe", lambda e, c8=c8, bk=bk: e.matmul(bk[:, :], wv_[:, c8, :], at_g[:, c8, :], start=(c8 == 0), stop=(c8 == 7)),
                             list(strk) + list(t_at), [bt], signal=(c8 == 7))
                    s.op("dve", lambda e, bk=bk, mg=self.modG[:, sl, dc:dc + 1]: e.scalar_tensor_tensor(out=xot, in0=bk[:, :], scalar=mg, in1=xct,
                                                                        op0=ALU.mult, op1=ALU.add), [bt, t_xc, self.t_modAG], [t_xo])
                    s.dma("sp", lambda e: e.dma_start(out=xs_v[:, dc, G * 512:(G + 1) * 512], in_=xot), [self.xs_trk[dc][G]], [t_xo], semkey="AXo%d" % ci)
                self.step(wls, body)

        for n in range(NTILE + 2):
            ga = (lambda n=n: stageA(n)) if n < NTILE else None
            gb = (lambda n=n: stageB(n - 2, (0, 2), "Bx", pt[0:5], 0)) if n >= 2 else None
            gc = (lambda n=n: stageB(n - 2, (1, 3), "By", pt[5:10], 1)) if n >= 2 else None
            self.step(None, lambda a, b, ga=ga, gb=gb, gc=gc: self.interleave([ga() if ga else None, gb() if gb else None, gc() if gc else None]))
            if n >= 2 and (n - 2) % 4 == 3:
                outproj_steps((n - 2) // 4)
        self.run_steps()

    def phase_even(self, xs_v):
        nc, s = self.nc, self.s
        A = self.ein
        sl = 1
        s.barrier()
        ident_b, ones_b, t_const = self.ident_b, self.ones_b, self.t_const
        cvs, obs = self.cvs, self.obs
        t_cvs = [Trk("cvs%d" % g) for g in range(8)]
        t_obs = [Trk("obs%d" % n) for n in range(NTILE)]
        cv = Carve(self.arena)
        xin, t_xin = cv.f32(KC * 512).rearrange("p (k t) -> p k t", k=KC), Trk("exin")
        h, t_h = cv.bf(KC * 512).rearrange("p (k t) -> p k t", k=KC), Trk("eh")
        nr = {"sq": cv.ring(2, 512, "bf", "esq"), "tmp": cv.ring(2, 512, "f32", "etmp"), "rstd": (cv.f32(512), Trk("erstd"))}
        ones_f = cv.f32(128)
        cwT = cv.f32(124).rearrange("p (c j) -> p c j", c=4)
        cvec = cv.f32(12).rearrange("p (w c) -> p w c", w=3)
        tmask = cv.bf(512)
        lbt = [cv.f32(512) for d_ in range(2)]
        oml = [cv.f32(512) for d_ in range(2)]
        hgn = cv.f32(128)
        cumM = cv.f32(1024).rearrange("p (d m t) -> p d m t", d=2, m=4)
        scm = cv.bf(128).rearrange("p (d t) -> p d t", d=2)
        cflag = cv.f32(1)
        tokm = cv.f32(4)
        eps5 = cv.f32(1)
        epsq = cv.f32(1)
        t_ec = Trk("econst")
        base = cv.off

        s.op("dve", lambda e: e.memset(ones_f, 1.0), [], [t_ec])
        s.op("dve", lambda e: e.memset(eps5, 512.0 * 1e-5), [], [t_ec])
        s.op("dve", lambda e: e.memset(epsq, EPS), [], [t_ec])
        s.dma("sp", lambda e: e.dma_start(out=cwT.rearrange("p c j -> p (c j)"), in_=A["convwT"]), [t_ec], [], semkey="EC0")
        s.dma("sp", lambda e: e.dma_start(out=cvec.rearrange("p w c -> p (w c)"), in_=A["convvec"]), [t_ec], [], semkey="EC0")
        s.dma("pool", lambda e: e.dma_start(out=tmask, in_=A["tmask"]), [t_ec], [], semkey="EC1")
        s.dma("sp", lambda e: e.dma_start(out=hgn, in_=A["hgn"]), [t_ec], [], semkey="EC0")
        s.dma("sp", lambda e: e.dma_start(out=cumM.rearrange("p d m t -> p (d m t)"), in_=A["cumM"]), [t_ec], [], semkey="EC0")
        s.dma("pool", lambda e: e.dma_start(out=scm.rearrange("p d t -> p (d t)"), in_=A["scmask"]), [t_ec], [], semkey="EC1")
        s.dma("sp", lambda e: e.dma_start(out=cflag, in_=A["cflag"]), [t_ec], [], semkey="EC0")
        s.dma("sp", lambda e: e.dma_start(out=tokm, in_=A["tokm"]), [t_ec], [], semkey="EC0")
        raw = self.arena[0:1, base:base + 1536]
        t_raw = Trk("lbraw")
        for d_ in range(2):
            s.dma("sp", lambda e, d_=d_: e.dma_start(out=raw, in_=A["lbraw"][d_:d_ + 1, :]), [t_raw], [], semkey="EC2")
            s.op("act", lambda e: e.activation(out=raw, in_=raw, func=AF.Exp), [t_raw], [t_raw])
            s.op("dve", lambda e: e.tensor_tensor(out=raw[:, 512:1024], in0=raw[:, 512:1024], in1=raw[:, 1024:1536], op=ALU.add), [t_raw], [t_raw])
            s.op("dve", lambda e: e.tensor_tensor(out=raw[:, 512:1024], in0=raw[:, 512:1024], in1=raw[:, 0:512], op=ALU.add), [t_raw], [t_raw])
            s.op("dve", lambda e: e.reciprocal(out=raw[:, 512:1024], in_=raw[:, 512:1024]), [t_raw], [t_raw])
            s.op("dve", lambda e: e.tensor_tensor(out=raw[:, 0:512], in0=raw[:, 0:512], in1=raw[:, 512:1024], op=ALU.mult), [t_raw], [t_raw])
            bk, bt = self.bank()
            s.op("pe", lambda e, bk=bk: e.matmul(bk[:, :], ones_f[0:1, :], raw[:, 0:512], start=True, stop=True), [t_raw, t_ec], [bt])
            s.op("dve", lambda e, bk=bk, d_=d_: e.tensor_copy(out=lbt[d_], in_=bk[:, :]), [bt], [t_ec])
            s.op("dve", lambda e, d_=d_: e.tensor_scalar(out=oml[d_], in0=lbt[d_], scalar1=-1.0, scalar2=1.0, op0=ALU.mult, op1=ALU.add), [t_ec], [t_ec])
        s.barrier()

        def load_norm(G, pool="all"):
            s.dma("sp", lambda e: e.dma_start(out=xin, in_=xs_v[:, :, G * 512:(G + 1) * 512]), [t_xin],
                  [self.xs_trk[kc][G] for kc in range(KC)], semkey="EXin")
            self.norm_tile(xin, t_xin, sl, h, t_h, nr, pool=pool)

        cv1 = Carve(self.arena)
        cv1.off = base
        wcv = cv1.bf(KC * 1024).rearrange("p (k c) -> p k c", k=KC)
        t_wcv = [Trk("wcv%d" % i) for i in range(2)]
        aT = cv1.bf(4 * (NT + 32)).rearrange("p (c t) -> p c t", c=4)
        t_aT = [Trk("aT%d" % g) for g in range(8)]
        t_apad = Trk("apad")
        diag = cv1.bf(4 * 31 * 128).rearrange("p (c j q) -> p c j q", c=4, j=31)
        t_diag = Trk("diag")
        yb = cv1.f32(4 * 512).rearrange("p (c t) -> p c t", c=4)
        t_yb = [Trk("yb%d" % c) for c in range(4)]
        ybf = cv1.ring(2, 512, "bf", "ybf")
        ysq = cv1.ring(2, 512, "bf", "ysq")
        sgr = cv1.ring(2, 512, "f32", "sgr")
        agr = cv1.ring(2, 512, "f32", "agr")
        mu, t_mu = cv1.f32(512), Trk("mu")
        rs, t_rs = cv1.f32(512), Trk("rs")
        zt = cv1.ring(2, 512, "f32", "zt")
        co = [cv1.bf(4 * 512).rearrange("p (c t) -> p c t", c=4) for i in range(2)]
        t_co = [Trk("co%d" % i) for i in range(2)]
        wv = A["ev_w_in"][0].rearrange("(kc p) c -> p kc c", p=128)
        for i in range(2):
            s.dma("pool", lambda e, i=i: e.dma_start(out=wcv[:, :, i * 512:(i + 1) * 512], in_=wv[:, :, i * 512:(i + 1) * 512]), [t_wcv[i]], [], semkey="EW%d" % i)
        s.op("pool", lambda e: e.memset(aT[:, :, 0:15], 0.0), [], [t_apad])
        s.op("pool", lambda e: e.memset(aT[:, :, 15 + NT:NT + 32], 0.0), [], [t_apad])
        for cc in range(4):
            for j in range(31):
                s.op("dve", lambda e, cc=cc, j=j: e.tensor_scalar(out=diag[:, cc, j, :], in0=ident_b[:, :], scalar1=cwT[:, cc, j:j + 1], scalar2=None, op0=ALU.mult),
                     [t_const, t_ec], [t_diag])
        cn = {"i": 0}

        def glu(G):
            load_norm(G, "P")
            yield
            for cc in range(4):
                ba, bat = self.bank("P")
                bg, bgt = self.bank("P")
                for kc in range(KC):
                    s.op("pe", lambda e, kc=kc, cc=cc, ba=ba: e.matmul(ba[:, :], wcv[:, kc, cc * 128:(cc + 1) * 128], h[:, kc, :], start=(kc == 0), stop=(kc == KC - 1)),
                         [t_wcv[0], t_h], [bat], signal=(kc == KC - 1))
                for kc in range(KC):
                    s.op("pe", lambda e, kc=kc, cc=cc, bg=bg: e.matmul(bg[:, :], wcv[:, kc, 512 + cc * 128:512 + (cc + 1) * 128], h[:, kc, :], start=(kc == 0), stop=(kc == KC - 1)),
                         [t_wcv[1], t_h], [bgt], signal=(kc == KC - 1))
                i = cn["i"] % 2
                cn["i"] += 1
                sg_, t_sg = sgr[i]
                ag_, t_ag = agr[i]
                s.op("act", lambda e, bg=bg, sg_=sg_: e.activation(out=sg_, in_=bg[:, :], func=AF.Sigmoid), [bgt], [t_sg])
                s.op("dve", lambda e, ba=ba, sg_=sg_, ag_=ag_: e.tensor_tensor(out=ag_, in0=ba[:, :], in1=sg_, op=ALU.mult), [bat, t_sg], [t_ag])
                dst = aT[:, cc, 15 + G * 512: 15 + (G + 1) * 512]
                s.op("pool", lambda e, ag_=ag_, dst=dst: e.tensor_tensor(out=dst, in0=ag_, in1=tmask, op=ALU.mult), [t_ag, t_ec], [t_aT[G]])
                yield

        def conv(G):
            ci = G % 2
            for cc in range(4):
                bk, bt = self.bank("S")
                rd = [t_aT[g] for g in (G - 1, G, G + 1) if 0 <= g < 8] + [t_apad, t_diag]
                for j in range(31):
                    src = aT[:, cc, G * 512 + j: G * 512 + j + 512]
                    s.op("pe", lambda e, cc=cc, j=j, src=src, bk=bk: e.matmul(bk[:, :], diag[:, cc, j, :], src, start=(j == 0), stop=(j == 30)),
                         rd, [bt], signal=(j == 30))
                s.op("act", lambda e, cc=cc, bk=bk: e.activation(out=yb[:, cc, :], in_=bk[:, :], func=AF.Identity, bias=cvec[:, 0, cc:cc + 1], scale=1.0),
                     [bt, t_ec], [t_yb[cc]])
                yield
            b1, b1t = self.bank("S")
            b2, b2t = self.bank("S")
            for cc in range(4):
                yf, t_yf = ybf[cc % 2]
                yq, t_yq = ysq[cc % 2]
                s.op("dve", lambda e, cc=cc, yf=yf: e.tensor_copy(out=yf, in_=yb[:, cc, :]), [t_yb[cc]], [t_yf])
                s.op("act", lambda e, cc=cc, yq=yq: e.activation(out=yq, in_=yb[:, cc, :], func=AF.Square), [t_yb[cc]], [t_yq])
                s.op("pe", lambda e, cc=cc, yf=yf, b1=b1: e.matmul(b1[:, :], ones_b[:, :], yf, start=(cc == 0), stop=(cc == 3)), [t_yf, t_const], [b1t])
                s.op("pe", lambda e, cc=cc, yq=yq, b2=b2: e.matmul(b2[:, :], ones_b[:, :], yq, start=(cc == 0), stop=(cc == 3)), [t_yq, t_const], [b2t])
            s.op("act", lambda e, b1=b1: e.activation(out=mu, in_=b1[:, :], func=AF.Copy, scale=1.0 / 512.0), [b1t], [t_mu])
            s.op("dve", lambda e, b1=b1: e.tensor_tensor(out=rs, in0=b1[:, :], in1=mu, op=ALU.mult), [b1t, t_mu], [t_rs])
            s.op("dve", lambda e, b2=b2: e.tensor_tensor(out=rs, in0=b2[:, :], in1=rs, op=ALU.subtract), [b2t, t_rs], [t_rs])
            s.op("act", lambda e: e.activation(out=rs, in_=rs, func=AF.Ln, bias=eps5[:, 0:1], scale=1.0), [t_rs, t_ec], [t_rs])
            s.op("act", lambda e: e.activation(out=rs, in_=rs, func=AF.Exp, scale=-0.5), [t_rs], [t_rs])
            yield
            for cc in range(4):
                z_, t_z = zt[cc % 2]
                s.op("dve", lambda e, cc=cc, z_=z_: e.tensor_tensor(out=z_, in0=yb[:, cc, :], in1=mu, op=ALU.subtract), [t_yb[cc], t_mu], [t_z])
                s.op("pool", lambda e, z_=z_: e.tensor_tensor(out=z_, in0=z_, in1=rs, op=ALU.mult), [t_z, t_rs], [t_z])
                s.op("act", lambda e, cc=cc, z_=z_, ci=ci: e.activation(out=co[ci][:, cc, :], in_=z_, func=AF.Silu, bias=cvec[:, 2, cc:cc + 1], scale=self.lng_s[:, cc:cc + 1]),
                     [t_z, t_ec], [t_co[ci]])
                yield
            s.dma("sp", lambda e, ci=ci: e.dma_start(out=cvs[:, :, G * 512:(G + 1) * 512], in_=co[ci]), [t_cvs[G]], [t_co[ci]], semkey="ECo%d" % ci)

        self.lng_s = cv1.f32(4)
        s.op("dve", lambda e: e.tensor_scalar(out=self.lng_s, in0=cvec[:, 1, :], scalar1=float(np.sqrt(512.0)), scalar2=None, op0=ALU.mult), [t_ec], [t_ec])
        if self.mode == "even_a":
            return
        for G in range(10):
            gg = (lambda G=G: glu(G)) if G < 8 else None
            gc = (lambda G=G: conv(G - 2)) if (G >= 2 and self.mode != "even_b") else None
            self.interleave([gg() if gg else None, gc() if gc else None])
        if self.mode in ("even_b", "even_c"):
            return

        s.barrier()
        cv2 = Carve(self.arena)
        cv2.off = base
        whg = cv2.bf(KC * 2048).rearrange("p (k c) -> p k c", k=KC)
        t_whg = [Trk("whg%d" % i) for i in range(4)]
        S_, t_S = cv2.f32(512).rearrange("p (h v) -> p h v", h=4), Trk("S")
        Sb, t_Sb = cv2.bf(512).rearrange("p (h v) -> p h v", h=4), Trk("Sb")
        qs, t_qs = cv2.f32(512), Trk("qs")
        ff, t_ff = cv2.f32(512), Trk("ff")
        gl, t_gl = cv2.f32(512), Trk("gl")
        kk, t_kk = cv2.f32(512), Trk("kk")
        er = cv2.ring(2, 512, "f32", "er")
        qh_t, t_qh = cv2.bf(512), Trk("qh_t")
        qt_t, t_qt = cv2.bf(512), Trk("qt_t")
        kt_t, t_kt = cv2.bf(512), Trk("kt_t")
        qtT, t_qtT = cv2.bf(512).rearrange("p (h t) -> p h t", h=4), Trk("qtT")
        ktT, t_ktT = cv2.bf(512).rearrange("p (h t) -> p h t", h=4), Trk("ktT")
        P2 = []
        for i in range(2):
            P2.append({
                "qhT": (cv2.bf(512).rearrange("p (h t) -> p h t", h=4), Trk("qhT%d" % i)),
                "scT": (cv2.bf(256), Trk("scT%d" % i)),
                "v": (cv2.bf(512), Trk("v%d" % i)),
                "kh": (cv2.bf(512), Trk("kh%d" % i)),
                "dec": (cv2.f32(8), Trk("dec%d" % i)),
                "gs": (cv2.f32(512), Trk("gs%d" % i)),
                "ob": (cv2.f32(512), Trk("ob%d" % i)),
                "ost": (cv2.f32(512), Trk("ost%d" % i)),
            })
        osum, t_osum = cv2.f32(512), Trk("osum")
        osq, t_osq = cv2.f32(512), Trk("osq")
        oss, t_oss = cv2.f32(4), Trk("oss")
        r_b, t_rb = cv2.bf(512), Trk("r_b")
        rT = cv2.bf(4 * 512).rearrange("p (c t) -> p c t", c=4)
        t_rT = [Trk("rT%d" % i) for i in range(4)]
        cvl, t_cvl = cv2.bf(4 * 512).rearrange("p (c t) -> p c t", c=4), Trk("cvl")
        xc = cv2.ring(2, 512, "f32", "exc")
        xo = cv2.ring(2, 512, "f32", "exo")
        cnx = {"i": 0}
        wo_v = A["ev_w_out"][0].rearrange("(c8 p) d -> p c8 d", p=128)

        def prep(n, d_, P, fwd_final):
            G, p4 = n // 4, n % 4
            if (d_ == 0 and p4 == 0) or (d_ == 1 and p4 == 3):
                load_norm(G, "P")
                yield
            ncomp = 4 if fwd_final else 3
            bks = []
            for c in range(ncomp):
                bk, bt = self.bank("P")
                for kc in range(KC):
                    s.op("pe", lambda e, kc=kc, c=c, bk=bk: e.matmul(bk[:, :], h[:, kc, p4 * 128:(p4 + 1) * 128], whg[:, kc, c * 512:(c + 1) * 512],
                                                                     start=(kc == 0), stop=(kc == KC - 1)), [t_h, t_whg[c]], [bt], signal=(kc == KC - 1))
                bks.append((bk, bt))
            (bq, bqt), (bz, bzt), (bv, bvt) = bks[0], bks[1], bks[2]
            yield
            s.op("act", lambda e: e.activation(out=qs, in_=bq[:, :], func=AF.Silu), [bqt], [t_qs])
            s.op("act", lambda e: e.activation(out=ff, in_=bz[:, :], func=AF.Sigmoid), [bzt], [t_ff])
            vb, t_vb = P["v"]
            s.op("act", lambda e: e.activation(out=vb, in_=bv[:, :], func=AF.Copy), [bvt], [t_vb])
            if fwd_final:
                gs, t_gs = P["gs"]
                bg, bgt = bks[3]
                s.op("act", lambda e: e.activation(out=gs, in_=bg[:, :], func=AF.Silu), [bgt], [t_gs])
            yield
            s.op("dve", lambda e: e.tensor_tensor(out=ff, in0=ff, in1=oml[d_], op=ALU.mult), [t_ff, t_ec], [t_ff])
            s.op("dve", lambda e: e.tensor_tensor(out=ff, in0=ff, in1=lbt[d_], op=ALU.add), [t_ff, t_ec], [t_ff])
            yield
            s.op("act", lambda e: e.activation(out=gl, in_=ff, func=AF.Ln), [t_ff], [t_gl])
            s.op("dve", lambda e: e.tensor_scalar(out=gl, in0=gl, scalar1=tokm[:, p4:p4 + 1], scalar2=None, op0=ALU.mult), [t_gl, t_ec], [t_gl])
            s.op("pool", lambda e: e.tensor_scalar(out=kk, in0=ff, scalar1=-1.0, scalar2=1.0, op0=ALU.mult, op1=ALU.add), [t_ff], [t_kk])
            yield
            if self.mode == "even_e1":
                return
            bb = []
            for m in range(3):
                bk, bt = self.bank("P")
                s.op("pe", lambda e, m=m, bk=bk: e.matmul(bk[:, :], cumM[:, d_, m, :], gl, start=True, stop=True), [t_gl, t_ec], [bt])
                bb.append((bk, bt))
                yield
            be, bet = self.bank("P")
            s.op("pe", lambda e, be=be: e.matmul(be[:, :], cumM[:, d_, 3, :], gl, start=True, stop=True), [t_gl, t_ec], [bet])
            ee, t_ee = er[0]
            s.op("act", lambda e, be=be, ee=ee: e.activation(out=ee, in_=be[:, :], func=AF.Exp), [bet], [t_ee])
            yield
            bd, bdt = self.bank("P")
            for hh in range(4):
                s.op("pe", lambda e, hh=hh, bd=bd, ee=ee: e.transpose(out=bd[:, hh * 128:(hh + 1) * 128], in_=ee[:, hh * 128:(hh + 1) * 128], identity=self.ident_f[:]),
                     [t_ee, t_const], [bdt], signal=(hh == 3))
            dec, t_dec = P["dec"]
            s.op("dve", lambda e, bd=bd, dec=dec: e.tensor_copy(out=dec.rearrange("p (h c) -> p h c", h=4),
                                                               in_=bd[:, :].rearrange("p (h c t) -> p h c t", h=4, c=2)[:, :, :, 0]), [bdt], [t_dec])
            yield
            kh_, t_kh = P["kh"]
            specs = [(0, 1.0, qs, t_qs, qh_t, t_qh), (1, 1.0, qs, t_qs, qt_t, t_qt), (1, -1.0, kk, t_kk, kt_t, t_kt), (2, 1.0, kk, t_kk, kh_, t_kh)]
            for i, (m, sc_, src, t_src, dst, t_dst) in enumerate(specs):
                e_, t_e = er[i % 2]
                bk, bt = bb[m]
                s.op("act", lambda e, e_=e_, bk=bk, sc_=sc_: e.activation(out=e_, in_=bk[:, :], func=AF.Exp, scale=sc_), [bt], [t_e])
                eng = "dve" if i % 2 == 0 else "pool"
                s.op(eng, lambda e, e_=e_, src=src, dst=dst: e.tensor_tensor(out=dst, in0=src, in1=e_, op=ALU.mult), [t_e, t_src], [t_dst])
                yield
            if self.mode in ("even_e2", "even_e2a"):
                return
            bA, bAt = self.bank("P")
            bAb = bA[:, :].bitcast(BF16)
            bB, bBt = self.bank("P")
            bBb = bB[:, :].bitcast(BF16)
            for hh in range(4):
                s.op("pe", lambda e, hh=hh: e.transpose(out=bAb[:, hh * 128:(hh + 1) * 128], in_=qh_t[:, hh * 128:(hh + 1) * 128], identity=ident_b[:]),
                     [t_qh, t_const], [bAt], signal=False)
            for hh in range(4):
                s.op("pe", lambda e, hh=hh: e.transpose(out=bAb[:, 512 + hh * 128:512 + (hh + 1) * 128], in_=qt_t[:, hh * 128:(hh + 1) * 128], identity=ident_b[:]),
                     [t_qt, t_const], [bAt], signal=(hh == 3))
            for hh in range(4):
                s.op("pe", lambda e, hh=hh: e.transpose(out=bBb[:, hh * 128:(hh + 1) * 128], in_=kt_t[:, hh * 128:(hh + 1) * 128], identity=ident_b[:]),
                     [t_kt, t_const], [bBt], signal=(hh == 3))
            qhT, t_qhT = P["qhT"]
            yield
            s.op("act", lambda e: e.activation(out=qhT.rearrange("p h t -> p (h t)"), in_=bAb[:, 0:512], func=AF.Copy), [bAt], [t_qhT])
            s.op("dve", lambda e: e.tensor_copy(out=qtT.rearrange("p h t -> p (h t)"), in_=bAb[:, 512:1024]), [bAt], [t_qtT])
            s.op("act", lambda e: e.activation(out=ktT.rearrange("p h t -> p (h t)"), in_=bBb[:, 0:512], func=AF.Copy), [bBt], [t_ktT])
            yield
            if self.mode == "even_e3":
                return
            bs, bst = self.bank("P")
            for c in range(2):
                for hh in range(4):
                    last = (c == 1 and hh == 3)
                    s.op("pe", lambda e, c=c, hh=hh: e.matmul(bs[c * 64:(c + 1) * 64, hh * 64:(hh + 1) * 64], ktT[:, hh, c * 64:(c + 1) * 64],
                                                              qtT[:, hh, c * 64:(c + 1) * 64], start=True, stop=True),
                         [t_ktT, t_qtT], [bst], signal=last)
            scT, t_scT = P["scT"]
            s.op("dve", lambda e: e.tensor_tensor(out=scT.rearrange("p (h t) -> p h t", h=4), in0=bs[:, 0:256].rearrange("p (h t) -> p h t", h=4),
                                                  in1=scm[:, d_, :].unsqueeze(1).to_broadcast([128, 4, 64]), op=ALU.mult), [bst, t_ec], [t_scT])

        def seq(n, d_, P, fwd_final):
            G, p4 = n // 4, n % 4
            qhT, t_qhT = P["qhT"]
            scT, t_scT = P["scT"]
            vb, t_vb = P["v"]
            kh_, t_kh = P["kh"]
            dec, t_dec = P["dec"]
            bo, bot = self.bank("S")
            chunks = (0, 1) if d_ == 0 else (1, 0)
            for c in chunks:
                cg = n * 2 + c
                if (d_ == 0 and cg % 8 == 0) or (d_ == 1 and cg % 8 == 3):
                    s.op("dve", lambda e: e.tensor_scalar(out=S_.rearrange("p h v -> p (h v)"), in0=S_.rearrange("p h v -> p (h v)"), scalar1=cflag[:, 0:1],
                                                          scalar2=None, op0=ALU.mult), [t_S, t_ec], [t_S])
                    s.op("act", lambda e: e.activation(out=Sb.rearrange("p h v -> p (h v)"), in_=S_.rearrange("p h v -> p (h v)"), func=AF.Copy), [t_S], [t_Sb])
                for hh in range(4):
                    ov = bo[c * 64:(c + 1) * 64, hh * 128:(hh + 1) * 128]
                    s.op("pe", lambda e, c=c, hh=hh, ov=ov: e.matmul(ov, qhT[:, hh, c * 64:(c + 1) * 64], Sb[:, hh, :], start=True, stop=False),
                         [t_qhT, t_Sb], [bot], signal=False)
                    last = (hh == 3 and c == chunks[1])
                    s.op("pe", lambda e, c=c, hh=hh, ov=ov: e.matmul(ov, scT[c * 64:(c + 1) * 64, hh * 64:(hh + 1) * 64], vb[c * 64:(c + 1) * 64, hh * 128:(hh + 1) * 128],
                                                                    start=False, stop=True), [t_scT, t_vb], [bot], signal=(hh == 3))
                yield
                bkv, bkvt = self.bank("S")
                for hh in range(4):
                    s.op("pe", lambda e, c=c, hh=hh, bkv=bkv: e.matmul(bkv[:, hh * 128:(hh + 1) * 128], kh_[c * 64:(c + 1) * 64, hh * 128:(hh + 1) * 128],
                                                                      vb[c * 64:(c + 1) * 64, hh * 128:(hh + 1) * 128], start=True, stop=True),
                         [t_kh, t_vb], [bkvt], signal=(hh == 3))
                yield
                s.op("dve", lambda e, c=c: e.tensor_tensor(out=S_, in0=S_, in1=dec[:, c:8:2].unsqueeze(2).to_broadcast([128, 4, 128]), op=ALU.mult),
                     [t_S, t_dec], [t_S])
                s.op("dve", lambda e, bkv=bkv: e.tensor_tensor(out=S_.rearrange("p h v -> p (h v)"), in0=bkv[:, :], in1=S_.rearrange("p h v -> p (h v)"), op=ALU.add),
                     [t_S, bkvt], [t_S])
                s.op("act", lambda e: e.activation(out=Sb.rearrange("p h v -> p (h v)"), in_=S_.rearrange("p h v -> p (h v)"), func=AF.Copy), [t_S], [t_Sb])
                if (d_ == 0 and cg % 8 == 3) or (d_ == 1 and cg % 8 == 0):
                    slot = cg // 8
                    s.dma("sp", lambda e, slot=slot: e.dma_start(out=A["hs_out"][slot, d_].rearrange("h k v -> k h v"), in_=S_), [self.t_hs], [t_S], semkey="EHS")
            if not fwd_final:
                ost, t_ost = P["ost"]
                s.op("act", lambda e: e.activation(out=ost, in_=bo[:, :], func=AF.Copy), [bot], [t_ost])
                s.dma("sp", lambda e: e.dma_start(out=obs[n], in_=ost), [t_obs[n]], [t_ost], semkey="EOst%d" % (n % 2))
                yield
                return
            ob, t_ob = P["ob"]
            gs, t_gs = P["gs"]
            s.op("dve", lambda e: e.tensor_tensor(out=osum, in0=bo[:, :], in1=ob, op=ALU.add), [bot, t_ob], [t_osum])
            yield
            s.op("act", lambda e: e.activation(out=osq, in_=osum, func=AF.Square), [t_osum], [t_osq])
            s.op("dve", lambda e: e.tensor_reduce(out=oss, in_=osq.rearrange("p (h v) -> p h v", h=4), axis=AX.X, op=ALU.add), [t_osq], [t_oss])
            s.op("act", lambda e: e.activation(out=oss, in_=oss, func=AF.Ln, bias=epsq[:, 0:1], scale=1.0 / 128.0), [t_oss, t_ec], [t_oss])
            s.op("act", lambda e: e.activation(out=oss, in_=oss, func=AF.Exp, scale=-0.5), [t_oss], [t_oss])
            yield
            s.op("dve", lambda e: e.tensor_tensor(out=osum.rearrange("p (h v) -> p h v", h=4), in0=osum.rearrange("p (h v) -> p h v", h=4),
                                                  in1=oss.unsqueeze(2).to_broadcast([128, 4, 128]), op=ALU.mult), [t_osum, t_oss], [t_osum])
            s.op("pool", lambda e: e.tensor_tensor(out=osum.rearrange("p (h v) -> p h v", h=4), in0=osum.rearrange("p (h v) -> p h v", h=4),
                                                   in1=hgn.unsqueeze(1).to_broadcast([128, 4, 128]), op=ALU.mult), [t_osum, t_ec], [t_osum])
            s.op("dve", lambda e: e.tensor_tensor(out=r_b, in0=osum, in1=gs, op=ALU.mult), [t_osum, t_gs], [t_rb])
            yield
            bT, bTt = self.bank("S")
            bTb = bT[:, :].bitcast(BF16)
            for hh in range(4):
                s.op("pe", lambda e, hh=hh: e.transpose(out=bTb[:, hh * 128:(hh + 1) * 128], in_=r_b[:, hh * 128:(hh + 1) * 128], identity=ident_b[:]),
                     [t_rb, t_const], [bTt], signal=(hh == 3))
            s.op("act", lambda e: e.activation(out=rT[:, :, p4 * 128:(p4 + 1) * 128], in_=bTb[:, 0:512].rearrange("p (c t) -> p c t", c=4), func=AF.Copy),
                 [bTt], [t_rT[p4]])

        def outproj_steps(G):
            def ldc(a, b):
                s.dma("sp", lambda e: e.dma_start(out=cvl, in_=cvs[:, :, G * 512:(G + 1) * 512]), [t_cvl], [t_cvs[G]], semkey="ECl")
            self.step(None, ldc)
            for dc in range(KC):
                def wl(e, st, dc=dc):
                    return e.dma_start(out=st[:, 0:1024].rearrange("p (c8 c) -> p c8 c", c8=8), in_=wo_v[:, :, dc * 128:(dc + 1) * 128])

                def body(st, strk, dc=dc):
                    wv_ = st[:, 0:1024].rearrange("p (c8 c) -> p c8 c", c8=8)
                    ci = cnx["i"] % 2
                    cnx["i"] += 1
                    xct, t_xc = xc[ci]
                    xot, t_xo = xo[ci]
                    s.dma("sp", lambda e: e.dma_start(out=xct, in_=xs_v[:, dc, G * 512:(G + 1) * 512]), [t_xc], [self.xs_trk[dc][G]], semkey="EXc%d" % ci)
                    bk, bt = self.bank()
                    for c8 in range(8):
                        rhs = cvl[:, c8, :] if c8 < 4 else rT[:, c8 - 4, :]
                        s.op("pe", lambda e, c8=c8, bk=bk, rhs=rhs: e.matmul(bk[:, :], wv_[:, c8, :], rhs, start=(c8 == 0), stop=(c8 == 7)),
                             list(strk) + list(t_rT) + [t_cvl], [bt], signal=(c8 == 7))
                    s.op("dve", lambda e, bk=bk, mg=self.modG[:, sl, dc:dc + 1]: e.scalar_tensor_tensor(out=xot, in0=bk[:, :], scalar=mg, in1=xct,
                                                                        op0=ALU.mult, op1=ALU.add), [bt, t_xc, self.t_modAG], [t_xo])
                    s.dma("sp", lambda e: e.dma_start(out=xs_v[:, dc, G * 512:(G + 1) * 512], in_=xot), [self.xs_trk[dc][G]], [t_xo], semkey="EXo%d" % ci)
                self.step([wl], body)

        for d_ in (1, 0):
            fwd_final = (d_ == 0)
            if fwd_final and (self.mode == "even_d" or self.mode.startswith("even_e")):
                return
            s.barrier()
            cols = [1024, 1536 + 512 * d_, 2560, 3072]
            for c in range(4 if fwd_final else 3):
                s.dma("pool", lambda e, c=c, cols=cols: e.dma_start(out=whg[:, :, c * 512:(c + 1) * 512], in_=wv[:, :, cols[c]:cols[c] + 512]), [t_whg[c]], [], semkey="EWh%d" % c)
            s.dma("sp", lambda e, d_=d_: e.dma_start(out=S_, in_=A["s0"][d_].rearrange("h k v -> k h v")), [t_S], [], semkey="ES0")
            s.op("act", lambda e: e.activation(out=Sb.rearrange("p h v -> p (h v)"), in_=S_.rearrange("p h v -> p (h v)"), func=AF.Copy), [t_S], [t_Sb])
            order = list(range(NTILE)) if d_ == 0 else list(range(NTILE - 1, -1, -1))
            for i, n in enumerate(order + [None]):
                if n is not None:
                    P = P2[i % 2]
                    if fwd_final:
                        ob, t_ob = P["ob"]
                        self.step(None, lambda a, b, n=n, ob=ob, t_ob=t_ob, i=i: s.dma(
                            "sp", lambda e: e.dma_start(out=ob, in_=obs[n]), [t_ob], [t_obs[n]], semkey="EOb%d" % (i % 2)))
                gp = (lambda n=n, P=P: prep(n, d_, P, fwd_final)) if n is not None else None
                gs_ = None
                if i >= 1 and not self.mode.startswith("even_e"):
                    pn = order[i - 1]
                    gs_ = (lambda pn=pn, Pp=P2[(i - 1) % 2]: seq(pn, d_, Pp, fwd_final))
                self.step(None, lambda a, b, gp=gp, gs_=gs_: self.interleave([gp() if gp else None, gs_() if gs_ else None]))
                if d_ == 1 and self.pending_mod is not None and self.pending_mod[0] and i % 2 == 1:
                    self.steps.append(self.pending_mod[0].pop(0))
                if i >= 1 and not self.mode.startswith("even_e"):
                    if fwd_final and pn % 4 == 3:
                        outproj_steps(pn // 4)
            if d_ == 1 and self.pending_mod is not None:
                self.steps.extend(self.pending_mod[0])
                self.run_steps()
                self.pending_mod[1]()
                self.pending_mod = None
            self.run_steps()

    def phase_ffn(self, l, j, ffn_w_in, ffn_w_out, xs_v, ones_b, t_const, t1_out=None):
        nc, s = self.nc, self.s
        sl = 0 if j == 0 else 2
        TS = 1024
        NTT = TS // 512
        s.barrier()
        cv = Carve(self.arena)
        hT = cv.bf(KC * TS).rearrange("p (k t) -> p k t", k=KC)
        hT_trk = [Trk("hT%d" % i) for i in range(NTT)]
        hid = cv.bf(FC * TS).rearrange("p (f t) -> p f t", f=FC)
        hid_trk = [[Trk("hid%d_%d" % (f, i)) for i in range(NTT)] for f in range(FC)]
        xin = [cv.f32(KC * 512).rearrange("p (k t) -> p k t", k=KC) for i in range(2)]
        xin_trk = [Trk("xin%d" % i) for i in range(2)]
        nr = {"sq": cv.ring(2, 512, "bf", "sq"), "tmp": cv.ring(2, 512, "f32", "tmp"), "rstd": (cv.f32(512), Trk("rstd"))}
        sg = [cv.bf(512) for i in range(3)]
        sg_trk = [Trk("sg%d" % i) for i in range(3)]
        xc = [cv.f32(512) for i in range(3)]
        xc_trk = [Trk("xc%d" % i) for i in range(3)]
        xo = [cv.f32(512) for i in range(3)]
        xo_trk = [Trk("xo%d" % i) for i in range(3)]
        wi = ffn_w_in[l, j].rearrange("(kc p) c -> p kc c", p=128)
        wo = ffn_w_out[l, j].rearrange("(fc p) d -> p fc d", p=128)
        cnt = {"sg": 0, "xc": 0}

        NST = NT // TS
        seqL, seqN, seqA, seqB = [], [], [[] for _ in range(NST)], [[] for _ in range(NST)]
        for st_i in range(NT // TS):
            t0 = st_i * TS

            def load_body(_a, _b, st_i=st_i, t0=t0):
                for tt in range(NTT):
                    g = (t0 // 512 + tt)
                    i = g % 2
                    s.dma("sp", lambda e, i=i, g=g: e.dma_start(out=xin[i], in_=xs_v[:, :, g * 512:(g + 1) * 512]), [xin_trk[i]],
                          [self.xs_trk[kc][g] for kc in range(KC)], semkey="FXin%d" % i)

            def norm_body(_a, _b, st_i=st_i, t0=t0):
                for tt in range(NTT):
                    g = (t0 // 512 + tt)
                    i = g % 2
                    self.norm_tile(xin[i], xin_trk[i], sl, hT[:, :, tt * 512:(tt + 1) * 512], hT_trk[tt], nr)
            seqL.append((None, load_body))
            seqN.append((None, norm_body))

            for jb in range(11):
                def wl_g(e, st, jb=jb):
                    return e.dma_start(out=st[:, 0:2048].rearrange("p (k c) -> p k c", k=KC), in_=wi[:, :, jb * 256:(jb + 1) * 256])

                def wl_u(e, st, jb=jb):
                    return e.dma_start(out=st[:, 2048:4096].rearrange("p (k c) -> p k c", k=KC),
                                       in_=wi[:, :, DFF + jb * 256: DFF + (jb + 1) * 256])

                def bodyA(st, strk, jb=jb):
                    wg = st[:, 0:2048].rearrange("p (k c) -> p k c", k=KC)
                    wu = st[:, 2048:4096].rearrange("p (k c) -> p k c", k=KC)
                    for fs in range(2):
                        f = jb * 2 + fs
                        for tt in range(NTT):
                            bg, bgt = self.bank()
                            bu, but = self.bank()
                            for kc in range(KC):
                                s.op("pe", lambda e, kc=kc, fs=fs, tt=tt, bg=bg: e.matmul(
                                    bg[:, :], wg[:, kc, fs * 128:(fs + 1) * 128], hT[:, kc, tt * 512:(tt + 1) * 512],
                                    start=(kc == 0), stop=(kc == KC - 1)), list(strk) + [hT_trk[tt]], [bgt], signal=(kc == KC - 1))
                            for kc in range(KC):
                                s.op("pe", lambda e, kc=kc, fs=fs, tt=tt, bu=bu: e.matmul(
                                    bu[:, :], wu[:, kc, fs * 128:(fs + 1) * 128], hT[:, kc, tt * 512:(tt + 1) * 512],
                                    start=(kc == 0), stop=(kc == KC - 1)), list(strk) + [hT_trk[tt]], [but], signal=(kc == KC - 1))
                            si = cnt["sg"] % 3
                            cnt["sg"] += 1
                            s.op("act", lambda e, si=si, bg=bg: e.activation(out=sg[si], in_=bg[:, :], func=AF.Silu), [bgt], [sg_trk[si]])
                            s.op("dve", lambda e, si=si, bu=bu, f=f, tt=tt: e.tensor_tensor(
                                out=hid[:, f, tt * 512:(tt + 1) * 512], in0=bu[:, :], in1=sg[si], op=ALU.mult),
                                [but, sg_trk[si]], [hid_trk[f][tt]])
                seqA[st_i].append(([wl_g, wl_u], bodyA))

            for dc in range(KC):
                def wl_o(e, st, dc=dc):
                    return e.dma_start(out=st[:, 0:FC * 128].rearrange("p (f c) -> p f c", f=FC), in_=wo[:, :, dc * 128:(dc + 1) * 128])

                def bodyB(st, strk, dc=dc, t0=t0):
                    wv = st[:, 0:FC * 128].rearrange("p (f c) -> p f c", f=FC)
                    for tt in range(NTT):
                        g = t0 // 512 + tt
                        ci = cnt["xc"] % 3
                        cnt["xc"] += 1
                        s.dma("sp", lambda e, ci=ci, g=g: e.dma_start(out=xc[ci], in_=xs_v[:, dc, g * 512:(g + 1) * 512]),
                              [xc_trk[ci]], [self.xs_trk[dc][g]], semkey="FXc%d" % ci)
                        bk, bt = self.bank()
                        for f in range(FC):
                            s.op("pe", lambda e, f=f, tt=tt, bk=bk: e.matmul(
                                bk[:, :], wv[:, f, :], hid[:, f, tt * 512:(tt + 1) * 512], start=(f == 0), stop=(f == FC - 1)),
                                list(strk) + [hid_trk[f][tt]], [bt], signal=(f == FC - 1))
                        s.op("dve", lambda e, ci=ci, bk=bk, mg=self.modG[:, sl, dc:dc + 1]: e.scalar_tensor_tensor(
                            out=xo[ci], in0=bk[:, :], scalar=mg, in1=xc[ci], op0=ALU.mult, op1=ALU.add),
                            [bt, xc_trk[ci], self.t_modAG], [xo_trk[ci]])
                        s.dma("sp", lambda e, ci=ci, g=g: e.dma_start(out=xs_v[:, dc, g * 512:(g + 1) * 512], in_=xo[ci]),
                              [self.xs_trk[dc][g]], [xo_trk[ci]], semkey="FXo%d" % ci)
                seqB[st_i].append(([wl_o], bodyB))
        t1l = t1c = None
        if t1_out is not None:
            t1l, t1c = self.t1_setup(xs_v, t1_out, cv.off)
        gps = TS // 512
        order = [seqL[0], seqN[0]] + seqA[0]
        for st_i in range(NST):
            bsteps = list(seqB[st_i])
            if st_i + 1 < NST:
                order.append(seqL[st_i + 1])
                bsteps.insert(2, seqN[st_i + 1])
            order += bsteps
            if t1l is not None:
                order.append((None, lambda a, b, st_i=st_i: [t1l(st_i * gps + g) for g in range(gps)]))
            if st_i + 1 < NST:
                nxt = list(seqA[st_i + 1])
                if t1c is not None:
                    for g in range(gps):
                        pos = min(len(nxt), 3 + 4 * g + g)
                        nxt.insert(pos, (None, lambda a, b, tt=st_i * gps + g: t1c(tt)))
                order += nxt
            elif t1c is not None:
                for g in range(gps):
                    order.append((None, lambda a, b, tt=st_i * gps + g: t1c(tt)))
        self.steps.extend(order)
        self.run_steps()


_CACHE = {}


def _get_prog(mode):
    if mode not in _CACHE:
        p = Prog(mode)
        p.build()
        _CACHE[mode] = p
    return _CACHE[mode]


NEG = -30000.0


def _attn_masks():
    key = np.arange(128)[:, None]
    q = np.arange(128)[None, :]
    vis = np.zeros((128, 128), np.float32)
    hid = np.full((128, 128), NEG, np.float32)
    prev_band = np.where(key >= q, 0.0, NEG).astype(np.float32)
    next_band = np.where(key <= q, 0.0, NEG).astype(np.float32)
    am_s = np.zeros((128, 6, 2, 128), np.float32)
    am_p = np.zeros((128, 6, 2, 128), np.float32)
    for v in range(6):
        am_s[:, v, 0] = hid if v == 0 else prev_band
        am_s[:, v, 1] = hid if v == 5 else next_band
        am_p[:, v, 0] = vis if v == 2 else hid
        am_p[:, v, 1] = vis if v in (0, 1) else hid
    return np.ascontiguousarray(am_s.reshape(128, 1536)), np.ascontiguousarray(am_p.reshape(128, 1536))


def _scan_consts():
    sidx = np.arange(128)[:, None]
    tidx = np.arange(128)[None, :]
    same = (sidx // 64) == (tidx // 64)
    ls = sidx % 64
    cm = np.zeros((128, 2, 4, 128), np.float32)
    cm[:, 0, 3] = same
    cm[:, 1, 3] = same
    sm = np.zeros((128, 2, 64), np.float32)
    mb = same & (sidx <= tidx)
    sel = same & (ls <= 31)
    cm[:, 0, 0] = mb
    cm[:, 0, 1] = mb.astype(np.float32) - sel.astype(np.float32)
    cm[:, 0, 2] = same & (sidx > tidx)
    mb = same & (sidx >= tidx)
    sel = same & (ls >= 32)
    cm[:, 1, 0] = mb
    cm[:, 1, 1] = mb.astype(np.float32) - sel.astype(np.float32)
    cm[:, 1, 2] = same & (sidx < tidx)
    s_loc = (np.arange(128) % 64)[:, None]
    t_loc = np.arange(64)[None, :]
    sm[:, 0] = (s_loc <= t_loc)
    sm[:, 1] = (s_loc >= t_loc)
    return np.ascontiguousarray(cm.reshape(128, 1024)), np.ascontiguousarray(sm.reshape(128, 128))


def _rope_tables():
    t = np.arange(NT)
    row = (t // 64).astype(np.float32)
    col = (t % 64).astype(np.float32)
    inv = (10000.0 ** (-np.arange(0, 32, 2, dtype=np.float32) / 32.0)).astype(np.float32)
    ar = (row[:, None] * inv[None, :]).astype(np.float32)
    ac = (col[:, None] * inv[None, :]).astype(np.float32)
    C = np.concatenate([np.cos(ar), np.cos(ar), np.cos(ac), np.cos(ac)], axis=1).astype(np.float32)
    S = np.concatenate([-np.sin(ar), np.sin(ar), -np.sin(ac), np.sin(ac)], axis=1).astype(np.float32)
    return np.ascontiguousarray(C), np.ascontiguousarray(S)


def make_in_maps(inputs, mode="full"):
    f32 = np.float32
    xp = np.asarray(inputs["x_prompt"], f32)
    xsm = np.asarray(inputs["x_sample"], f32)
    c = np.asarray(inputs["c"], f32)
    c_ctx = np.asarray(inputs["c_ctx"], f32)
    ident = np.eye(128, dtype=f32)
    b_mod = np.asarray(inputs["b_mod"], f32)
    bmodT = np.ascontiguousarray(b_mod.reshape(2, 72, 128).transpose(0, 2, 1))
    norm_g = np.asarray(inputs["norm_g"], f32)
    normgT = np.ascontiguousarray(norm_g.reshape(2, 3, KC, 128).transpose(0, 1, 3, 2))
    shared = {
        "ident": ident,
        "w_mod": np.asarray(inputs["w_mod"], f32),
        "bmodT": bmodT,
        "normgT": normgT,
    }
    if mode in ("full", "ffn1", "l0"):
        shared["ffn_w_in"] = np.asarray(inputs["ffn_w_in"], f32)
        shared["ffn_w_out"] = np.asarray(inputs["ffn_w_out"], f32)
    if mode in ("full", "attn"):
        shared["od_w_in"] = np.asarray(inputs["od_w_in"], f32)
        shared["od_w_out"] = np.asarray(inputs["od_w_out"], f32)
        shared["qkgain"] = np.ascontiguousarray(np.broadcast_to(
            np.concatenate([np.asarray(inputs["q_norm_g"], f32)[0], np.asarray(inputs["k_norm_g"], f32)[0]])[None, :], (128, 128)))
        shared["sinks"] = np.asarray(inputs["sinks"], f32).reshape(1, 16)
        am_s, am_p = _attn_masks()
        rc, rs = _rope_tables()
    if mode in ("full", "l0") or mode.startswith("even"):
        shared["ev_w_in"] = np.asarray(inputs["ev_w_in"], f32)
        shared["ev_w_out"] = np.asarray(inputs["ev_w_out"], f32)
        cw = np.asarray(inputs["conv_w"], f32)[0]
        shared["convwT"] = np.ascontiguousarray(cw.reshape(31, 4, 128).transpose(2, 1, 0).reshape(128, 124))
        vecs = np.stack([np.asarray(inputs[k], f32)[0] for k in ("conv_b", "conv_ln_g", "conv_ln_b")])
        shared["convvec"] = np.ascontiguousarray(vecs.reshape(3, 4, 128).transpose(2, 0, 1).reshape(128, 12))
        shared["lbraw"] = np.ascontiguousarray(np.asarray(inputs["hg_lb_raw"], f32).reshape(2, 1536))
        shared["hgn"] = np.ascontiguousarray(np.broadcast_to(np.asarray(inputs["hg_norm_g"], f32)[0][None, :], (128, 128)))
        cm, sm = _scan_consts()
        shared["cumM"] = cm
        shared["scmask"] = sm
    maps = []
    for core in range(NCORES):
        m = dict(shared)
        if mode in ("full", "l0") or mode.startswith("even"):
            if core < 4:
                m["tmask"] = np.ones((128, 512), f32)
                m["cflag"] = np.ones((128, 1), f32)
                m["tokm"] = np.ones((128, 4), f32)
                m["s0"] = np.ascontiguousarray(np.asarray(inputs["state_hgrn"], f32)[core, 0])
            else:
                tm = np.zeros((128, 512), f32)
                tm[:, :256] = 1.0
                m["tmask"] = tm
                m["cflag"] = np.zeros((128, 1), f32)
                tk = np.zeros((128, 4), f32)
                tk[:, :2] = 1.0
                m["tokm"] = tk
                m["s0"] = np.zeros((2, 4, 128, 128), f32)
        if mode in ("full", "attn"):
            if core < 4:
                m["amask"] = am_s
                m["cval"] = np.ones((128, 64), f32)
                m["ctx_k"] = np.ascontiguousarray(np.asarray(inputs["cache_k"], f32)[core, 0].reshape(256, 256))
                m["ctx_v"] = np.ascontiguousarray(np.asarray(inputs["cache_v"], f32)[core, 0].reshape(256, 256))
                m["ropeC"], m["ropeS"] = rc, rs
            else:
                m["amask"] = am_p
                m["cval"] = np.zeros((128, 64), f32)
                m["ctx_k"] = np.zeros((256, 256), f32)
                m["ctx_v"] = np.zeros((256, 256), f32)
                m["ropeC"] = np.ones((NT, 64), f32)
                m["ropeS"] = np.zeros((NT, 64), f32)
        if core < 4:
            m["x_in"] = np.ascontiguousarray(xsm[core])
            cv = c[core]
        else:
            xi = np.zeros((NT, D), f32)
            for sl in range(8):
                xi[sl * 512: sl * 512 + 256] = xp[(core - 4) * 8 + sl]
            m["x_in"] = xi
            cv = c_ctx
        m["cond"] = np.ascontiguousarray(cv.reshape(KC, 128).T)
        maps.append(m)
    return maps


def run(inputs, mode="full", trace=False):
    p = _get_prog(mode)
    maps = make_in_maps(inputs, mode)
    res = run_bass_kernel_spmd(p.nc, maps, core_ids=list(range(NCORES)), trace=trace)
    return res


def kernel(**inputs):
    res = run(inputs, "full")
    r = res.results
    y_sample = np.stack([np.asarray(r[i]["y"], np.float32) for i in range(4)], axis=0)
    y_prompt = np.zeros((32, 256, D), np.float32)
    hs = np.zeros((32, 1, 2, 4, 128, 128), np.float32)
    nk = np.zeros((32, 1, 256, 4, 64), np.float32)
    nv = np.zeros((32, 1, 256, 4, 64), np.float32)
    for core in range(4, 8):
        y = np.asarray(r[core]["y"], np.float32)
        for sl in range(8):
            b = (core - 4) * 8 + sl
            y_prompt[b] = y[sl * 512: sl * 512 + 256]
            hs[b, 0] = r[core]["hs_out"][sl]
            nk[b, 0] = np.asarray(r[core]["nk"][sl]).reshape(256, 4, 64)
            nv[b, 0] = np.asarray(r[core]["nv"][sl]).reshape(256, 4, 64)
    return (y_prompt, y_sample, hs, nk, nv)
```
